# Optimizing a Trainium2 kernel written in Bass

```python
import math
import jax
import jax.numpy as jnp
from jax import lax
import numpy as np

D_MODEL = 2048
BATCH = 8
SEQ = 4096
DEPTH = 4

N_MIXERS = 4
Q_BLOCK = 128
EPS = 1e-6
NEG = -1e30

T5_BUCKETS = 32
T5_MAX_DIST = 128
T5_HEADS = 16

SB_HEADS = 16
SB_HEAD_DIM = D_MODEL // SB_HEADS

NSA_HEADS = 16
NSA_GROUPS = 4
NSA_HEAD_DIM = 128
NSA_CMP_BLOCK = 32
NSA_CMP_STRIDE = 16
NSA_SEL_BLOCK = 64
NSA_TOPN = 16
NSA_WINDOW = 512
NSA_Q_BLOCK = 32
NSA_IN_DIM = NSA_HEADS * NSA_HEAD_DIM + 6 * NSA_GROUPS * NSA_HEAD_DIM + 3 * NSA_HEADS
FORCED_SCORE = 1e9

DIFF_HEADS = 16
DIFF_HEAD_DIM = D_MODEL // (2 * DIFF_HEADS)

MLA_HEADS = 16
MLA_Q_LORA = 768
MLA_KV_LORA = 512
MLA_NOPE = 128
MLA_ROPE = 64
MLA_V = 128
MLA_IN_DIM = MLA_Q_LORA + MLA_KV_LORA + MLA_ROPE
ROPE_THETA = 10000.0

FFN_DIM = 5632
CONV_WIDTH = 3

kernel_name = "hybrid_sb_nsa_diff_mla_convffn_adaln"


def rmsnorm(x, g):
    xf = x.astype(jnp.float32)
    y = xf * lax.rsqrt(jnp.mean(xf * xf, axis=-1, keepdims=True) + EPS)
    return (y * g).astype(x.dtype)


def modulate(x, g, shift, scale):
    return rmsnorm(x, g) * (1.0 + scale[:, None, :]) + shift[:, None, :]


def masked_softmax(logits, mask):
    logits = jnp.where(mask, logits.astype(jnp.float32), NEG)
    m = jnp.max(logits, axis=-1, keepdims=True)
    p = jnp.where(mask, jnp.exp(logits - m), 0.0)
    return p / jnp.maximum(jnp.sum(p, axis=-1, keepdims=True), 1e-30)


def t5_bucket(dist):
    n = jnp.maximum(dist, 0)
    max_exact = T5_BUCKETS // 2
    nf = jnp.maximum(n, 1).astype(jnp.float32)
    large = max_exact + (jnp.log(nf / max_exact) / math.log(T5_MAX_DIST / max_exact)
                         * (T5_BUCKETS - max_exact)).astype(jnp.int32)
    return jnp.where(n < max_exact, n, jnp.minimum(large, T5_BUCKETS - 1))


def rel_bias(t5_bias, dist):
    return jnp.moveaxis(t5_bias[t5_bucket(dist)].astype(jnp.float32), -1, 0)


def rope(x, pos):
    half = x.shape[-1] // 2
    inv = jnp.power(ROPE_THETA, -jnp.arange(half, dtype=jnp.float32) / half)
    ang = pos.astype(jnp.float32)[:, None] * inv[None, :]
    cos = jnp.cos(ang)[None, :, None, :]
    sin = jnp.sin(ang)[None, :, None, :]
    x1, x2 = x[..., :half], x[..., half:]
    return jnp.concatenate([x1 * cos - x2 * sin, x1 * sin + x2 * cos], axis=-1).astype(x.dtype)


def blockwise(fn, n_q, q_block):
    out = lax.map(fn, jnp.arange(n_q // q_block) * q_block)
    nb = out.shape[0]
    out = jnp.moveaxis(out, 0, 1)
    return out.reshape(out.shape[0], nb * q_block, *out.shape[3:])


def stick_breaking_attention(h, w_in, w_out):
    B, S, _ = h.shape
    qkv = (h @ w_in).reshape(B, S, 3, SB_HEADS, SB_HEAD_DIM)
    q, k, v = qkv[:, :, 0], qkv[:, :, 1], qkv[:, :, 2]
    scale = SB_HEAD_DIM ** -0.5
    kpos = jnp.arange(S)

    def block(t0):
        qb = lax.dynamic_slice_in_dim(q, t0, Q_BLOCK, axis=1)
        z = jnp.einsum('bqhd,bkhd->bhqk', qb, k).astype(jnp.float32) * scale
        qpos = t0 + jnp.arange(Q_BLOCK)
        past = kpos[None, :] < qpos[:, None]
        log_keep = jnp.where(past, jax.nn.log_sigmoid(-z), 0.0)
        later = lax.cumsum(log_keep, axis=3, reverse=True) - log_keep
        a = jnp.where(past, jnp.exp(jax.nn.log_sigmoid(z) + later), 0.0)
        return jnp.einsum('bhqk,bkhd->bqhd', a.astype(v.dtype), v)

    o = blockwise(block, S, Q_BLOCK)
    return o.reshape(B, S, SB_HEADS * SB_HEAD_DIM) @ w_out


def nsa_attention(h, w_in, cmp_pe, cmp_w1, cmp_w2, w_out, t5_bias):
    B, S, _ = h.shape
    H, G, dh = NSA_HEADS, NSA_GROUPS, NSA_HEAD_DIM
    R = H // G
    lc, sc, ls, W, Qb = NSA_CMP_BLOCK, NSA_CMP_STRIDE, NSA_SEL_BLOCK, NSA_WINDOW, NSA_Q_BLOCK
    scale = dh ** -0.5
    proj = h @ w_in
    q = proj[..., :H * dh].reshape(B, S, G, R, dh)
    kvs = proj[..., H * dh:H * dh + 6 * G * dh].reshape(B, S, 6, G, dh)
    gates = jax.nn.sigmoid(proj[..., H * dh + 6 * G * dh:].astype(jnp.float32)).reshape(B, S, 3, G, R)
    k_cmp_raw, v_cmp_raw, k_sel, v_sel, k_win, v_win = (kvs[:, :, i] for i in range(6))

    n_cmp = (S - lc) // sc + 1
    cmp_start = jnp.arange(n_cmp) * sc
    cmp_end = cmp_start + lc - 1
    cmp_idx = cmp_start[:, None] + jnp.arange(lc)[None, :]

    def compress(raw, pe, w1, w2):
        blocks = raw[:, cmp_idx] + pe[:, None, :]
        flat = jnp.moveaxis(blocks, 2, 3).reshape(B, n_cmp, G, lc * dh)
        return jax.nn.silu(flat @ w1) @ w2

    k_cmp = compress(k_cmp_raw, cmp_pe[0], cmp_w1[0], cmp_w2[0])
    v_cmp = compress(v_cmp_raw, cmp_pe[1], cmp_w1[1], cmp_w2[1])

    n_sel = S // ls
    n_top = min(NSA_TOPN, n_sel)
    sel_start = jnp.arange(n_sel) * ls
    overlap = ((cmp_start[:, None] < sel_start[None, :] + ls)
               & (cmp_start[:, None] + lc > sel_start[None, :])).astype(jnp.float32)
    k_blk = jnp.moveaxis(k_sel.reshape(B, n_sel, ls, G, dh), 3, 1)
    v_blk = jnp.moveaxis(v_sel.reshape(B, n_sel, ls, G, dh), 3, 1)
    blk = jnp.arange(n_sel)

    k_pad = jnp.pad(k_win, ((0, 0), (W, 0), (0, 0), (0, 0)))
    v_pad = jnp.pad(v_win, ((0, 0), (W, 0), (0, 0), (0, 0)))

    tb = t5_bias.reshape(T5_BUCKETS, G, R)
    b_idx = jnp.arange(B)[:, None, None, None]
    g_idx = jnp.arange(G)[None, :, None, None]
    g_idx5 = jnp.arange(G)[None, :, None, None, None]

    def block(t0):
        qpos = t0 + jnp.arange(Qb)
        qb = lax.dynamic_slice_in_dim(q, t0, Qb, axis=1)
        dist_c = qpos[:, None] - cmp_end[None, :]
        s_c = (jnp.einsum('bqgrd,bcgd->bgrqc', qb, k_cmp).astype(jnp.float32) * scale
               + rel_bias(t5_bias, dist_c).reshape(G, R, Qb, n_cmp))
        p_c = masked_softmax(s_c, dist_c >= 0)
        o_c = jnp.einsum('bgrqc,bcgd->bqgrd', p_c.astype(v_cmp.dtype), v_cmp)
        imp = jnp.einsum('bgrqc,cn->bgqn', p_c, overlap)
        cur = qpos // ls
        causal_blk = sel_start[None, :] <= qpos[:, None]
        forced = (blk[None, :] == 0) | (blk[None, :] == cur[:, None]) | (blk[None, :] == cur[:, None] - 1)
        score = jnp.where(causal_blk, jnp.where(forced, FORCED_SCORE, imp), -1.0)
        _, idx = lax.top_k(score, n_top)
        kg = k_blk[b_idx, g_idx, idx]
        vg = v_blk[b_idx, g_idx, idx]
        dist_s = qpos[:, None, None] - (idx[..., None] * ls + jnp.arange(ls))
        bias_s = jnp.moveaxis(tb[t5_bucket(dist_s), g_idx5], -1, 2).astype(jnp.float32)
        s_s = jnp.einsum('bqgrd,bgqnkd->bgrqnk', qb, kg).astype(jnp.float32) * scale + bias_s
        p_s = masked_softmax(s_s.reshape(B, G, R, Qb, n_top * ls),
                             (dist_s >= 0).reshape(B, G, 1, Qb, n_top * ls))
        o_s = jnp.einsum('bgrqm,bgqmd->bqgrd', p_s.astype(vg.dtype),
                         vg.reshape(B, G, Qb, n_top * ls, dh))
        kw = lax.dynamic_slice_in_dim(k_pad, t0, W + Qb, axis=1)
        vw = lax.dynamic_slice_in_dim(v_pad, t0, W + Qb, axis=1)
        kp = t0 - W + jnp.arange(W + Qb)
        dist_w = qpos[:, None] - kp[None, :]
        mask_w = (dist_w >= 0) & (dist_w < W) & (kp[None, :] >= 0)
        s_w = (jnp.einsum('bqgrd,bkgd->bgrqk', qb, kw).astype(jnp.float32) * scale
               + rel_bias(t5_bias, dist_w).reshape(G, R, Qb, W + Qb))
        p_w = masked_softmax(s_w, mask_w)
        o_w = jnp.einsum('bgrqk,bkgd->bqgrd', p_w.astype(vw.dtype), vw)
        gb = lax.dynamic_slice_in_dim(gates, t0, Qb, axis=1)[..., None]
        return gb[:, :, 0] * o_c + gb[:, :, 1] * o_s + gb[:, :, 2] * o_w

    o = blockwise(block, S, Qb)
    return o.reshape(B, S, H * dh) @ w_out


def diff_attention(h, w_in, lam, head_g, w_out, t5_bias, lambda_init):
    B, S, _ = h.shape
    H, d = DIFF_HEADS, DIFF_HEAD_DIM
    qkv = (h @ w_in).reshape(B, S, 3, H, 2, d)
    q, k = qkv[:, :, 0], qkv[:, :, 1]
    v = qkv[:, :, 2].reshape(B, S, H, 2 * d)
    lam_f = lam.astype(jnp.float32)
    lmbda = jnp.exp(jnp.sum(lam_f[0] * lam_f[1])) - jnp.exp(jnp.sum(lam_f[2] * lam_f[3])) + lambda_init
    scale = d ** -0.5
    kpos = jnp.arange(S)

    def block(t0):
        qb = lax.dynamic_slice_in_dim(q, t0, Q_BLOCK, axis=1)
        qpos = t0 + jnp.arange(Q_BLOCK)
        dist = qpos[:, None] - kpos[None, :]
        s = (jnp.einsum('bqhmd,bkhmd->bmhqk', qb, k).astype(jnp.float32) * scale
             + rel_bias(t5_bias, dist))
        p = masked_softmax(s, dist >= 0)
        a = p[:, 0] - lmbda * p[:, 1]
        return jnp.einsum('bhqk,bkhe->bqhe', a.astype(v.dtype), v)

    o = blockwise(block, S, Q_BLOCK)
    o = rmsnorm(o, head_g) * (1.0 - lambda_init)
    return o.reshape(B, S, H * 2 * d) @ w_out


def mla_attention(h, w_in, q_g, w_qb, kv_g, w_kvb, w_out):
    B, S, _ = h.shape
    H = MLA_HEADS
    a = h @ w_in
    cq = a[..., :MLA_Q_LORA]
    ckv = a[..., MLA_Q_LORA:MLA_Q_LORA + MLA_KV_LORA]
    kr = a[..., MLA_Q_LORA + MLA_KV_LORA:]
    qf = (rmsnorm(cq, q_g) @ w_qb).reshape(B, S, H, MLA_NOPE + MLA_ROPE)
    kv = (rmsnorm(ckv, kv_g) @ w_kvb).reshape(B, S, H, MLA_NOPE + MLA_V)
    pos = jnp.arange(S)
    q_nope = qf[..., :MLA_NOPE]
    q_rope = rope(qf[..., MLA_NOPE:], pos)
    k_nope, v = kv[..., :MLA_NOPE], kv[..., MLA_NOPE:]
    k_rope = rope(kr[:, :, None, :], pos)[:, :, 0]
    scale = (MLA_NOPE + MLA_ROPE) ** -0.5

    def block(t0):
        qn = lax.dynamic_slice_in_dim(q_nope, t0, Q_BLOCK, axis=1)
        qr = lax.dynamic_slice_in_dim(q_rope, t0, Q_BLOCK, axis=1)
        s = (jnp.einsum('bqhd,bkhd->bhqk', qn, k_nope)
             + jnp.einsum('bqhr,bkr->bhqk', qr, k_rope)).astype(jnp.float32) * scale
        qpos = t0 + jnp.arange(Q_BLOCK)
        p = masked_softmax(s, pos[None, :] <= qpos[:, None])
        return jnp.einsum('bhqk,bkhd->bqhd', p.astype(v.dtype), v)

    o = blockwise(block, S, Q_BLOCK)
    return o.reshape(B, S, H * MLA_V) @ w_out


def conv_ffn(h, w_up, conv_w, conv_b, w_down):
    u = h @ w_up
    ch = u.shape[-1]
    u = lax.conv_general_dilated(u, conv_w.astype(u.dtype)[:, None, :], window_strides=(1,),
                                 padding=[(CONV_WIDTH - 1, 0)],
                                 dimension_numbers=('NWC', 'WIO', 'NWC'),
                                 feature_group_count=ch) + conv_b
    gate, up = jnp.split(u, 2, axis=-1)
    return (jax.nn.silu(gate) * up) @ w_down


def setup_inputs(seed: int = 0) -> dict:
    key = jax.random.key(seed)
    ks = iter(jax.random.split(key, 40))

    def nrm(shape, scale):
        return jax.random.normal(next(ks), shape, jnp.float32) * scale

    def gain(shape):
        return 1.0 + nrm(shape, 0.05)

    D = D_MODEL
    F2 = 2 * FFN_DIM
    n_a, n_b, n_c, n_d = (len(range(m, DEPTH, N_MIXERS)) for m in range(N_MIXERS))
    lcd = NSA_CMP_BLOCK * NSA_HEAD_DIM
    return {
        'x': nrm((BATCH, SEQ, D), 1.0),
        'c': nrm((BATCH, D), 1.0),
        't5_bias': nrm((T5_BUCKETS, T5_HEADS), 0.3),
        'ada_w': nrm((DEPTH, D, 6 * D), 0.5 * D ** -0.5),
        'ada_b': nrm((DEPTH, 6 * D), 0.01),
        'norm_g': gain((DEPTH, 2, D)),
        'final_g': gain((D,)),
        'ffn_w_up': nrm((DEPTH, D, F2), D ** -0.5),
        'ffn_conv_w': nrm((DEPTH, CONV_WIDTH, F2), CONV_WIDTH ** -0.5),
        'ffn_conv_b': nrm((DEPTH, F2), 0.02),
        'ffn_w_down': nrm((DEPTH, FFN_DIM, D), FFN_DIM ** -0.5),
        'sb_w_in': nrm((n_a, D, 3 * SB_HEADS * SB_HEAD_DIM), D ** -0.5),
        'sb_w_out': nrm((n_a, SB_HEADS * SB_HEAD_DIM, D), (SB_HEADS * SB_HEAD_DIM) ** -0.5),
        'nsa_w_in': nrm((n_b, D, NSA_IN_DIM), D ** -0.5),
        'nsa_cmp_pe': nrm((n_b, 2, NSA_CMP_BLOCK, NSA_HEAD_DIM), 0.5),
        'nsa_cmp_w1': nrm((n_b, 2, lcd, NSA_HEAD_DIM), lcd ** -0.5),
        'nsa_cmp_w2': nrm((n_b, 2, NSA_HEAD_DIM, NSA_HEAD_DIM), NSA_HEAD_DIM ** -0.5),
        'nsa_w_out': nrm((n_b, NSA_HEADS * NSA_HEAD_DIM, D), (NSA_HEADS * NSA_HEAD_DIM) ** -0.5),
        'diff_w_in': nrm((n_c, D, 3 * DIFF_HEADS * 2 * DIFF_HEAD_DIM), D ** -0.5),
        'diff_lambda': nrm((n_c, 4, DIFF_HEAD_DIM), 0.1),
        'diff_head_g': gain((n_c, 2 * DIFF_HEAD_DIM)),
        'diff_w_out': nrm((n_c, DIFF_HEADS * 2 * DIFF_HEAD_DIM, D), (DIFF_HEADS * 2 * DIFF_HEAD_DIM) ** -0.5),
        'mla_w_in': nrm((n_d, D, MLA_IN_DIM), D ** -0.5),
        'mla_q_g': gain((n_d, MLA_Q_LORA)),
        'mla_w_qb': nrm((n_d, MLA_Q_LORA, MLA_HEADS * (MLA_NOPE + MLA_ROPE)), MLA_Q_LORA ** -0.5),
        'mla_kv_g': gain((n_d, MLA_KV_LORA)),
        'mla_w_kvb': nrm((n_d, MLA_KV_LORA, MLA_HEADS * (MLA_NOPE + MLA_V)), MLA_KV_LORA ** -0.5),
        'mla_w_out': nrm((n_d, MLA_HEADS * MLA_V, D), (MLA_HEADS * MLA_V) ** -0.5),
    }


def reference(x, c, t5_bias, ada_w, ada_b, norm_g, final_g, ffn_w_up, ffn_conv_w, ffn_conv_b,
              ffn_w_down, sb_w_in, sb_w_out, nsa_w_in, nsa_cmp_pe, nsa_cmp_w1, nsa_cmp_w2,
              nsa_w_out, diff_w_in, diff_lambda, diff_head_g, diff_w_out, mla_w_in, mla_q_g,
              mla_w_qb, mla_kv_g, mla_w_kvb, mla_w_out):
    c_act = jax.nn.silu(c)
    for i in range(DEPTH):
        m, j = i % N_MIXERS, i // N_MIXERS
        mod = c_act @ ada_w[i] + ada_b[i]
        shift1, scale1, gate1, shift2, scale2, gate2 = jnp.split(mod, 6, axis=-1)
        h = modulate(x, norm_g[i, 0], shift1, scale1)
        if m == 0:
            y = stick_breaking_attention(h, sb_w_in[j], sb_w_out[j])
        elif m == 1:
            y = nsa_attention(h, nsa_w_in[j], nsa_cmp_pe[j], nsa_cmp_w1[j], nsa_cmp_w2[j],
                              nsa_w_out[j], t5_bias)
        elif m == 2:
            y = diff_attention(h, diff_w_in[j], diff_lambda[j], diff_head_g[j], diff_w_out[j],
                               t5_bias, 0.8 - 0.6 * math.exp(-0.3 * i))
        else:
            y = mla_attention(h, mla_w_in[j], mla_q_g[j], mla_w_qb[j], mla_kv_g[j],
                              mla_w_kvb[j], mla_w_out[j])
        x = x + gate1[:, None, :] * y
        h = modulate(x, norm_g[i, 1], shift2, scale2)
        x = x + gate2[:, None, :] * conv_ffn(h, ffn_w_up[i], ffn_conv_w[i], ffn_conv_b[i], ffn_w_down[i])
    return rmsnorm(x, final_g)
```

```python
import math
from contextlib import ExitStack

import numpy as np
import ml_dtypes

import concourse.bass as bass
import concourse.mybir as mybir
from concourse.bass_utils import run_bass_kernel_spmd

F32 = mybir.dt.float32
BF16 = mybir.dt.bfloat16
AF = mybir.ActivationFunctionType
ALU = mybir.AluOpType
AX = mybir.AxisListType

S = 4096
D = 2048
KC = 16
TB = 512
NB = S // TB
FF = 5632
HC = FF // 128
EPS = 1e-6
DEPTH = 4
NSLOT = 20
EPOCH = 50000


class Buf:
    __slots__ = ("w", "r")

    def __init__(self):
        self.w = None
        self.r = {}


class T:
    def __init__(self, t):
        self.t = t
        self.b = Buf()

    def __getitem__(self, idx):
        return self.t[idx]


def _b(x):
    return x.b if isinstance(x, T) else x


class Sched:
    def __init__(self, nc, stack):
        self.nc = nc
        self.stack = stack
        self.eng = {"pe": nc.tensor, "act": nc.scalar, "dve": nc.vector, "pool": nc.gpsimd, "sp": nc.sync}
        self.cnt = {k: 0 for k in ("pe", "act", "dve", "pool")}
        self.sems = {k: [] for k in self.cnt}
        self.waited = {k: {} for k in self.eng}
        self.dq = {}
        for q in ("sp", "pool"):
            self.dq[q] = {
                "sems": [stack.enter_context(nc.semaphore(f"dq_{q}_{i}")) for i in range(NSLOT)],
                "uses": [0] * NSLOT,
                "n": 0,
            }
        self.nwaits = 0

    def _sem(self, e, epoch):
        while len(self.sems[e]) <= epoch:
            self.sems[e].append(self.stack.enter_context(self.nc.semaphore(f"s_{e}_{len(self.sems[e])}")))
        return self.sems[e][epoch]

    def _wait(self, waiter, ev):
        if ev[0] == "e":
            key = ("e", ev[1])
            val = ev[2]
        else:
            key = ("d", ev[1], ev[2])
            val = ev[3]
        w = self.waited[waiter]
        if w.get(key, 0) >= val:
            return
        w[key] = val
        self.nwaits += 1
        if ev[0] == "e":
            epoch = (val - 1) // EPOCH
            self.eng[waiter].wait_ge(self._sem(ev[1], epoch), (val - 1) % EPOCH + 1)
        else:
            self.eng[waiter].wait_ge(self.dq[ev[1]]["sems"][ev[2]], 16 * val)

    def _deps(self, waiter, reads, writes, is_dma):
        for x in reads:
            b = _b(x)
            if b.w is not None:
                ev = b.w
                same = ev[0] == "e" and ev[1] == waiter and not is_dma
                if same and waiter == "pe":
                    continue
                self._wait(waiter, ev)
        for x in writes:
            b = _b(x)
            if b.w is not None:
                ev = b.w
                same = ev[0] == "e" and ev[1] == waiter and not is_dma
                if not same:
                    self._wait(waiter, ev)
            for ev in b.r.values():
                same = ev[0] == "e" and ev[1] == waiter and not is_dma
                if not same:
                    self._wait(waiter, ev)

    def op(self, e, fn, reads=(), writes=(), inc=True):
        self._deps(e, reads, writes, False)
        ins = fn(self.eng[e])
        if inc:
            self.cnt[e] += 1
            seq = self.cnt[e]
            ins.then_inc(self._sem(e, (seq - 1) // EPOCH), 1)
        else:
            seq = self.cnt[e] + 1
        ev = ("e", e, seq)
        for x in reads:
            _b(x).r[("e", e)] = ev
        for x in writes:
            b = _b(x)
            b.w = ev
            b.r = {}
        return ins

    def dma(self, q, out, in_, reads=(), writes=(), **kw):
        dq = self.dq[q]
        slot = dq["n"] % NSLOT
        dq["n"] += 1
        if dq["uses"][slot] > 0:
            self._wait(q, ("d", q, slot, dq["uses"][slot]))
        self._deps(q, reads, writes, True)
        ins = self.eng[q].dma_start(out=out, in_=in_, **kw)
        ins.then_inc(dq["sems"][slot], 16)
        dq["uses"][slot] += 1
        ev = ("d", q, slot, dq["uses"][slot])
        for x in reads:
            _b(x).r[("d", q, slot)] = ev
        for x in writes:
            b = _b(x)
            b.w = ev
            b.r = {}
        return ins

    def barrier(self):
        evs = []
        for e in self.cnt:
            if self.cnt[e] > 0:
                evs.append(("e", e, self.cnt[e]))
        for q, dq in self.dq.items():
            for s in range(NSLOT):
                if dq["uses"][s] > 0:
                    evs.append(("d", q, s, dq["uses"][s]))
        for waiter in self.eng:
            for ev in evs:
                if ev[0] == "e" and ev[1] == waiter:
                    continue
                self._wait(waiter, ev)


class Ctx:
    pass


_uid = [0]


def sb(cx, ph, shape, dtype, name="t"):
    _uid[0] += 1
    return T(ph.enter_context(cx.nc.sbuf_tensor(f"{name}_{_uid[0]}", list(shape), dtype)))


def ps(cx, ph, shape, dtype=F32, name="p"):
    _uid[0] += 1
    return T(ph.enter_context(cx.nc.psum_tensor(f"{name}_{_uid[0]}", list(shape), dtype)))


def mm(cx, out_t, out_ap, lhsT_t, lhsT_ap, rhs_t, rhs_ap, start, stop, inc=None):
    if inc is None:
        inc = stop
    cx.s.op("pe", lambda e: e.matmul(out_ap, lhsT_ap, rhs_ap, start=start, stop=stop),
            reads=[lhsT_t, rhs_t], writes=[out_t], inc=inc)


def load_cols(cx, ph, dst, dst_ap_fn, src_rows_ap, nrows):
    s = cx.s
    with ExitStack() as lph:
        st = sb(cx, lph, [nrows, 128], F32, "lc_st")
        s.dma("sp", st[:], src_rows_ap, writes=[st])
        pt = ps(cx, lph, [128, nrows], F32, "lc_ps")
        s.op("pe", lambda e: e.transpose(pt[:], st[:], cx.ident_f[0:nrows, 0:nrows]), reads=[st, cx.ident_f], writes=[pt])
        s.op("dve", lambda e: e.tensor_copy(out=dst_ap_fn(), in_=pt[:]), reads=[pt], writes=[dst])
        s.barrier()


def prologue(cx):
    s = cx.s
    with ExitStack() as ph:
        xin = [sb(cx, ph, [128, D], F32, "xin") for _ in range(2)]
        xst = [sb(cx, ph, [128, KC, 128], F32, "xst") for _ in range(2)]
        pts = [ps(cx, ph, [128, 4, 128], F32, "ptp") for _ in range(4)]
        k = 0
        for tt in range(S // 128):
            xi = xin[tt % 2]
            xs = xst[tt % 2]
            s.dma("sp", xi[:], cx.x[tt * 128:(tt + 1) * 128, :], writes=[xi])
            for g in range(4):
                pt = pts[k % 4]
                k += 1
                for j in range(4):
                    kc = g * 4 + j
                    s.op("pe", lambda e: e.transpose(pt[:, j, :], xi[:, kc * 128:(kc + 1) * 128], cx.ident_f[:]),
                         reads=[xi, cx.ident_f], writes=[pt], inc=(j == 3))
                if g % 2 == 0:
                    s.op("act", lambda e: e.copy(out=xs[:, g * 4:(g + 1) * 4, :], in_=pt[:]), reads=[pt], writes=[xs])
                else:
                    s.op("dve", lambda e: e.tensor_copy(out=xs[:, g * 4:(g + 1) * 4, :], in_=pt[:]), reads=[pt], writes=[xs])
            s.dma("sp", cx.xT_pk[:, :, tt * 128:(tt + 1) * 128], xs[:], reads=[xs], writes=[cx.xT_b[tt // 4]])
        s.barrier()


def epilogue(cx, final_norm):
    s = cx.s
    with ExitStack() as ph:
        gcol = sb(cx, ph, [128, KC], F32, "fg")
        if final_norm:
            load_cols(cx, ph, gcol, lambda: gcol[:], cx.final_g.rearrange("(r p) -> r p", p=128), KC)
        xb = [sb(cx, ph, [128, KC, TB], F32, "exb") for _ in range(2)]
        sq = [sb(cx, ph, [128, TB], F32, "esq") for _ in range(2)]
        rs = sb(cx, ph, [128, TB], F32, "ers")
        ot = [sb(cx, ph, [128, D], F32, "eot") for _ in range(2)]
        pss = ps(cx, ph, [128, TB], F32, "epss")
        pts = [ps(cx, ph, [128, 4, 128], F32, "eptp") for _ in range(4)]
        k = 0
        for tb in range(NB):
            x_ = xb[tb % 2]
            s.dma("sp", x_[:], cx.xT_pk[:, :, tb * TB:(tb + 1) * TB], reads=[cx.xT_b[tb]], writes=[x_])
            if final_norm:
                rstd_block(cx, x_, sq, pss, rs, D)
                for kc in range(KC):
                    s.op("pool", lambda e: e.tensor_tensor(out=x_[:, kc, :], in0=x_[:, kc, :], in1=rs[:], op=ALU.mult),
                         reads=[x_, rs], writes=[x_])
                    s.op("dve", lambda e: e.tensor_scalar(out=x_[:, kc, :], in0=x_[:, kc, :], scalar1=gcol[:, kc:kc + 1],
                                                          scalar2=None, op0=ALU.mult), reads=[x_, gcol], writes=[x_])
            for t4 in range(TB // 128):
                tt = tb * (TB // 128) + t4
                o_ = ot[tt % 2]
                for g in range(4):
                    pt = pts[k % 4]
                    k += 1
                    for j in range(4):
                        kc = g * 4 + j
                        s.op("pe", lambda e: e.transpose(pt[:, j, :], x_[:, kc, t4 * 128:(t4 + 1) * 128], cx.ident_f[:]),
                             reads=[x_, cx.ident_f], writes=[pt], inc=(j == 3))
                    dst = o_[:, g * 512:(g + 1) * 512]
                    src = pt[:].rearrange("p a b -> p (a b)")
                    if g % 2 == 0:
                        s.op("act", lambda e: e.copy(out=dst, in_=src), reads=[pt], writes=[o_])
                    else:
                        s.op("dve", lambda e: e.tensor_copy(out=dst, in_=src), reads=[pt], writes=[o_])
                s.dma("sp", cx.out[tt * 128:(tt + 1) * 128, :], o_[:], reads=[o_])
        s.barrier()


def rstd_block(cx, x_, sq, pss, rs, nfeat, nch=None, eps=EPS):
    s = cx.s
    if nch is None:
        nch = nfeat // 128
    for kc in range(nch):
        q = sq[kc % 2]
        s.op("act", lambda e: e.activation(out=q[:], in_=x_[:, kc, :], func=AF.Square), reads=[x_], writes=[q])
        mm(cx, pss, pss[:], cx.ones_f, cx.ones_f[:], q, q[:], start=(kc == 0), stop=(kc == nch - 1), inc=True)
    s.op("dve", lambda e: e.tensor_scalar(out=rs[:], in0=pss[:], scalar1=1.0 / nfeat, scalar2=eps, op0=ALU.mult, op1=ALU.add),
         reads=[pss], writes=[rs])
    s.op("act", lambda e: e.activation(out=rs[:], in_=rs[:], func=AF.Sqrt), reads=[rs], writes=[rs])
    s.op("dve", lambda e: e.reciprocal(out=rs[:], in_=rs[:]), reads=[rs], writes=[rs])


def ada_phase(cx, i):
    s = cx.s
    with ExitStack() as ph:
        wt = [sb(cx, ph, [128, KC, 512], F32, "adaw") for _ in range(2)]
        pm = ps(cx, ph, [128, 96], F32, "adap")
        bcol = sb(cx, ph, [128, 96], F32, "adab")
        load_cols(cx, ph, bcol, lambda: bcol[:], cx.ada_b[i].rearrange("(r p) -> r p", p=128), 96)
        gcol = sb(cx, ph, [128, 2 * KC], F32, "ng")
        load_cols(cx, ph, gcol, lambda: gcol[:], cx.norm_g[i].rearrange("a (r p) -> (a r) p", p=128), 2 * KC)
        wsrc = cx.ada_w[i].rearrange("(kc p) f -> p kc f", p=128)
        nblk = 6 * D // 512
        for blk in range(nblk):
            w = wt[blk % 2]
            s.dma("sp", w[:], wsrc[:, :, blk * 512:(blk + 1) * 512], writes=[w])
            for j in range(4):
                fc = blk * 4 + j
                for kc in range(KC):
                    mm(cx, pm, pm[:, fc:fc + 1], w, w[:, kc, j * 128:(j + 1) * 128], cx.cact, cx.cact[:, kc:kc + 1],
                       start=(kc == 0), stop=(kc == KC - 1))
        m = cx.modc
        s.op("dve", lambda e: e.tensor_tensor(out=m[:, 0:96], in0=pm[:], in1=bcol[:], op=ALU.add), reads=[pm, bcol], writes=[m])
        for a, sc0 in ((0, 16), (1, 64)):
            dst = m[:, 96 + a * 16:112 + a * 16]
            s.op("dve", lambda e: e.tensor_scalar(out=dst, in0=m[:, sc0:sc0 + 16], scalar1=1.0, scalar2=None, op0=ALU.add),
                 reads=[m], writes=[m])
            s.op("dve", lambda e: e.tensor_tensor(out=dst, in0=dst, in1=gcol[:, a * 16:(a + 1) * 16], op=ALU.mult),
                 reads=[m, gcol], writes=[m])
        s.barrier()


def SHIFT(a):
    return 0 if a == 0 else 48


def GATE(a):
    return 32 if a == 0 else 80


def GM(a):
    return 96 if a == 0 else 112


def norm_block(cx, a, x_, h_, sq, pss, rs, tmp):
    s = cx.s
    rstd_block(cx, x_, sq, pss, rs, D)
    m = cx.modc
    for kc in range(KC):
        t_ = tmp[kc % 2]
        s.op("dve", lambda e: e.tensor_tensor(out=t_[:], in0=x_[:, kc, :], in1=rs[:], op=ALU.mult), reads=[x_, rs], writes=[t_])
        s.op("dve", lambda e: e.tensor_scalar(out=h_[:, kc, :], in0=t_[:], scalar1=m[:, GM(a) + kc:GM(a) + kc + 1],
                                              scalar2=m[:, SHIFT(a) + kc:SHIFT(a) + kc + 1], op0=ALU.mult, op1=ALU.add),
             reads=[t_, m], writes=[h_])


def ffn_precast(cx, i):
    s = cx.s
    for r in range(8):
        s.dma("pool", cx.wup_b[r * 256:(r + 1) * 256, :], cx.ffn_w_up[i][r * 256:(r + 1) * 256, :], writes=[cx.wup_buf])
    for r in range(4):
        s.dma("pool", cx.wdn_b[r * 1408:(r + 1) * 1408, :], cx.ffn_w_down[i][r * 1408:(r + 1) * 1408, :], writes=[cx.wdn_buf])


MIX_W = {"sb": ("sb_w_in", 6144, "sb_w_out"), "nsa": ("nsa_w_in", 5168, "nsa_w_out"), "diff": ("diff_w_in", 6144, "diff_w_out"),
         "mla": (None, 0, "mla_w_out")}


def mixer_precast(cx, layer):
    if cx.mix_precast_done.get(layer):
        return
    cx.mix_precast_done[layer] = True
    s = cx.s
    win, F, wout = MIX_W[MIXERS[layer % 4]]
    j = layer // 4
    if win is not None:
        src = getattr(cx, win)[j]
        for r in range(8):
            s.dma("pool", cx.win_b[r * 256:(r + 1) * 256, 0:F], src[r * 256:(r + 1) * 256, :], writes=[cx.win_buf])
    src = getattr(cx, wout)[j]
    for r in range(2):
        s.dma("pool", cx.wout_b[r * 1024:(r + 1) * 1024, :], src[r * 1024:(r + 1) * 1024, :], writes=[cx.wout_buf])


def ffn_phase(cx, i):
    s = cx.s
    if i in cx.next_mixer:
        mixer_precast(cx, cx.next_mixer[i])
    if not cx.precast_done.get(i):
        ffn_precast(cx, i)
        cx.precast_done[i] = True
    with ExitStack() as ph:
        cw = sb(cx, ph, [128, 3 * 88], F32, "cw")
        cb = sb(cx, ph, [128, 88], F32, "cb")
        cwsrc = cx.ffn_conv_w[i].rearrange("w (r p) -> (w r) p", p=128)
        for part in range(3):
            load_cols(cx, ph, cw, lambda: cw[:, part * 88:(part + 1) * 88], cwsrc[part * 88:(part + 1) * 88, :], 88)
        load_cols(cx, ph, cb, lambda: cb[:], cx.ffn_conv_b[i].rearrange("(r p) -> r p", p=128), 88)
        carry = sb(cx, ph, [128, 88, 2], F32, "carry")
        s.op("dve", lambda e: e.memset(carry[:], 0.0), writes=[carry])

        xc = [sb(cx, ph, [128, TB], F32, "fxc") for _ in range(3)]
        h_ = sb(cx, ph, [128, KC, TB], BF16, "fh")
        aT = sb(cx, ph, [128, HC, TB], BF16, "faT")
        sq = [sb(cx, ph, [128, TB], F32, "fsq") for _ in range(2)]
        tmp = [sb(cx, ph, [128, TB], F32, "ftmp") for _ in range(2)]
        rs = sb(cx, ph, [128, TB], F32, "frs")
        wup = [sb(cx, ph, [128, KC, 2, 256], BF16, "wup") for _ in range(2)]
        wdn = [sb(cx, ph, [128, 22, 256], BF16, "wdn") for _ in range(2)]
        ug = [sb(cx, ph, [128, TB + 2], F32, "ug") for _ in range(2)]
        uu = [sb(cx, ph, [128, TB + 2], F32, "uu") for _ in range(2)]
        cg = [sb(cx, ph, [128, TB], F32, "cg") for _ in range(2)]
        cu = [sb(cx, ph, [128, TB], F32, "cu") for _ in range(2)]
        sg = [sb(cx, ph, [128, TB], F32, "sg") for _ in range(2)]
        xo = [sb(cx, ph, [128, TB], F32, "xo") for _ in range(2)]
        pss = ps(cx, ph, [128, TB], F32, "fpss")
        pg = [ps(cx, ph, [128, TB], F32, "fpg") for _ in range(2)]
        pu = [ps(cx, ph, [128, TB], F32, "fpu") for _ in range(2)]
        pd = [ps(cx, ph, [128, TB], F32, "fpd") for _ in range(2)]
        wsrc = cx.wup_b.rearrange("(kc p) f -> p kc f", p=128)
        dsrc = cx.wdn_b.rearrange("(hc p) f -> p hc f", p=128)
        m = cx.modc
        NJ2 = HC // 2

        def load_wup(j2, w):
            s.dma("sp", w[:, :, 0, :], wsrc[:, :, j2 * 256:(j2 + 1) * 256], reads=[cx.wup_buf], writes=[w])
            s.dma("sp", w[:, :, 1, :], wsrc[:, :, FF + j2 * 256:FF + (j2 + 1) * 256], reads=[cx.wup_buf], writes=[w])

        def load_wdn(idx, w):
            d2, half = idx // 2, idx % 2
            s.dma("sp", w[:], dsrc[:, half * 22:(half + 1) * 22, d2 * 256:(d2 + 1) * 256], reads=[cx.wdn_buf], writes=[w])

        nx = 0
        for tb in range(NB):
            t0 = tb * TB
            load_wup(0, wup[0])
            for kc in range(KC):
                x_ = xc[nx % 3]
                nx += 1
                s.dma("sp", x_[:], cx.xT[kc, :, t0:t0 + TB], reads=[cx.xT_b[tb]], writes=[x_])
                q = sq[kc % 2]
                s.op("act", lambda e: e.activation(out=q[:], in_=x_[:], func=AF.Square), reads=[x_], writes=[q])
                mm(cx, pss, pss[:], cx.ones_f, cx.ones_f[:], q, q[:], start=(kc == 0), stop=(kc == KC - 1), inc=True)
            s.op("dve", lambda e: e.tensor_scalar(out=rs[:], in0=pss[:], scalar1=1.0 / D, scalar2=EPS, op0=ALU.mult, op1=ALU.add),
                 reads=[pss], writes=[rs])
            s.op("act", lambda e: e.activation(out=rs[:], in_=rs[:], func=AF.Sqrt), reads=[rs], writes=[rs])
            s.op("dve", lambda e: e.reciprocal(out=rs[:], in_=rs[:]), reads=[rs], writes=[rs])
            for kc in range(KC):
                x_ = xc[nx % 3]
                nx += 1
                s.dma("sp", x_[:], cx.xT[kc, :, t0:t0 + TB], reads=[cx.xT_b[tb]], writes=[x_])
                t_ = tmp[kc % 2]
                s.op("dve", lambda e: e.tensor_tensor(out=t_[:], in0=x_[:], in1=rs[:], op=ALU.mult), reads=[x_, rs], writes=[t_])
                s.op("dve", lambda e: e.tensor_scalar(out=h_[:, kc, :], in0=t_[:], scalar1=m[:, GM(1) + kc:GM(1) + kc + 1],
                                                      scalar2=m[:, SHIFT(1) + kc:SHIFT(1) + kc + 1], op0=ALU.mult, op1=ALU.add),
                     reads=[t_, m], writes=[h_])
            for j2 in range(NJ2):
                w = wup[j2 % 2]
                if j2 + 1 < NJ2:
                    load_wup(j2 + 1, wup[(j2 + 1) % 2])
                else:
                    load_wdn(0, wdn[0])
                for jj in range(2):
                    j = j2 * 2 + jj
                    g_ps, u_ps = pg[j % 2], pu[j % 2]
                    for kc in range(KC):
                        mm(cx, g_ps, g_ps[:], w, w[:, kc, 0, jj * 128:(jj + 1) * 128], h_, h_[:, kc, :], start=(kc == 0), stop=(kc == KC - 1))
                    for kc in range(KC):
                        mm(cx, u_ps, u_ps[:], w, w[:, kc, 1, jj * 128:(jj + 1) * 128], h_, h_[:, kc, :], start=(kc == 0), stop=(kc == KC - 1))
                    for (p_, u_, c_, col) in ((g_ps, ug[j % 2], cg[j % 2], j), (u_ps, uu[j % 2], cu[j % 2], HC + j)):
                        s.op("act", lambda e: e.copy(out=u_[:, 2:TB + 2], in_=p_[:]), reads=[p_], writes=[u_])
                        s.op("dve", lambda e: e.tensor_copy(out=u_[:, 0:2], in_=carry[:, col, :]), reads=[carry], writes=[u_])
                        s.op("dve", lambda e: e.tensor_copy(out=carry[:, col, :], in_=u_[:, TB:TB + 2]), reads=[u_], writes=[carry])
                        s.op("dve", lambda e: e.tensor_scalar(out=c_[:], in0=u_[:, 2:TB + 2], scalar1=cw[:, 2 * 88 + col:2 * 88 + col + 1],
                                                              scalar2=cb[:, col:col + 1], op0=ALU.mult, op1=ALU.add),
                             reads=[u_, cw, cb], writes=[c_])
                        s.op("dve", lambda e: e.scalar_tensor_tensor(out=c_[:], in0=u_[:, 1:TB + 1], scalar=cw[:, 88 + col:88 + col + 1],
                                                                     in1=c_[:], op0=ALU.mult, op1=ALU.add),
                             reads=[u_, cw, c_], writes=[c_])
                        s.op("dve", lambda e: e.scalar_tensor_tensor(out=c_[:], in0=u_[:, 0:TB], scalar=cw[:, col:col + 1],
                                                                     in1=c_[:], op0=ALU.mult, op1=ALU.add),
                             reads=[u_, cw, c_], writes=[c_])
                    s_ = sg[j % 2]
                    c_g, c_u = cg[j % 2], cu[j % 2]
                    s.op("act", lambda e: e.activation(out=s_[:], in_=c_g[:], func=AF.Silu), reads=[c_g], writes=[s_])
                    s.op("dve", lambda e: e.tensor_tensor(out=aT[:, j, :], in0=s_[:], in1=c_u[:], op=ALU.mult), reads=[s_, c_u], writes=[aT])
            nw = 0
            for d2 in range(KC // 2):
                for half in range(2):
                    w = wdn[nw % 2]
                    if nw + 1 < KC:
                        load_wdn(nw + 1, wdn[(nw + 1) % 2])
                    for dd in range(2):
                        p_ = pd[dd]
                        for hh in range(22):
                            hc = half * 22 + hh
                            mm(cx, p_, p_[:], w, w[:, hh, dd * 128:(dd + 1) * 128], aT, aT[:, hc, :], start=(hc == 0), stop=(hc == HC - 1),
                               inc=(hh == 21))
                    nw += 1
                for dd in range(2):
                    dc = d2 * 2 + dd
                    p_ = pd[dd]
                    x_ = xc[nx % 3]
                    nx += 1
                    s.dma("sp", x_[:], cx.xT[dc, :, t0:t0 + TB], reads=[cx.xT_b[tb]], writes=[x_])
                    o_ = xo[dc % 2]
                    s.op("dve", lambda e: e.scalar_tensor_tensor(out=o_[:], in0=p_[:], scalar=m[:, GATE(1) + dc:GATE(1) + dc + 1],
                                                                 in1=x_[:], op0=ALU.mult, op1=ALU.add),
                         reads=[p_, m, x_], writes=[o_])
                    s.dma("sp", cx.xT[dc, :, t0:t0 + TB], o_[:], reads=[o_], writes=[cx.xT_b[tb]])
        s.barrier()


def pk(w2d):
    return w2d.rearrange("(kc p) f -> p kc f", p=128)


def inproj_phase(cx, groups, extra=None):
    s = cx.s
    with ExitStack() as ph:
        x_ = sb(cx, ph, [128, KC, TB], F32, "ix")
        h_ = sb(cx, ph, [128, KC, TB], BF16, "ih")
        sq = [sb(cx, ph, [128, TB], F32, "isq") for _ in range(2)]
        tmp = [sb(cx, ph, [128, TB], F32, "itmp") for _ in range(2)]
        rs = sb(cx, ph, [128, TB], F32, "irs")
        wt = [sb(cx, ph, [128, KC, 512], BF16, "iw") for _ in range(2)]
        stg = [sb(cx, ph, [128, 4, 512], BF16, "istg") for _ in range(2)]
        gst = sb(cx, ph, [128, TB], F32, "igst")
        pss = ps(cx, ph, [128, TB], F32, "ipss")
        pp = [ps(cx, ph, [128, 512], F32, "ipp") for _ in range(4)]
        ng = len(groups)
        seq = [(tb, gi) for tb in range(NB) for gi in range(ng)]

        def wl(idx):
            tb, gi = seq[idx]
            kind, wsrc, c0, ncols, _ = groups[gi]
            w = wt[idx % 2]
            s.dma("sp", w[:, :, 0:ncols], wsrc[:, :, c0:c0 + ncols], reads=[cx.win_buf], writes=[w])

        k = 0
        wl(0)
        for idx, (tb, gi) in enumerate(seq):
            t0 = tb * TB
            if gi == 0:
                s.dma("sp", x_[:], cx.xT_pk[:, :, t0:t0 + TB], reads=[cx.xT_b[tb]], writes=[x_])
                norm_block(cx, 0, x_, h_, sq, pss, rs, tmp)
            if idx + 1 < len(seq):
                wl(idx + 1)
            kind, wsrc, c0, ncols, store_fn = groups[gi]
            w = wt[idx % 2]
            st = stg[idx % 2]
            if kind == "gate":
                p_ = pp[k % 4]
                k += 1
                for kc in range(KC):
                    mm(cx, p_, p_[0:ncols, :], w, w[:, kc, 0:ncols], h_, h_[:, kc, :], start=(kc == 0), stop=(kc == KC - 1))
                s.op("act", lambda e: e.activation(out=gst[0:ncols, :], in_=p_[0:ncols, :], func=AF.Sigmoid), reads=[p_], writes=[gst])
                store_fn(tb, gst)
                continue
            if kind == "fm":
                nch = (ncols + 127) // 128
                for j in range(nch):
                    cw_ = min(128, ncols - j * 128)
                    p_ = pp[k % 4]
                    k += 1
                    for kc in range(KC):
                        mm(cx, p_, p_[0:cw_, :], w, w[:, kc, j * 128:j * 128 + cw_], h_, h_[:, kc, :], start=(kc == 0), stop=(kc == KC - 1))
                    if j % 2 == 0:
                        s.op("act", lambda e: e.copy(out=st[0:cw_, j, :], in_=p_[0:cw_, :]), reads=[p_], writes=[st])
                    else:
                        s.op("dve", lambda e: e.tensor_copy(out=st[0:cw_, j, :], in_=p_[0:cw_, :]), reads=[p_], writes=[st])
            else:
                for t4 in range(4):
                    p_ = pp[k % 4]
                    k += 1
                    for kc in range(KC):
                        mm(cx, p_, p_[:, 0:ncols], h_, h_[:, kc, t4 * 128:(t4 + 1) * 128], w, w[:, kc, 0:ncols], start=(kc == 0), stop=(kc == KC - 1))
                    if t4 % 2 == 0:
                        s.op("act", lambda e: e.copy(out=st[:, t4, 0:ncols], in_=p_[:, 0:ncols]), reads=[p_], writes=[st])
                    else:
                        s.op("dve", lambda e: e.tensor_copy(out=st[:, t4, 0:ncols], in_=p_[:, 0:ncols]), reads=[p_], writes=[st])
            store_fn(tb, st)
        s.barrier()
    maybe_precast(cx)


def maybe_precast(cx):
    i = getattr(cx, "cur_layer", None)
    if i is not None and cx.has_ffn.get(i) and not cx.precast_done.get(i):
        ffn_precast(cx, i)
        cx.precast_done[i] = True


def outproj_phase(cx, oT_pk, w2d):
    s = cx.s
    with ExitStack() as ph:
        x_ = [sb(cx, ph, [128, KC, TB], F32, "ox") for _ in range(2)]
        o_ = [sb(cx, ph, [128, KC, TB], BF16, "oo") for _ in range(2)]
        wt = [sb(cx, ph, [128, KC, 512], BF16, "ow") for _ in range(2)]
        xo = [sb(cx, ph, [128, TB], F32, "oxo") for _ in range(2)]
        pd = [ps(cx, ph, [128, TB], F32, "opd") for _ in range(2)]
        wsrc = pk(cx.wout_b)
        m = cx.modc
        n = 0
        nw = 0
        s.dma("sp", wt[0][:], wsrc[:, :, 0:512], reads=[cx.wout_buf], writes=[wt[0]])
        for tb in range(NB):
            t0 = tb * TB
            xb, ob = x_[tb % 2], o_[tb % 2]
            s.dma("sp", xb[:], cx.xT_pk[:, :, t0:t0 + TB], reads=[cx.xT_b[tb]], writes=[xb])
            s.dma("sp", ob[:], oT_pk[:, :, t0:t0 + TB], writes=[ob])
            for d4 in range(4):
                w = wt[nw % 2]
                if not (tb == NB - 1 and d4 == 3):
                    nxt = (d4 + 1) % 4
                    s.dma("sp", wt[(nw + 1) % 2][:], wsrc[:, :, nxt * 512:(nxt + 1) * 512], reads=[cx.wout_buf], writes=[wt[(nw + 1) % 2]])
                nw += 1
                for dd in range(4):
                    dc = d4 * 4 + dd
                    p_ = pd[n % 2]
                    for hc in range(KC):
                        mm(cx, p_, p_[:], w, w[:, hc, dd * 128:(dd + 1) * 128], ob, ob[:, hc, :], start=(hc == 0), stop=(hc == KC - 1))
                    xo_ = xo[n % 2]
                    s.op("dve", lambda e: e.scalar_tensor_tensor(out=xo_[:], in0=p_[:], scalar=m[:, GATE(0) + dc:GATE(0) + dc + 1],
                                                                 in1=xb[:, dc, :], op0=ALU.mult, op1=ALU.add),
                         reads=[p_, m, xb], writes=[xo_])
                    s.dma("sp", cx.xT[dc, :, t0:t0 + TB], xo_[:], reads=[xo_], writes=[cx.xT_b[tb]])
                    n += 1
        s.barrier()


def fm_store(cx, scr_pk, chunk0):
    def f(tb, st):
        cx.s.dma("sp", scr_pk[:, chunk0:chunk0 + 4, tb * TB:(tb + 1) * TB], st[:], reads=[st])
    return f


def tm_store(cx, scr, col0, ncols=512):
    v = scr.rearrange("(t p) c -> p t c", p=128)

    def f(tb, st):
        cx.s.dma("sp", v[:, tb * 4:tb * 4 + 4, col0:col0 + ncols], st[:, :, 0:ncols], reads=[st])
    return f


def sb_mixer(cx, j):
    s = cx.s
    nc = cx.nc
    w_in = pk(cx.win_b)
    qT, kT = cx.scrA, cx.scrB
    vv = cx.scrV
    qT_pk = qT.rearrange("k p t -> p k t")
    kT_pk = kT.rearrange("k p t -> p k t")
    groups = []
    for g in range(4):
        groups.append(("fm", w_in, g * 512, 512, fm_store(cx, qT_pk, g * 4)))
    for g in range(4):
        groups.append(("fm", w_in, 2048 + g * 512, 512, fm_store(cx, kT_pk, g * 4)))
    for g in range(4):
        groups.append(("tm", w_in, 4096 + g * 512, 512, tm_store(cx, vv, g * 512)))
    inproj_phase(cx, groups)

    scale = 128 ** -0.5
    oT_pk = cx.scrO.rearrange("k p t -> p k t")
    vview = vv.rearrange("(t p) c -> p t c", p=128)
    with ExitStack() as ph:
        NH = 4
        qh = [sb(cx, ph, [128, S], BF16, "sq") for _ in range(NH)]
        kh = [sb(cx, ph, [128, S], BF16, "sk") for _ in range(NH)]
        vh = [sb(cx, ph, [128, 32, 128], BF16, "sv") for _ in range(NH)]

        class CB:
            pass
        cbs = []
        for ci in range(2):
            cb_ = CB()
            cb_.e_ = [sb(cx, ph, [128, TB], F32, "se") for _ in range(2)]
            cb_.sp_ = [sb(cx, ph, [128, TB], F32, "ssp") for _ in range(2)]
            cb_.spb = [sb(cx, ph, [128, TB], BF16, "sspb") for _ in range(3)]
            cb_.u_ = [sb(cx, ph, [128, TB], F32, "su") for _ in range(3)]
            cb_.ar = [sb(cx, ph, [128, TB], F32, "sar") for _ in range(2)]
            cb_.aT = [sb(cx, ph, [128, TB], BF16, "saT") for _ in range(2)]
            cb_.ost = [sb(cx, ph, [128, TB], BF16, "sost") for _ in range(2)]
            cbs.append(cb_)
        for cb_ in cbs:
            cb_.zps = [ps(cx, ph, [128, TB], F32, "szp") for _ in range(2)]
            cb_.R = ps(cx, ph, [128, TB], F32, "sR")
            cb_.acc = ps(cx, ph, [128, TB], F32, "sacc")

        def load_head(h):
            i = h % NH
            s.dma("sp", qh[i][:], qT[h, :, :], writes=[qh[i]])
            s.dma("sp", kh[i][:], kT[h, :, :], writes=[kh[i]])
            s.dma("sp", vh[i][:], vview[:, :, h * 128:(h + 1) * 128], writes=[vh[i]])

        def make_chain(h, c, cb_):
            q_, k_, v_ = qh[h % NH], kh[h % NH], vh[h % NH]
            kts = list(range(4 * c + 3, -1, -1))
            nk = len(kts)
            zps, R, a_ps = cb_.zps, cb_.R, cb_.acc
            e_, sp_, spb, u_, ar, aT = cb_.e_, cb_.sp_, cb_.spb, cb_.u_, cb_.ar, cb_.aT

            def S1(i):
                kt = kts[i]
                z = zps[i % 2]
                mm(cx, z, z[:], k_, k_[:, kt * 128:(kt + 1) * 128], q_, q_[:, c * TB:(c + 1) * TB], True, True)

            def S2(i):
                kt = kts[i]
                i2, i3 = i % 2, i % 3
                z = zps[i2]
                s.op("act", lambda e: e.activation(out=e_[i2][:], in_=z[:], func=AF.Exp, scale=scale), reads=[z], writes=[e_[i2]])
                s.op("act", lambda e: e.activation(out=sp_[i2][:], in_=e_[i2][:], func=AF.Ln, bias=cx.ones_f[:, 0:1]),
                     reads=[e_[i2], cx.ones_f], writes=[sp_[i2]])
                s.op("dve", lambda e: e.scalar_tensor_tensor(out=u_[i3][:], in0=z[:], scalar=scale, in1=sp_[i2][:],
                                                             op0=ALU.mult, op1=ALU.subtract),
                     reads=[z, sp_[i2]], writes=[u_[i3]])
                if kt >= 4 * c:
                    mk = cx.mask_lt[kt - 4 * c]
                    s.op("pool", lambda e: e.tensor_tensor(out=spb[i3][:], in0=sp_[i2][:], in1=mk[:], op=ALU.mult),
                         reads=[sp_[i2], mk], writes=[spb[i3]])
                else:
                    s.op("dve", lambda e: e.tensor_copy(out=spb[i3][:], in_=sp_[i2][:]), reads=[sp_[i2]], writes=[spb[i3]])

            def S3(i):
                if i > 0:
                    j3 = (i - 1) % 3
                    mm(cx, R, R[:], cx.tri_le, cx.tri_le[:], spb[j3], spb[j3][:], False, True)
                mm(cx, R, R[:], cx.tri_gt, cx.tri_gt[:], spb[i % 3], spb[i % 3][:], i == 0, True)

            def S4(i):
                kt = kts[i]
                i2, i3 = i % 2, i % 3
                s.op("dve", lambda e: e.tensor_tensor(out=ar[i2][:], in0=u_[i3][:], in1=R[:], op=ALU.subtract),
                     reads=[u_[i3], R], writes=[ar[i2]])
                s.op("act", lambda e: e.activation(out=aT[i2][:], in_=ar[i2][:], func=AF.Exp), reads=[ar[i2]], writes=[aT[i2]])
                if kt >= 4 * c:
                    mkb = cx.mask_lt_b[kt - 4 * c]
                    s.op("pool", lambda e: e.tensor_tensor(out=aT[i2][:], in0=aT[i2][:], in1=mkb[:], op=ALU.mult),
                         reads=[aT[i2], mkb], writes=[aT[i2]])

            def S5(i):
                kt = kts[i]
                i2 = i % 2
                mm(cx, a_ps, a_ps[:], v_, v_[:, kt, :], aT[i2], aT[i2][:], start=(i == 0), stop=(i == nk - 1), inc=True)

            def fin():
                o_st = cb_.ost[c % 2]
                s.op("act", lambda e: e.copy(out=o_st[:], in_=a_ps[:]), reads=[a_ps], writes=[o_st])
                s.dma("sp", cx.scrO[h, :, c * TB:(c + 1) * TB], o_st[:], reads=[o_st])
            return nk, [S1, S2, S3, S4, S5], fin

        load_head(0)
        load_head(1)
        for hp in range(8):
            if hp + 1 < 8:
                load_head(2 * hp + 2)
                load_head(2 * hp + 3)
            for c in range(NB):
                chains = [make_chain(2 * hp + ci, c, cbs[ci]) for ci in range(2)]
                nk = chains[0][0]
                for t in range(nk + 4):
                    for si in range(4, -1, -1):
                        i = t - si
                        if 0 <= i < nk:
                            for ch in chains:
                                ch[1][si](i)
                for ch in chains:
                    ch[2]()
        s.barrier()
    outproj_phase(cx, oT_pk, cx.sb_w_out[j])


def recip_act(cx, out_t, in_t, tmp_t):
    cx.s.op("act", lambda e: e.activation(out=tmp_t[:], in_=in_t[:], func=AF.Ln), reads=[in_t], writes=[tmp_t])
    cx.s.op("act", lambda e: e.activation(out=out_t[:], in_=tmp_t[:], func=AF.Exp, scale=-1.0), reads=[tmp_t], writes=[out_t])


def pipeline(n, stages):
    ns = len(stages)
    for t in range(n + ns - 1):
        for si in range(ns - 1, -1, -1):
            i = t - si
            if 0 <= i < n:
                stages[si](i)


def softmax_chain(cx, blocks, sps, pT, sbf, dens, accs, scale, far_t=None):
    s = cx.s
    n = len(blocks)
    nm = len(dens)

    def S1(i):
        b = blocks[i]
        if b.get("pre") is not None:
            b["pre"]()
        for m_ in range(nm):
            sp_ = sps[m_][i % 2]
            sc = b["score"][m_]
            for k, (lt, la, rt, ra) in enumerate(sc):
                mm(cx, sp_, sp_[:], lt, la, rt, ra, start=(k == 0), stop=(k == len(sc) - 1))

    def S2(i):
        b = blocks[i]
        for m_ in range(nm):
            sp_ = sps[m_][i % 2]
            p_ = pT[m_][i % len(pT[m_])]
            if b.get("tab") is not None:
                tt, ta = b["tab"]
                b_ = sbf[(i * nm + m_) % len(sbf)]
                s.op("dve", lambda e: e.scalar_tensor_tensor(out=b_[:], in0=sp_[:], scalar=scale, in1=ta, op0=ALU.mult, op1=ALU.add),
                     reads=[sp_, tt], writes=[b_])
                s.op("act", lambda e: e.activation(out=p_[:], in_=b_[:], func=AF.Exp), reads=[b_], writes=[p_])
            elif b.get("bias") is not None:
                s.op("act", lambda e: e.activation(out=p_[:], in_=sp_[:], func=AF.Exp, scale=scale, bias=b["bias"]), reads=[sp_, far_t], writes=[p_])
            else:
                s.op("act", lambda e: e.activation(out=p_[:], in_=sp_[:], func=AF.Exp, scale=scale), reads=[sp_], writes=[p_])
            if b.get("mask") is not None:
                mt, ma = b["mask"]
                s.op(b.get("mask_eng", "pool"), lambda e: e.tensor_tensor(out=p_[:], in0=p_[:], in1=ma, op=ALU.mult), reads=[p_, mt], writes=[p_])

    def S3(i):
        b = blocks[i]
        vt, va = b["v"]
        for m_ in range(nm):
            p_ = pT[m_][i % len(pT[m_])]
            mm(cx, dens[m_], dens[m_][:], cx.ones_b, cx.ones_b[:], p_, p_[:], start=(i == 0), stop=(i == n - 1), inc=True)
            mm(cx, accs[m_], accs[m_][:], vt, va, p_, p_[:], start=(i == 0), stop=(i == n - 1), inc=True)
            if b.get("extra") is not None:
                et, ea, ep = b["extra"]
                mm(cx, ep, ep[:], et, ea, p_, p_[:], start=(i == 0), stop=(i == n - 1), inc=True)

    pipeline(n, [S1, S2, S3])


def mla_mixer(cx, j):
    s = cx.s
    w_in = pk(cx.mla_w_in[j])
    w_qb = pk(cx.mla_w_qb[j])
    w_kvb = pk(cx.mla_w_kvb[j])
    qnT, knT, qrT, krT, vv = cx.scrA, cx.scrB, cx.scrC, cx.scrD, cx.scrV
    vview = vv.rearrange("(t p) c -> p t c", p=128)
    with ExitStack() as ph:
        qg = sb(cx, ph, [128, 6], F32, "mqg")
        kg = sb(cx, ph, [128, 4], F32, "mkg")
        load_cols(cx, ph, qg, lambda: qg[:], cx.mla_q_g[j].rearrange("(r p) -> r p", p=128), 6)
        load_cols(cx, ph, kg, lambda: kg[:], cx.mla_kv_g[j].rearrange("(r p) -> r p", p=128), 4)
        x_ = sb(cx, ph, [128, KC, TB], F32, "mx")
        h_ = sb(cx, ph, [128, KC, TB], BF16, "mh")
        sq = [sb(cx, ph, [128, TB], F32, "msq") for _ in range(2)]
        tmp = [sb(cx, ph, [128, TB], F32, "mtmp") for _ in range(2)]
        rs = sb(cx, ph, [128, TB], F32, "mrs")
        w1 = sb(cx, ph, [128, KC, 1344], BF16, "mw1")
        cq = sb(cx, ph, [128, 6, TB], F32, "mcq")
        ckv = sb(cx, ph, [128, 4, TB], F32, "mckv")
        cqn = sb(cx, ph, [128, 6, TB], BF16, "mcqn")
        ckvn = sb(cx, ph, [128, 4, TB], BF16, "mckvn")
        cs = sb(cx, ph, [64, TB], F32, "mcos")
        sn = sb(cx, ph, [64, TB], F32, "msin")
        r1 = [sb(cx, ph, [64, TB], F32, "mr1") for _ in range(2)]
        r2 = [sb(cx, ph, [64, TB], F32, "mr2") for _ in range(2)]
        wq = [sb(cx, ph, [128, 6, 192], BF16, "mwq") for _ in range(2)]
        wk = [sb(cx, ph, [128, 4, 256], BF16, "mwk") for _ in range(2)]
        stq = [sb(cx, ph, [128, TB], BF16, "mstq") for _ in range(2)]
        str_ = [sb(cx, ph, [64, TB], BF16, "mstr") for _ in range(2)]
        stk = [sb(cx, ph, [128, TB], BF16, "mstk") for _ in range(2)]
        stv = [sb(cx, ph, [128, 4, 128], BF16, "mstv") for _ in range(2)]
        pss = ps(cx, ph, [128, TB], F32, "mpss")
        pp = [ps(cx, ph, [128, TB], F32, "mpp") for _ in range(3)]
        pr = [ps(cx, ph, [64, TB], F32, "mpr") for _ in range(4)]
        s.dma("pool", w1[:], w_in[:, :, :], writes=[w1])
        k = 0

        def rope(pa, pb, out_ap, out_t, i2):
            s.op("dve", lambda e: e.tensor_tensor(out=r1[i2][:], in0=pa[0:64, :], in1=cs[:], op=ALU.mult), reads=[pa, cs], writes=[r1[i2]])
            s.op("dve", lambda e: e.tensor_tensor(out=r2[i2][:], in0=pb[0:64, :], in1=sn[:], op=ALU.mult), reads=[pb, sn], writes=[r2[i2]])
            s.op("pool", lambda e: e.tensor_tensor(out=out_ap, in0=r1[i2][:], in1=r2[i2][:], op=ALU.add), reads=[r1[i2], r2[i2]], writes=[out_t])

        nr = 0
        for tb in range(NB):
            t0 = tb * TB
            s.dma("sp", x_[:], cx.xT_pk[:, :, t0:t0 + TB], reads=[cx.xT_b[tb]], writes=[x_])
            s.dma("sp", cs[:], cx.consts_d["rope_cos"][:, t0:t0 + TB], writes=[cs])
            s.dma("sp", sn[:], cx.consts_d["rope_sin"][:, t0:t0 + TB], writes=[sn])
            norm_block(cx, 0, x_, h_, sq, pss, rs, tmp)
            for jj in range(10):
                p_ = pp[k % 3]
                k += 1
                for kc in range(KC):
                    mm(cx, p_, p_[:], w1, w1[:, kc, jj * 128:(jj + 1) * 128], h_, h_[:, kc, :], start=(kc == 0), stop=(kc == KC - 1))
                dst_t, dst = (cq, cq[:, jj, :]) if jj < 6 else (ckv, ckv[:, jj - 6, :])
                if jj % 2 == 0:
                    s.op("act", lambda e: e.copy(out=dst, in_=p_[:]), reads=[p_], writes=[dst_t])
                else:
                    s.op("dve", lambda e: e.tensor_copy(out=dst, in_=p_[:]), reads=[p_], writes=[dst_t])
            pa, pb = pr[0], pr[1]
            for kc in range(KC):
                mm(cx, pa, pa[0:64, :], w1, w1[:, kc, 1280:1344], h_, h_[:, kc, :], start=(kc == 0), stop=(kc == KC - 1))
            for kc in range(KC):
                mm(cx, pb, pb[0:32, :], w1, w1[:, kc, 1312:1344], h_, h_[:, kc, :], start=(kc == 0), stop=(kc == KC - 1))
            for kc in range(KC):
                mm(cx, pb, pb[32:64, :], w1, w1[:, kc, 1280:1312], h_, h_[:, kc, :], start=(kc == 0), stop=(kc == KC - 1))
            st_ = str_[nr % 2]
            rope(pa, pb, st_[:], st_, nr % 2)
            s.dma("sp", krT[:, t0:t0 + TB], st_[:], reads=[st_])
            nr += 1
            for (src, dstn, g_, nch, nf) in ((cq, cqn, qg, 6, 768), (ckv, ckvn, kg, 4, 512)):
                rstd_block(cx, src, sq, pss, rs, nf, nch=nch)
                for kc in range(nch):
                    t_ = tmp[kc % 2]
                    s.op("pool", lambda e: e.tensor_tensor(out=t_[:], in0=src[:, kc, :], in1=rs[:], op=ALU.mult), reads=[src, rs], writes=[t_])
                    s.op("dve", lambda e: e.tensor_scalar(out=dstn[:, kc, :], in0=t_[:], scalar1=g_[:, kc:kc + 1], scalar2=None, op0=ALU.mult),
                         reads=[t_, g_], writes=[dstn])
            def load_hw(h):
                s.dma("pool", wq[h % 2][:], w_qb[:, :, h * 192:(h + 1) * 192], writes=[wq[h % 2]])
                s.dma("pool", wk[h % 2][:], w_kvb[:, :, h * 256:(h + 1) * 256], writes=[wk[h % 2]])

            load_hw(0)
            for h in range(16):
                wq_, wk_ = wq[h % 2], wk[h % 2]
                p_ = pp[k % 3]
                k += 1
                for kc in range(6):
                    mm(cx, p_, p_[:], wq_, wq_[:, kc, 0:128], cqn, cqn[:, kc, :], start=(kc == 0), stop=(kc == 5))
                sq_ = stq[h % 2]
                s.op("act", lambda e: e.copy(out=sq_[:], in_=p_[:]), reads=[p_], writes=[sq_])
                s.dma("sp", qnT[h, :, t0:t0 + TB], sq_[:], reads=[sq_])
                pa, pb = pr[2 * (h % 2)], pr[2 * (h % 2) + 1]
                for kc in range(6):
                    mm(cx, pa, pa[0:64, :], wq_, wq_[:, kc, 128:192], cqn, cqn[:, kc, :], start=(kc == 0), stop=(kc == 5))
                for kc in range(6):
                    mm(cx, pb, pb[0:32, :], wq_, wq_[:, kc, 160:192], cqn, cqn[:, kc, :], start=(kc == 0), stop=(kc == 5))
                for kc in range(6):
                    mm(cx, pb, pb[32:64, :], wq_, wq_[:, kc, 128:160], cqn, cqn[:, kc, :], start=(kc == 0), stop=(kc == 5))
                st_ = str_[nr % 2]
                rope(pa, pb, st_[:], st_, nr % 2)
                s.dma("sp", qrT[h, :, t0:t0 + TB], st_[:], reads=[st_])
                nr += 1
                p_ = pp[k % 3]
                k += 1
                for kc in range(4):
                    mm(cx, p_, p_[:], wk_, wk_[:, kc, 0:128], ckvn, ckvn[:, kc, :], start=(kc == 0), stop=(kc == 3))
                sk_ = stk[h % 2]
                s.op("dve", lambda e: e.tensor_copy(out=sk_[:], in_=p_[:]), reads=[p_], writes=[sk_])
                s.dma("sp", knT[h, :, t0:t0 + TB], sk_[:], reads=[sk_])
                p_ = pp[k % 3]
                k += 1
                for t4 in range(4):
                    for kc in range(4):
                        mm(cx, p_, p_[:, t4 * 128:(t4 + 1) * 128], ckvn, ckvn[:, kc, t4 * 128:(t4 + 1) * 128], wk_, wk_[:, kc, 128:256],
                           start=(kc == 0), stop=(kc == 3))
                if h + 1 < 16:
                    load_hw(h + 1)
                sv_ = stv[h % 2]
                s.op("act", lambda e: e.copy(out=sv_[:].rearrange("p a b -> p (a b)"), in_=p_[:]), reads=[p_], writes=[sv_])
                s.dma("sp", vview[:, tb * 4:tb * 4 + 4, h * 128:(h + 1) * 128], sv_[:], reads=[sv_])
        s.barrier()
    maybe_precast(cx)

    scale = 192 ** -0.5
    with ExitStack() as ph:
        qn = [sb(cx, ph, [128, S], BF16, "aqn") for _ in range(2)]
        qr = [sb(cx, ph, [64, S], BF16, "aqr") for _ in range(2)]
        kn = [sb(cx, ph, [128, S], BF16, "akn") for _ in range(2)]
        kr = sb(cx, ph, [64, S], BF16, "akr")
        vh = [sb(cx, ph, [128, 32, 128], BF16, "av") for _ in range(2)]
        pT = [sb(cx, ph, [128, TB], BF16, "apT") for _ in range(3)]
        rr = sb(cx, ph, [128, TB], F32, "arr")
        rr2 = sb(cx, ph, [128, TB], F32, "arr2")
        ost = [sb(cx, ph, [128, TB], BF16, "aost") for _ in range(2)]
        sps4 = [ps(cx, ph, [128, TB], F32, "asp") for _ in range(4)]
        den = [ps(cx, ph, [128, TB], F32, "aden") for _ in range(2)]
        acc = [ps(cx, ph, [128, TB], F32, "aacc") for _ in range(2)]
        s.dma("sp", kr[:], krT[:, :], writes=[kr])

        def load_head(h, i):
            s.dma("sp", qn[i][:], qnT[h, :, :], writes=[qn[i]])
            s.dma("sp", qr[i][:], qrT[h, :, :], writes=[qr[i]])
            s.dma("sp", kn[i][:], knT[h, :, :], writes=[kn[i]])
            s.dma("sp", vh[i][:], vview[:, :, h * 128:(h + 1) * 128], writes=[vh[i]])

        load_head(0, 0)
        nq = 0
        for h in range(16):
            if h + 1 < 16:
                load_head(h + 1, (h + 1) % 2)
            i = h % 2
            for c in range(NB):
                d_, a_ = den[nq % 2], acc[nq % 2]
                nkt = 4 * c + 4
                qs = slice(c * TB, (c + 1) * TB)
                blocks = []
                for kt in range(nkt):
                    ks = slice(kt * 128, (kt + 1) * 128)
                    blk = {"score": [[(kn[i], kn[i][:, ks], qn[i], qn[i][:, qs]), (kr, kr[:, ks], qr[i], qr[i][:, qs])]],
                           "v": (vh[i], vh[i][:, kt, :])}
                    if kt >= 4 * c:
                        mk = cx.mask_le_b[kt - 4 * c]
                        blk["mask"] = (mk, mk[:])
                    blocks.append(blk)
                softmax_chain(cx, blocks, [sps4[(nq % 2) * 2:(nq % 2) * 2 + 2]], [pT], None, [d_], [a_], scale)
                recip_act(cx, rr, d_, rr2)
                o_st = ost[nq % 2]
                s.op("dve", lambda e: e.tensor_tensor(out=o_st[:], in0=a_[:], in1=rr[:], op=ALU.mult), reads=[a_, rr], writes=[o_st])
                s.dma("sp", cx.scrO[h, :, c * TB:(c + 1) * TB], o_st[:], reads=[o_st])
                nq += 1
        s.barrier()
    outproj_phase(cx, cx.scrO.rearrange("k p t -> p k t"), cx.mla_w_out[j])


def qkv_groups(cx, w_in):
    qT_pk = cx.scrA.rearrange("k p t -> p k t")
    kT_pk = cx.scrB.rearrange("k p t -> p k t")
    groups = []
    for g in range(4):
        groups.append(("fm", w_in, g * 512, 512, fm_store(cx, qT_pk, g * 4)))
    for g in range(4):
        groups.append(("fm", w_in, 2048 + g * 512, 512, fm_store(cx, kT_pk, g * 4)))
    for g in range(4):
        groups.append(("tm", w_in, 4096 + g * 512, 512, tm_store(cx, cx.scrV, g * 512)))
    return groups


def diff_mixer(cx, j, layer):
    s = cx.s
    lam_init = 0.8 - 0.6 * math.exp(-0.3 * layer)
    inproj_phase(cx, qkv_groups(cx, pk(cx.win_b)))
    qT, kT, vv = cx.scrA, cx.scrB, cx.scrV
    vview = vv.rearrange("(t p) c -> p t c", p=128)
    scale = 64 ** -0.5
    with ExitStack() as ph:
        hg = sb(cx, ph, [128, 1], F32, "dhg")
        neglam = sb(cx, ph, [128, 1], F32, "dnl")
        far = sb(cx, ph, [128, 16], F32, "dfar")
        load_cols(cx, ph, hg, lambda: hg[:], cx.diff_head_g[j].rearrange("(r p) -> r p", p=128), 1)
        s.op("dve", lambda e: e.tensor_scalar(out=hg[:], in0=hg[:], scalar1=1.0 - lam_init, scalar2=None, op0=ALU.mult), reads=[hg], writes=[hg])
        s.dma("sp", far[:], cx.consts_d["t5far"][:, :], writes=[far])
        with ExitStack() as lp:
            lrow = sb(cx, lp, [1, 256], F32, "dlrow")
            lpr = sb(cx, lp, [1, 128], F32, "dlpr")
            lsum = sb(cx, lp, [1, 2], F32, "dlsum")
            lex = sb(cx, lp, [1, 2], F32, "dlex")
            lv = sb(cx, lp, [1, 1], F32, "dlv")
            lps = ps(cx, lp, [128, 1], F32, "dlps")
            s.dma("sp", lrow[:], cx.diff_lambda[j].rearrange("a d -> (a d)").rearrange("(o n) -> o n", o=1), writes=[lrow])
            lr3 = lrow[:].rearrange("o (a d) -> o a d", a=4)
            s.op("dve", lambda e: e.tensor_tensor(out=lpr[:].rearrange("o (a d) -> o a d", a=2), in0=lr3[:, 0:4:2, :], in1=lr3[:, 1:4:2, :], op=ALU.mult),
                 reads=[lrow], writes=[lpr])
            s.op("dve", lambda e: e.reduce_sum(out=lsum[:], in_=lpr[:].rearrange("o (a d) -> o a d", a=2), axis=AX.X), reads=[lpr], writes=[lsum])
            s.op("act", lambda e: e.activation(out=lex[:], in_=lsum[:], func=AF.Exp), reads=[lsum], writes=[lex])
            s.op("dve", lambda e: e.tensor_tensor(out=lv[:], in0=lex[:, 1:2], in1=lex[:, 0:1], op=ALU.subtract), reads=[lex], writes=[lv])
            s.op("dve", lambda e: e.tensor_scalar(out=lv[:], in0=lv[:], scalar1=-lam_init, scalar2=None, op0=ALU.add), reads=[lv], writes=[lv])
            mm(cx, lps, lps[:], cx.ones_f, cx.ones_f[0:1, :], lv, lv[:], True, True)
            s.op("dve", lambda e: e.tensor_copy(out=neglam[:], in_=lps[:]), reads=[lps], writes=[neglam])
            s.barrier()

        qh = [sb(cx, ph, [128, S], BF16, "dq") for _ in range(2)]
        kh = [sb(cx, ph, [128, S], BF16, "dk") for _ in range(2)]
        vh = [sb(cx, ph, [128, 32, 128], BF16, "dv") for _ in range(2)]
        tab = [sb(cx, ph, [128, 5, TB], F32, "dtab") for _ in range(2)]
        sbf = [sb(cx, ph, [128, TB], F32, "dsbf") for _ in range(4)]
        pT = [sb(cx, ph, [128, TB], BF16, "dpT") for _ in range(4)]
        rr = [sb(cx, ph, [128, TB], F32, "drr") for _ in range(2)]
        t0_ = sb(cx, ph, [128, TB], F32, "dt0")
        t1_ = sb(cx, ph, [128, TB], F32, "dt1")
        o_ = sb(cx, ph, [128, TB], F32, "do")
        osq = sb(cx, ph, [128, TB], F32, "dosq")
        rs = sb(cx, ph, [128, TB], F32, "drs")
        ost = [sb(cx, ph, [128, TB], BF16, "dost") for _ in range(2)]
        sps = [ps(cx, ph, [128, TB], F32, "dsp") for _ in range(4)]
        den = [ps(cx, ph, [128, TB], F32, "dden") for _ in range(2)]
        acc = [ps(cx, ph, [128, TB], F32, "dacc") for _ in range(2)]
        pn = den[0]

        def load_head(h, i):
            s.dma("sp", qh[i][:], qT[h, :, :], writes=[qh[i]])
            s.dma("sp", kh[i][:], kT[h, :, :], writes=[kh[i]])
            s.dma("sp", vh[i][:], vview[:, :, h * 128:(h + 1) * 128], writes=[vh[i]])
            s.dma("sp", tab[i][:], cx.consts_d["t5tab"][h].rearrange("m p q -> p m q"), writes=[tab[i]])

        load_head(0, 0)
        nq = 0
        for h in range(16):
            if h + 1 < 16:
                load_head(h + 1, (h + 1) % 2)
            i = h % 2
            for c in range(NB):
                nkt = 4 * c + 4
                qs = slice(c * TB, (c + 1) * TB)
                blocks = []
                for kt in range(nkt):
                    ks = slice(kt * 128, (kt + 1) * 128)
                    mi = kt - 4 * c + 1
                    blk = {"score": [[(kh[i], kh[i][64 * m_:64 * m_ + 64, ks], qh[i], qh[i][64 * m_:64 * m_ + 64, qs])] for m_ in range(2)],
                           "v": (vh[i], vh[i][:, kt, :])}
                    if mi >= 0:
                        blk["tab"] = (tab[i], tab[i][:, mi, :])
                    else:
                        blk["bias"] = far[:, h:h + 1]
                    blocks.append(blk)
                softmax_chain(cx, blocks, [sps[0:2], sps[2:4]], [pT[0:2], pT[2:4]], sbf, den, acc, scale, far_t=far)
                for m_ in range(2):
                    recip_act(cx, rr[m_], den[m_], osq)
                s.op("dve", lambda e: e.tensor_tensor(out=t0_[:], in0=acc[0][:], in1=rr[0][:], op=ALU.mult), reads=[acc[0], rr[0]], writes=[t0_])
                s.op("dve", lambda e: e.tensor_tensor(out=t1_[:], in0=acc[1][:], in1=rr[1][:], op=ALU.mult), reads=[acc[1], rr[1]], writes=[t1_])
                s.op("dve", lambda e: e.scalar_tensor_tensor(out=o_[:], in0=t1_[:], scalar=neglam[:, 0:1], in1=t0_[:], op0=ALU.mult, op1=ALU.add),
                     reads=[t1_, neglam, t0_], writes=[o_])
                s.op("act", lambda e: e.activation(out=osq[:], in_=o_[:], func=AF.Square), reads=[o_], writes=[osq])
                mm(cx, pn, pn[:], cx.ones_f, cx.ones_f[:], osq, osq[:], True, True)
                s.op("dve", lambda e: e.tensor_scalar(out=rs[:], in0=pn[:], scalar1=1.0 / 128, scalar2=EPS, op0=ALU.mult, op1=ALU.add), reads=[pn], writes=[rs])
                s.op("act", lambda e: e.activation(out=rs[:], in_=rs[:], func=AF.Sqrt), reads=[rs], writes=[rs])
                s.op("dve", lambda e: e.reciprocal(out=rs[:], in_=rs[:]), reads=[rs], writes=[rs])
                s.op("pool", lambda e: e.tensor_tensor(out=o_[:], in0=o_[:], in1=rs[:], op=ALU.mult), reads=[o_, rs], writes=[o_])
                o_st = ost[nq % 2]
                s.op("dve", lambda e: e.tensor_scalar(out=o_st[:], in0=o_[:], scalar1=hg[:, 0:1], scalar2=None, op0=ALU.mult), reads=[o_, hg], writes=[o_st])
                s.dma("sp", cx.scrO[h, :, c * TB:(c + 1) * TB], o_st[:], reads=[o_st])
                nq += 1
        s.barrier()
    outproj_phase(cx, cx.scrO.rearrange("k p t -> p k t"), cx.diff_w_out[j])


def nsa_mixer(cx, j):
    s = cx.s
    w_in = pk(cx.win_b)
    qT, kvT, vv, gsc = cx.scrA, cx.scrB, cx.scrV, cx.scrG
    qT_pk = qT.rearrange("k p t -> p k t")
    kvT_pk = kvT.rearrange("k p t -> p k t")
    groups = []
    for g in range(4):
        groups.append(("fm", w_in, g * 512, 512, fm_store(cx, qT_pk, g * 4)))
    groups.append(("fm", w_in, 2048, 512, fm_store(cx, kvT_pk, 0)))
    groups.append(("fm", w_in, 2560, 512, fm_store(cx, kvT_pk, 4)))
    groups.append(("fm", w_in, 3072, 512, fm_store(cx, kvT_pk, 8)))
    groups.append(("tm", w_in, 3584, 512, tm_store(cx, vv, 0)))
    groups.append(("fm", w_in, 4096, 512, fm_store(cx, kvT_pk, 12)))
    groups.append(("tm", w_in, 4608, 512, tm_store(cx, vv, 512)))

    def gate_store(tb, st):
        s.dma("sp", gsc[:, tb * TB:(tb + 1) * TB], st[0:48, :], reads=[st])
    groups.append(("gate", w_in, 5120, 48, gate_store))
    inproj_phase(cx, groups)

    vview = vv.rearrange("(t p) c -> p t c", p=128)
    scale = 128 ** -0.5
    C = cx.consts_d
    with ExitStack() as ph:
        kcT = sb(cx, ph, [128, 4, 256], BF16, "nkcT")
        vc = sb(cx, ph, [128, 4, 2, 128], BF16, "nvc")
        s.op("dve", lambda e: e.memset(kcT[:], 0.0), writes=[kcT])
        s.op("dve", lambda e: e.memset(vc[:], 0.0), writes=[vc])
        with ExitStack() as cp:
            raw = [sb(cx, cp, [128, S], BF16, "nraw") for _ in range(2)]
            w1 = sb(cx, cp, [128, 32, 128], BF16, "nw1")
            w2 = sb(cx, cp, [128, 128], BF16, "nw2")
            pef = sb(cx, cp, [128, 32], F32, "npef")
            peb = sb(cx, cp, [128, 32], BF16, "npeb")
            b1 = sb(cx, cp, [128, 1], F32, "nb1")
            s1 = sb(cx, cp, [128, 256], BF16, "ns1")
            ph1 = ps(cx, cp, [128, 256], F32, "nph1")
            pb = ps(cx, cp, [128, 1], F32, "npb")
            pk2 = ps(cx, cp, [128, 256], F32, "npk2")
            pv = ps(cx, cp, [128, 128], F32, "npv")
            nr = 0
            for kv in range(2):
                s.dma("pool", w1[:], pk(cx.nsa_cmp_w1[j, kv]), writes=[w1])
                s.dma("pool", w2[:], cx.nsa_cmp_w2[j, kv], writes=[w2])
                load_cols(cx, cp, pef, lambda: pef[:], cx.nsa_cmp_pe[j, kv], 32)
                s.op("dve", lambda e: e.tensor_copy(out=peb[:], in_=pef[:]), reads=[pef], writes=[peb])
                for l in range(32):
                    mm(cx, pb, pb[:], w1, w1[:, l, :], peb, peb[:, l:l + 1], start=(l == 0), stop=(l == 31))
                s.op("dve", lambda e: e.tensor_copy(out=b1[:], in_=pb[:]), reads=[pb], writes=[b1])
                for g in range(4):
                    rw = raw[nr % 2]
                    nr += 1
                    s.dma("sp", rw[:], kvT[kv * 4 + g, :, :], writes=[rw])
                    r3 = rw[:].rearrange("p (c s) -> p c s", s=16)
                    for l in range(32):
                        rhs = r3[:, 0:255, l] if l < 16 else r3[:, 1:256, l - 16]
                        mm(cx, ph1, ph1[:, 0:255], w1, w1[:, l, :], rw, rhs, start=(l == 0), stop=(l == 31))
                    s.op("act", lambda e: e.activation(out=s1[:, 0:255], in_=ph1[:, 0:255], func=AF.Silu, bias=b1[:, 0:1]),
                         reads=[ph1, b1], writes=[s1])
                    if kv == 0:
                        mm(cx, pk2, pk2[:, 0:255], w2, w2[:], s1, s1[:, 0:255], True, True)
                        s.op("dve", lambda e: e.tensor_copy(out=kcT[:, g, 0:255], in_=pk2[:, 0:255]), reads=[pk2], writes=[kcT])
                    else:
                        for cc in range(2):
                            M = 128 if cc == 0 else 127
                            mm(cx, pv, pv[0:M, :], s1, s1[:, cc * 128:cc * 128 + M], w2, w2[:], True, True)
                            s.op("dve", lambda e: e.tensor_copy(out=vc[0:M, g, cc, :], in_=pv[0:M, :]), reads=[pv], writes=[vc])
            s.barrier()

        gbs = [sb(cx, ph, [128, TB], F32, "ngbs") for _ in range(2)]
        ovl = sb(cx, ph, [128, 2, 64], BF16, "novl")
        Eexp = sb(cx, ph, [64, S], BF16, "nE")
        far = sb(cx, ph, [128, 16], F32, "nfar")
        wmask = [sb(cx, ph, [128, TB], BF16, "nwm") for _ in range(4)]
        ksel = sb(cx, ph, [128, S], BF16, "nksel")
        kwin = sb(cx, ph, [128, S], BF16, "nkwin")
        vsel = sb(cx, ph, [128, 32, 128], BF16, "nvsel")
        vwin = sb(cx, ph, [128, 32, 128], BF16, "nvwin")
        qh = [sb(cx, ph, [128, S], BF16, "nq") for _ in range(4)]
        tabb = [sb(cx, ph, [128, 5, TB], F32, "ntab") for _ in range(2)]
        scA = sb(cx, ph, [128, 4, 64], F32, "nscA")
        scB = sb(cx, ph, [128, 4, 64], F32, "nscB")
        ctab = [sb(cx, ph, [128, TB], F32, "nctab") for _ in range(2)]
        sbf = [sb(cx, ph, [128, TB], F32, "nsbf") for _ in range(2)]
        pT = [sb(cx, ph, [128, TB], BF16, "npT") for _ in range(3)]
        mskall = sb(cx, ph, [128, 32, TB], BF16, "nmsk")
        rr = sb(cx, ph, [128, TB], F32, "nrr")
        fbr = sb(cx, ph, [128, TB], F32, "nfbr")
        tmpo = sb(cx, ph, [128, TB], F32, "ntmpo")
        oacc = [sb(cx, ph, [128, TB], F32, "noacc") for _ in range(4)]
        ost = [sb(cx, ph, [128, TB], BF16, "nost") for _ in range(2)]
        impT = sb(cx, ph, [64, TB], F32, "nimpT")
        itmp = sb(cx, ph, [64, TB], F32, "nitmp")
        sc = [sb(cx, ph, [128, 64], F32, "nsc") for _ in range(4)]
        sc2 = [sb(cx, ph, [128, 64], F32, "nsc2") for _ in range(4)]
        mx8 = [sb(cx, ph, [128, 8], F32, "nmx8") for _ in range(4)]
        selq = [sb(cx, ph, [128, 64], F32, "nselq") for _ in range(4)]
        selT = sb(cx, ph, [64, TB], BF16, "nselT")
        sps = [ps(cx, ph, [128, TB], F32, "nsp") for _ in range(2)]
        dens = [ps(cx, ph, [128, TB], F32, "nden") for _ in range(2)]
        accs = [ps(cx, ph, [128, TB], F32, "nacc") for _ in range(2)]
        imp_ps = ps(cx, ph, [64, TB], F32, "nimp")
        msk_ps = ps(cx, ph, [128, TB], F32, "nmskp")
        tp = msk_ps

        s.dma("pool", ovl[:], C["ovl"].rearrange("c p n -> p c n"), writes=[ovl])
        s.dma("pool", Eexp[:], C["eexp"][:, :], writes=[Eexp])
        s.dma("sp", far[:], C["t5far"][:, :], writes=[far])
        for o_ in range(4):
            s.dma("pool", wmask[o_][:], C["wmask"][o_], writes=[wmask[o_]])

        state = {"n": 0, "no": 0, "ng": 0, "nt": 0, "nc": 0}

        def get_tab(hd):
            t_ = tabb[state["nt"] % 2]
            state["nt"] += 1
            s.dma("sp", t_[:], C["t5tab"][hd].rearrange("m p q -> p m q"), writes=[t_])
            return t_

        def finish(br, r, hd, qs, first):
            den, acc = dens[state["nc"] % 2], accs[state["nc"] % 2]
            s.op("dve", lambda e: e.tensor_scalar(out=rr[:], in0=den[:], scalar1=1e-30, scalar2=None, op0=ALU.max), reads=[den], writes=[rr])
            row = br * 16 + hd
            gb = gbs[state["ng"] % 2]
            state["ng"] += 1
            s.dma("sp", gb[:], gsc[row:row + 1, qs].partition_broadcast(128), writes=[gb])
            recip_act(cx, rr, rr, tmpo)
            s.op("dve", lambda e: e.tensor_tensor(out=fbr[:], in0=gb[:], in1=rr[:], op=ALU.mult), reads=[gb, rr], writes=[fbr])
            if first:
                s.op("dve", lambda e: e.tensor_tensor(out=oacc[r][:], in0=acc[:], in1=fbr[:], op=ALU.mult), reads=[acc, fbr], writes=[oacc[r]])
            else:
                s.op("dve", lambda e: e.tensor_tensor(out=tmpo[:], in0=acc[:], in1=fbr[:], op=ALU.mult), reads=[acc, fbr], writes=[tmpo])
                s.op("pool", lambda e: e.tensor_tensor(out=oacc[r][:], in0=oacc[r][:], in1=tmpo[:], op=ALU.add), reads=[oacc[r], tmpo], writes=[oacc[r]])
            state["nc"] += 1

        def chain(blocks):
            softmax_chain(cx, blocks, [sps], [pT], sbf, [dens[state["nc"] % 2]], [accs[state["nc"] % 2]], scale, far_t=far)

        for g in range(4):
            s.dma("sp", ksel[:], kvT[8 + g, :, :], writes=[ksel])
            s.dma("sp", kwin[:], kvT[12 + g, :, :], writes=[kwin])
            s.dma("sp", vsel[:], vview[:, :, g * 128:(g + 1) * 128], writes=[vsel])
            s.dma("sp", vwin[:], vview[:, :, 512 + g * 128:512 + (g + 1) * 128], writes=[vwin])
            for r in range(4):
                hd = g * 4 + r
                s.dma("sp", qh[r][:], qT[hd, :, :], writes=[qh[r]])
            for cq in range(NB):
                qs = slice(cq * TB, (cq + 1) * TB)
                s.dma("sp", scA[:], C["scA"][cq * 4:cq * 4 + 4].rearrange("t p n -> p t n"), writes=[scA])
                s.dma("sp", scB[:], C["scB"][cq * 4:cq * 4 + 4].rearrange("t p n -> p t n"), writes=[scB])
                ccs = [0] if cq <= 3 else [0, 1]
                for r in range(4):
                    hd = g * 4 + r
                    blocks = []
                    for cc in ccs:
                        m_ = 32 * cq - 128 * cc - 2
                        row0 = 222 - m_
                        ct = ctab[state["n"] % 2]
                        state["n"] += 1

                        def pre(ct=ct, row0=row0, hd=hd):
                            s.dma("sp", ct[:], C["t5cmp"][hd, row0:row0 + 128, :], writes=[ct])
                        blocks.append({"pre": pre, "score": [[(kcT, kcT[:, g, cc * 128:(cc + 1) * 128], qh[r], qh[r][:, qs])]],
                                       "tab": (ct, ct[:]), "v": (vc, vc[:, g, cc, :]), "extra": (ovl, ovl[:, cc, :], imp_ps)})
                    chain(blocks)
                    finish(0, r, hd, qs, True)
                    if r == 0:
                        s.op("dve", lambda e: e.tensor_tensor(out=impT[:], in0=imp_ps[:], in1=rr[0:64, :], op=ALU.mult), reads=[imp_ps, rr], writes=[impT])
                    else:
                        s.op("dve", lambda e: e.tensor_tensor(out=itmp[:], in0=imp_ps[:], in1=rr[0:64, :], op=ALU.mult), reads=[imp_ps, rr], writes=[itmp])
                        s.op("pool", lambda e: e.tensor_tensor(out=impT[:], in0=impT[:], in1=itmp[:], op=ALU.add), reads=[impT, itmp], writes=[impT])
                R4 = range(4)
                for t4 in R4:
                    s.op("pe", lambda e: e.transpose(tp[:, t4 * 64:(t4 + 1) * 64], impT[:, t4 * 128:(t4 + 1) * 128], cx.ident_f[0:64, 0:64]),
                         reads=[impT, cx.ident_f], writes=[tp])
                for t4 in R4:
                    s.op("dve", lambda e: e.tensor_tensor(out=sc[t4][:], in0=tp[:, t4 * 64:(t4 + 1) * 64], in1=scA[:, t4, :], op=ALU.mult), reads=[tp, scA], writes=[sc[t4]])
                for t4 in R4:
                    s.op("dve", lambda e: e.tensor_tensor(out=sc[t4][:], in0=sc[t4][:], in1=scB[:, t4, :], op=ALU.add), reads=[sc[t4], scB], writes=[sc[t4]])
                for t4 in R4:
                    s.op("dve", lambda e: e.max(out=mx8[t4][:], in_=sc[t4][:]), reads=[sc[t4]], writes=[mx8[t4]])
                for t4 in R4:
                    s.op("dve", lambda e: e.match_replace(out=sc2[t4][:], in_to_replace=mx8[t4][:], in_values=sc[t4][:], imm_value=-3.0), reads=[mx8[t4], sc[t4]], writes=[sc2[t4]])
                for t4 in R4:
                    s.op("dve", lambda e: e.max(out=mx8[t4][:], in_=sc2[t4][:]), reads=[sc2[t4]], writes=[mx8[t4]])
                for t4 in R4:
                    s.op("dve", lambda e: e.match_replace(out=sc2[t4][:], in_to_replace=mx8[t4][:], in_values=sc2[t4][:], imm_value=-3.0), reads=[mx8[t4], sc2[t4]], writes=[sc2[t4]])
                for t4 in R4:
                    s.op("dve", lambda e: e.tensor_tensor(out=selq[t4][:], in0=sc[t4][:], in1=sc2[t4][:], op=ALU.is_gt), reads=[sc[t4], sc2[t4]], writes=[selq[t4]])
                for t4 in R4:
                    s.op("pe", lambda e: e.transpose(tp[0:64, t4 * 128:(t4 + 1) * 128], selq[t4][:], cx.ident_f[:]), reads=[selq[t4], cx.ident_f], writes=[tp])
                s.op("dve", lambda e: e.tensor_copy(out=selT[:], in_=tp[0:64, :]), reads=[tp], writes=[selT])
                nkt = 4 * cq + 4
                for kt in range(nkt):
                    mm(cx, msk_ps, msk_ps[:], Eexp, Eexp[:, kt * 128:(kt + 1) * 128], selT, selT[:], True, True)
                    s.op("act", lambda e: e.copy(out=mskall[:, kt, :], in_=msk_ps[:]), reads=[msk_ps], writes=[mskall])
                for r in range(4):
                    hd = g * 4 + r
                    tb_ = get_tab(hd)
                    blocks = []
                    for kt in range(nkt):
                        mi = kt - 4 * cq + 1
                        ks = slice(kt * 128, (kt + 1) * 128)
                        blk = {"score": [[(ksel, ksel[:, ks], qh[r], qh[r][:, qs])]], "mask": (mskall, mskall[:, kt, :]),
                               "mask_eng": "dve", "v": (vsel, vsel[:, kt, :])}
                        if mi >= 0:
                            blk["tab"] = (tb_, tb_[:, mi, :])
                        else:
                            blk["bias"] = far[:, hd:hd + 1]
                        blocks.append(blk)
                    chain(blocks)
                    finish(1, r, hd, qs, False)
                for r in range(4):
                    hd = g * 4 + r
                    kts = list(range(max(0, 4 * cq - 4), 4 * cq + 4))
                    tb_ = get_tab(hd)
                    blocks = []
                    for kt in kts:
                        off = kt - 4 * cq
                        ks = slice(kt * 128, (kt + 1) * 128)
                        blk = {"score": [[(kwin, kwin[:, ks], qh[r], qh[r][:, qs])]], "v": (vwin, vwin[:, kt, :])}
                        if off >= 0:
                            blk["tab"] = (tb_, tb_[:, off + 1, :])
                        elif off == -1:
                            blk["tab"] = (tb_, tb_[:, 0, :])
                            blk["mask"] = (wmask[3], wmask[3][:])
                        else:
                            blk["bias"] = far[:, hd:hd + 1]
                            blk["mask"] = (wmask[off + 4], wmask[off + 4][:])
                        blocks.append(blk)
                    chain(blocks)
                    finish(2, r, hd, qs, False)
                    o_st = ost[state["no"] % 2]
                    state["no"] += 1
                    s.op("act", lambda e: e.copy(out=o_st[:], in_=oacc[r][:]), reads=[oacc[r]], writes=[o_st])
                    s.dma("sp", cx.scrO[hd, :, qs], o_st[:], reads=[o_st])
        s.barrier()
    outproj_phase(cx, cx.scrO.rearrange("k p t -> p k t"), cx.nsa_w_out[j])


def weight_shapes(nl):
    return {
        "ada_w": (nl, D, 6 * D), "ada_b": (nl, 6 * D), "norm_g": (nl, 2, D), "final_g": (D,),
        "ffn_w_up": (nl, D, 2 * FF), "ffn_conv_w": (nl, 3, 2 * FF), "ffn_conv_b": (nl, 2 * FF),
        "ffn_w_down": (nl, FF, D),
        "sb_w_in": (1, D, 6144), "sb_w_out": (1, D, D),
        "nsa_w_in": (1, D, 5168), "nsa_cmp_pe": (1, 2, 32, 128), "nsa_cmp_w1": (1, 2, 4096, 128),
        "nsa_cmp_w2": (1, 2, 128, 128), "nsa_w_out": (1, D, D),
        "diff_w_in": (1, D, 6144), "diff_lambda": (1, 4, 64), "diff_head_g": (1, 128), "diff_w_out": (1, D, D),
        "mla_w_in": (1, D, 1344), "mla_q_g": (1, 768), "mla_w_qb": (1, 768, 3072), "mla_kv_g": (1, 512),
        "mla_w_kvb": (1, 512, 4096), "mla_w_out": (1, D, D),
    }


def build(stages, nl=DEPTH):
    nc = bass.Bass("TRN2", target_bir_lowering=False)
    cx = Ctx()
    cx.nc = nc
    cx.x = nc.dram_tensor("x", [S, D], F32, kind="ExternalInput").ap()
    cx.c16 = nc.dram_tensor("c16", [KC, 128], F32, kind="ExternalInput").ap()
    cx.identf_d = nc.dram_tensor("identf", [128, 128], F32, kind="ExternalInput").ap()
    for name, shp in weight_shapes(nl).items():
        setattr(cx, name, nc.dram_tensor(name, list(shp), F32, kind="ExternalInput").ap())
    cx.out = nc.dram_tensor("out", [S, D], F32, kind="ExternalOutput").ap()
    cx.xT = nc.dram_tensor("xT", [KC, 128, S], F32).ap()
    cx.xT_pk = cx.xT.rearrange("k p t -> p k t")
    cx.xT_b = [Buf() for _ in range(NB)]
    cx.scrA = nc.dram_tensor("scrA", [KC, 128, S], BF16).ap()
    cx.scrB = nc.dram_tensor("scrB", [KC, 128, S], BF16).ap()
    cx.scrO = nc.dram_tensor("scrO", [KC, 128, S], BF16).ap()
    cx.scrV = nc.dram_tensor("scrV", [S, D], BF16).ap()
    cx.scrC = nc.dram_tensor("scrC", [KC, 64, S], BF16).ap()
    cx.scrD = nc.dram_tensor("scrD", [64, S], BF16).ap()
    cx.scrG = nc.dram_tensor("scrG", [48, S], F32).ap()
    cx.wup_b = nc.dram_tensor("wup_b", [D, 2 * FF], BF16).ap()
    cx.wdn_b = nc.dram_tensor("wdn_b", [FF, D], BF16).ap()
    cx.wup_buf, cx.wdn_buf = Buf(), Buf()
    cx.win_b = nc.dram_tensor("win_b", [D, 6144], BF16).ap()
    cx.wout_b = nc.dram_tensor("wout_b", [D, D], BF16).ap()
    cx.win_buf, cx.wout_buf = Buf(), Buf()
    cx.mix_precast_done = {}
    cx.precast_done = {}
    cx.consts_d = {k: nc.dram_tensor(k, list(v), F32, kind="ExternalInput").ap() for k, v in CONST_SHAPES.items() if k != "identf"}
    with ExitStack() as top:
        cx.s = Sched(nc, top)
        s = cx.s
        cx.ident_f = sb(cx, top, [128, 128], F32, "identf")
        cx.ones_f = sb(cx, top, [128, 128], F32, "onesf")
        cx.cact = sb(cx, top, [128, KC], F32, "cact")
        cx.modc = sb(cx, top, [128, 128], F32, "modc")
        s.dma("sp", cx.ident_f[:], cx.identf_d[:, :], writes=[cx.ident_f])
        s.op("dve", lambda e: e.memset(cx.ones_f[:], 1.0), writes=[cx.ones_f])
        cx.ones_b = sb(cx, top, [128, 128], BF16, "onesb")
        s.op("dve", lambda e: e.memset(cx.ones_b[:], 1.0), writes=[cx.ones_b])
        cx.mask_lt, cx.mask_lt_b, cx.mask_le_b = [], [], []
        cx.tri_gt = sb(cx, top, [128, 128], BF16, "trigt")
        cx.tri_le = sb(cx, top, [128, 128], BF16, "trile")
        for m_ in range(4):
            cx.mask_lt.append(sb(cx, top, [128, TB], F32, "mlt"))
            cx.mask_lt_b.append(sb(cx, top, [128, TB], BF16, "mltb"))
            cx.mask_le_b.append(sb(cx, top, [128, TB], BF16, "mleb"))
        with ExitStack() as ph:
            tmpf = sb(cx, ph, [128, 128], F32, "tmpf")
            s.dma("sp", tmpf[:], cx.consts_d["tri_gt"][:, :], writes=[tmpf])
            s.op("dve", lambda e: e.tensor_copy(out=cx.tri_gt[:], in_=tmpf[:]), reads=[tmpf], writes=[cx.tri_gt])
            s.op("dve", lambda e: e.tensor_tensor(out=cx.tri_le[:], in0=cx.ones_b[:], in1=cx.tri_gt[:], op=ALU.subtract),
                 reads=[cx.ones_b, cx.tri_gt], writes=[cx.tri_le])
            for m_ in range(4):
                f_, b_, b2 = cx.mask_lt[m_], cx.mask_lt_b[m_], cx.mask_le_b[m_]
                s.dma("sp", f_[:], cx.consts_d["mask_lt"][m_], writes=[f_])
                s.op("dve", lambda e: e.tensor_copy(out=b_[:], in_=f_[:]), reads=[f_], writes=[b_])
                tl = sb(cx, ph, [128, TB], F32, "mle")
                s.dma("sp", tl[:], cx.consts_d["mask_le"][m_], writes=[tl])
                s.op("dve", lambda e: e.tensor_copy(out=b2[:], in_=tl[:]), reads=[tl], writes=[b2])
            s.barrier()
        with ExitStack() as ph:
            craw = sb(cx, ph, [128, KC], F32, "craw")
            load_cols(cx, ph, craw, lambda: craw[:], cx.c16[:, :], KC)
            s.op("act", lambda e: e.activation(out=cx.cact[:], in_=craw[:], func=AF.Silu), reads=[craw], writes=[cx.cact])
            s.barrier()
        cx.has_ffn = {st[1]: True for st in stages if isinstance(st, tuple) and st[0] == "ffn"}
        mixer_layers = [st[2] for st in stages if isinstance(st, tuple) and len(st) == 3]
        cx.next_mixer = {}
        for a_, b_ in zip(mixer_layers[:-1], mixer_layers[1:]):
            cx.next_mixer[a_] = b_
        for st in stages:
            if isinstance(st, tuple) and len(st) == 3:
                cx.cur_layer = st[2]
                mixer_precast(cx, st[2])
            if st == "pro":
                prologue(cx)
            elif st == "epi":
                epilogue(cx, True)
            elif st == "epi_raw":
                epilogue(cx, False)
            elif st[0] == "ada":
                ada_phase(cx, st[1])
            elif st[0] == "ffn":
                ffn_phase(cx, st[1])
            elif st[0] == "nsa":
                nsa_mixer(cx, st[1])
            elif st[0] == "sb":
                sb_mixer(cx, st[1])
            elif st[0] == "mla":
                mla_mixer(cx, st[1])
            elif st[0] == "diff":
                diff_mixer(cx, st[1], st[2])
            else:
                raise ValueError(st)
        s.barrier()
    return nc


CONST_SHAPES = {"identf": (128, 128), "tri_gt": (128, 128), "mask_lt": (4, 128, TB), "mask_le": (4, 128, TB),
                "rope_cos": (64, S), "rope_sin": (64, S), "t5far": (128, 16), "t5tab": (16, 5, 128, TB),
                "t5cmp": (16, 480, TB), "ovl": (2, 128, 64), "eexp": (64, S),
                "wmask": (4, 128, TB), "scA": (32, 128, 64), "scB": (32, 128, 64)}


def t5_bucket_np(dist):
    n = np.maximum(dist, 0)
    nf = np.maximum(n, 1).astype(np.float32)
    large = 16 + (np.log(nf / np.float32(16)) / np.float32(math.log(128 / 16)) * np.float32(16)).astype(np.int32)
    return np.where(n < 16, n, np.minimum(large, 31))


def make_consts(t5_bias):
    j = np.arange(128)[:, None]
    i = np.arange(TB)[None, :]
    c = {"identf": np.eye(128, dtype=np.float32)}
    c["tri_gt"] = (np.arange(128)[:, None] > np.arange(128)[None, :]).astype(np.float32)
    c["mask_lt"] = np.stack([((128 * m + j) < i) for m in range(4)]).astype(np.float32)
    c["mask_le"] = np.stack([((128 * m + j) <= i) for m in range(4)]).astype(np.float32)
    half = 32
    inv = np.power(np.float32(10000.0), -np.arange(half, dtype=np.float32) / np.float32(half)).astype(np.float32)
    ang = (np.arange(S, dtype=np.float32)[None, :] * inv[:, None]).astype(np.float32)
    cos, sin = np.cos(ang).astype(np.float32), np.sin(ang).astype(np.float32)
    c["rope_cos"] = np.concatenate([cos, cos], 0)
    c["rope_sin"] = np.concatenate([-sin, sin], 0)
    c["t5far"] = np.ascontiguousarray(np.broadcast_to(t5_bias[31][None, :], (128, 16))).astype(np.float32)
    tabs = np.empty((16, 5, 128, TB), np.float32)
    for mi in range(5):
        dist = i - j - 128 * (mi - 1)
        g = t5_bias[t5_bucket_np(dist)]
        g = np.where((dist >= 0)[:, :, None], g, np.float32(-1e30))
        tabs[:, mi] = np.moveaxis(g, -1, 0)
    c["t5tab"] = tabs
    up = (np.arange(480) - 222)[:, None]
    dist = i - 16 * up + 1
    g = t5_bias[t5_bucket_np(dist)]
    g = np.where((dist >= 0)[:, :, None], g, np.float32(-1e30))
    c["t5cmp"] = np.ascontiguousarray(np.moveaxis(g, -1, 0)).astype(np.float32)
    cidx = np.arange(256)[:, None]
    nidx = np.arange(64)[None, :]
    ov = ((16 * cidx < 64 * nidx + 64) & (16 * cidx + 32 > 64 * nidx) & (cidx < 255)).astype(np.float32)
    c["ovl"] = ov.reshape(2, 128, 64)
    c["eexp"] = (np.arange(64)[:, None] == (np.arange(S)[None, :] // 64)).astype(np.float32)
    c["wmask"] = np.stack([((i - j - 128 * off) < 512) for off in (-4, -3, -2, -1)]).astype(np.float32)
    qpos = np.arange(S)[:, None]
    causal = (64 * nidx) <= qpos
    cur = qpos // 64
    forced = (nidx == 0) | (nidx == cur) | (nidx == cur - 1)
    c["scA"] = (causal & ~forced).astype(np.float32).reshape(32, 128, 64)
    c["scB"] = np.where(causal, np.where(forced, np.float32(1e9), np.float32(0)), np.float32(-1)).astype(np.float32).reshape(32, 128, 64)
    return c


MIXERS = ["sb", "nsa", "diff", "mla"]
_WNAMES = None


def full_stages(skip=()):
    st = ["pro"]
    for i in range(DEPTH):
        st.append(("ada", i))
        name = MIXERS[i % 4]
        if name not in skip:
            st.append((name, i // 4, i))
        st.append(("ffn", i))
    st.append("epi")
    return st


def kernel(**inputs):
    inputs = {k: np.ascontiguousarray(np.asarray(v, dtype=np.float32)) for k, v in inputs.items()}
    nc = build(full_stages(), nl=DEPTH)
    consts = make_consts(inputs["t5_bias"])
    B = inputs["x"].shape[0]
    in_maps = []
    for b in range(B):
        m = {"x": inputs["x"][b], "c16": inputs["c"][b].reshape(KC, 128)}
        m.update(consts)
        for k in weight_shapes(DEPTH):
            m[k] = inputs[k]
        in_maps.append(m)
    res = run_bass_kernel_spmd(nc, in_maps, core_ids=list(range(B)))
    return np.stack([r["out"] for r in res.results], axis=0).astype(np.float32)
```

```python
import math
from contextlib import ExitStack

import numpy as np
import ml_dtypes

import concourse.bass as bass
import concourse.mybir as mybir
from concourse.bass_utils import run_bass_kernel_spmd

F32 = mybir.dt.float32
BF16 = mybir.dt.bfloat16
AF = mybir.ActivationFunctionType
ALU = mybir.AluOpType
AX = mybir.AxisListType

S = 4096
D = 2048
KC = 16
TB = 512
NB = S // TB
FF = 5632
HC = FF // 128
EPS = 1e-6
DEPTH = 4
NSLOT = 20
EPOCH = 50000


class Buf:
    __slots__ = ("w", "r")

    def __init__(self):
        self.w = None
        self.r = {}


class T:
    def __init__(self, t):
        self.t = t
        self.b = Buf()

    def __getitem__(self, idx):
        return self.t[idx]


def _b(x):
    return x.b if isinstance(x, T) else x


class Sched:
    def __init__(self, nc, stack):
        self.nc = nc
        self.stack = stack
        self.eng = {"pe": nc.tensor, "act": nc.scalar, "dve": nc.vector, "pool": nc.gpsimd, "sp": nc.sync}
        self.cnt = {k: 0 for k in ("pe", "act", "dve", "pool")}
        self.sems = {k: [] for k in self.cnt}
        self.waited = {k: {} for k in self.eng}
        self.dq = {}
        for q in ("sp", "pool"):
            self.dq[q] = {
                "sems": [stack.enter_context(nc.semaphore(f"dq_{q}_{i}")) for i in range(NSLOT)],
                "uses": [0] * NSLOT,
                "n": 0,
            }
        self.nwaits = 0

    def _sem(self, e, epoch):
        while len(self.sems[e]) <= epoch:
            self.sems[e].append(self.stack.enter_context(self.nc.semaphore(f"s_{e}_{len(self.sems[e])}")))
        return self.sems[e][epoch]

    def _wait(self, waiter, ev):
        if ev[0] == "e":
            key = ("e", ev[1])
            val = ev[2]
        else:
            key = ("d", ev[1], ev[2])
            val = ev[3]
        w = self.waited[waiter]
        if w.get(key, 0) >= val:
            return
        w[key] = val
        self.nwaits += 1
        if ev[0] == "e":
            epoch = (val - 1) // EPOCH
            self.eng[waiter].wait_ge(self._sem(ev[1], epoch), (val - 1) % EPOCH + 1)
        else:
            self.eng[waiter].wait_ge(self.dq[ev[1]]["sems"][ev[2]], 16 * val)

    def _deps(self, waiter, reads, writes, is_dma):
        for x in reads:
            b = _b(x)
            if b.w is not None:
                ev = b.w
                same = ev[0] == "e" and ev[1] == waiter and not is_dma
                if same and waiter == "pe":
                    continue
                self._wait(waiter, ev)
        for x in writes:
            b = _b(x)
            if b.w is not None:
                ev = b.w
                same = ev[0] == "e" and ev[1] == waiter and not is_dma
                if not same:
                    self._wait(waiter, ev)
            for ev in b.r.values():
                same = ev[0] == "e" and ev[1] == waiter and not is_dma
                if not same:
                    self._wait(waiter, ev)

    def op(self, e, fn, reads=(), writes=(), inc=True):
        self._deps(e, reads, writes, False)
        ins = fn(self.eng[e])
        if inc:
            self.cnt[e] += 1
            seq = self.cnt[e]
            ins.then_inc(self._sem(e, (seq - 1) // EPOCH), 1)
        else:
            seq = self.cnt[e] + 1
        ev = ("e", e, seq)
        for x in reads:
            _b(x).r[("e", e)] = ev
        for x in writes:
            b = _b(x)
            b.w = ev
            b.r = {}
        return ins

    def dma(self, q, out, in_, reads=(), writes=(), **kw):
        dq = self.dq[q]
        slot = dq["n"] % NSLOT
        dq["n"] += 1
        if dq["uses"][slot] > 0:
            self._wait(q, ("d", q, slot, dq["uses"][slot]))
        self._deps(q, reads, writes, True)
        ins = self.eng[q].dma_start(out=out, in_=in_, **kw)
        ins.then_inc(dq["sems"][slot], 16)
        dq["uses"][slot] += 1
        ev = ("d", q, slot, dq["uses"][slot])
        for x in reads:
            _b(x).r[("d", q, slot)] = ev
        for x in writes:
            b = _b(x)
            b.w = ev
            b.r = {}
        return ins

    def barrier(self):
        evs = []
        for e in self.cnt:
            if self.cnt[e] > 0:
                evs.append(("e", e, self.cnt[e]))
        for q, dq in self.dq.items():
            for s in range(NSLOT):
                if dq["uses"][s] > 0:
                    evs.append(("d", q, s, dq["uses"][s]))
        for waiter in self.eng:
            for ev in evs:
                if ev[0] == "e" and ev[1] == waiter:
                    continue
                self._wait(waiter, ev)


class Ctx:
    pass


_uid = [0]


def sb(cx, ph, shape, dtype, name="t"):
    _uid[0] += 1
    return T(ph.enter_context(cx.nc.sbuf_tensor(f"{name}_{_uid[0]}", list(shape), dtype)))


def ps(cx, ph, shape, dtype=F32, name="p"):
    _uid[0] += 1
    return T(ph.enter_context(cx.nc.psum_tensor(f"{name}_{_uid[0]}", list(shape), dtype)))


def mm(cx, out_t, out_ap, lhsT_t, lhsT_ap, rhs_t, rhs_ap, start, stop, inc=None):
    if inc is None:
        inc = stop
    cx.s.op("pe", lambda e: e.matmul(out_ap, lhsT_ap, rhs_ap, start=start, stop=stop),
            reads=[lhsT_t, rhs_t], writes=[out_t], inc=inc)


def load_cols(cx, ph, dst, dst_ap_fn, src_rows_ap, nrows):
    s = cx.s
    with ExitStack() as lph:
        st = sb(cx, lph, [nrows, 128], F32, "lc_st")
        s.dma("sp", st[:], src_rows_ap, writes=[st])
        pt = ps(cx, lph, [128, nrows], F32, "lc_ps")
        s.op("pe", lambda e: e.transpose(pt[:], st[:], cx.ident_f[0:nrows, 0:nrows]), reads=[st, cx.ident_f], writes=[pt])
        s.op("dve", lambda e: e.tensor_copy(out=dst_ap_fn(), in_=pt[:]), reads=[pt], writes=[dst])
        s.barrier()


def prologue(cx):
    s = cx.s
    with ExitStack() as ph:
        xin = [sb(cx, ph, [128, D], F32, "xin") for _ in range(2)]
        xst = [sb(cx, ph, [128, KC, 128], F32, "xst") for _ in range(2)]
        pts = [ps(cx, ph, [128, 4, 128], F32, "ptp") for _ in range(4)]
        k = 0
        for tt in range(S // 128):
            xi = xin[tt % 2]
            xs = xst[tt % 2]
            s.dma("sp", xi[:], cx.x[tt * 128:(tt + 1) * 128, :], writes=[xi])
            for g in range(4):
                pt = pts[k % 4]
                k += 1
                for j in range(4):
                    kc = g * 4 + j
                    s.op("pe", lambda e: e.transpose(pt[:, j, :], xi[:, kc * 128:(kc + 1) * 128], cx.ident_f[:]),
                         reads=[xi, cx.ident_f], writes=[pt], inc=(j == 3))
                if g % 2 == 0:
                    s.op("act", lambda e: e.copy(out=xs[:, g * 4:(g + 1) * 4, :], in_=pt[:]), reads=[pt], writes=[xs])
                else:
                    s.op("dve", lambda e: e.tensor_copy(out=xs[:, g * 4:(g + 1) * 4, :], in_=pt[:]), reads=[pt], writes=[xs])
            s.dma("sp", cx.xT_pk[:, :, tt * 128:(tt + 1) * 128], xs[:], reads=[xs], writes=[cx.xT_b[tt // 4]])
        s.barrier()


def epilogue(cx, final_norm):
    s = cx.s
    with ExitStack() as ph:
        gcol = sb(cx, ph, [128, KC], F32, "fg")
        if final_norm:
            load_cols(cx, ph, gcol, lambda: gcol[:], cx.final_g.rearrange("(r p) -> r p", p=128), KC)
        xb = [sb(cx, ph, [128, KC, TB], F32, "exb") for _ in range(2)]
        sq = [sb(cx, ph, [128, TB], F32, "esq") for _ in range(2)]
        rs = sb(cx, ph, [128, TB], F32, "ers")
        ot = [sb(cx, ph, [128, D], F32, "eot") for _ in range(2)]
        pss = ps(cx, ph, [128, TB], F32, "epss")
        pts = [ps(cx, ph, [128, 4, 128], F32, "eptp") for _ in range(4)]
        k = 0
        for tb in range(NB):
            x_ = xb[tb % 2]
            s.dma("sp", x_[:], cx.xT_pk[:, :, tb * TB:(tb + 1) * TB], reads=[cx.xT_b[tb]], writes=[x_])
            if final_norm:
                rstd_block(cx, x_, sq, pss, rs, D)
                for kc in range(KC):
                    s.op("pool", lambda e: e.tensor_tensor(out=x_[:, kc, :], in0=x_[:, kc, :], in1=rs[:], op=ALU.mult),
                         reads=[x_, rs], writes=[x_])
                    s.op("dve", lambda e: e.tensor_scalar(out=x_[:, kc, :], in0=x_[:, kc, :], scalar1=gcol[:, kc:kc + 1],
                                                          scalar2=None, op0=ALU.mult), reads=[x_, gcol], writes=[x_])
            for t4 in range(TB // 128):
                tt = tb * (TB // 128) + t4
                o_ = ot[tt % 2]
                for g in range(4):
                    pt = pts[k % 4]
                    k += 1
                    for j in range(4):
                        kc = g * 4 + j
                        s.op("pe", lambda e: e.transpose(pt[:, j, :], x_[:, kc, t4 * 128:(t4 + 1) * 128], cx.ident_f[:]),
                             reads=[x_, cx.ident_f], writes=[pt], inc=(j == 3))
                    dst = o_[:, g * 512:(g + 1) * 512]
                    src = pt[:].rearrange("p a b -> p (a b)")
                    if g % 2 == 0:
                        s.op("act", lambda e: e.copy(out=dst, in_=src), reads=[pt], writes=[o_])
                    else:
                        s.op("dve", lambda e: e.tensor_copy(out=dst, in_=src), reads=[pt], writes=[o_])
                s.dma("sp", cx.out[tt * 128:(tt + 1) * 128, :], o_[:], reads=[o_])
        s.barrier()


def rstd_block(cx, x_, sq, pss, rs, nfeat, nch=None, eps=EPS):
    s = cx.s
    if nch is None:
        nch = nfeat // 128
    for kc in range(nch):
        q = sq[kc % 2]
        s.op("act", lambda e: e.activation(out=q[:], in_=x_[:, kc, :], func=AF.Square), reads=[x_], writes=[q])
        mm(cx, pss, pss[:], cx.ones_f, cx.ones_f[:], q, q[:], start=(kc == 0), stop=(kc == nch - 1), inc=True)
    s.op("dve", lambda e: e.tensor_scalar(out=rs[:], in0=pss[:], scalar1=1.0 / nfeat, scalar2=eps, op0=ALU.mult, op1=ALU.add),
         reads=[pss], writes=[rs])
    s.op("act", lambda e: e.activation(out=rs[:], in_=rs[:], func=AF.Sqrt), reads=[rs], writes=[rs])
    s.op("dve", lambda e: e.reciprocal(out=rs[:], in_=rs[:]), reads=[rs], writes=[rs])


def ada_phase(cx, i):
    s = cx.s
    with ExitStack() as ph:
        wt = [sb(cx, ph, [128, KC, 512], F32, "adaw") for _ in range(2)]
        pm = ps(cx, ph, [128, 96], F32, "adap")
        bcol = sb(cx, ph, [128, 96], F32, "adab")
        load_cols(cx, ph, bcol, lambda: bcol[:], cx.ada_b[i].rearrange("(r p) -> r p", p=128), 96)
        gcol = sb(cx, ph, [128, 2 * KC], F32, "ng")
        load_cols(cx, ph, gcol, lambda: gcol[:], cx.norm_g[i].rearrange("a (r p) -> (a r) p", p=128), 2 * KC)
        wsrc = cx.ada_w[i].rearrange("(kc p) f -> p kc f", p=128)
        nblk = 6 * D // 512
        for blk in range(nblk):
            w = wt[blk % 2]
            s.dma("sp", w[:], wsrc[:, :, blk * 512:(blk + 1) * 512], writes=[w])
            for j in range(4):
                fc = blk * 4 + j
                for kc in range(KC):
                    mm(cx, pm, pm[:, fc:fc + 1], w, w[:, kc, j * 128:(j + 1) * 128], cx.cact, cx.cact[:, kc:kc + 1],
                       start=(kc == 0), stop=(kc == KC - 1))
        m = cx.modc
        s.op("dve", lambda e: e.tensor_tensor(out=m[:, 0:96], in0=pm[:], in1=bcol[:], op=ALU.add), reads=[pm, bcol], writes=[m])
        for a, sc0 in ((0, 16), (1, 64)):
            dst = m[:, 96 + a * 16:112 + a * 16]
            s.op("dve", lambda e: e.tensor_scalar(out=dst, in0=m[:, sc0:sc0 + 16], scalar1=1.0, scalar2=None, op0=ALU.add),
                 reads=[m], writes=[m])
            s.op("dve", lambda e: e.tensor_tensor(out=dst, in0=dst, in1=gcol[:, a * 16:(a + 1) * 16], op=ALU.mult),
                 reads=[m, gcol], writes=[m])
        s.barrier()


def SHIFT(a):
    return 0 if a == 0 else 48


def GATE(a):
    return 32 if a == 0 else 80


def GM(a):
    return 96 if a == 0 else 112


def norm_block(cx, a, x_, h_, sq, pss, rs, tmp):
    s = cx.s
    rstd_block(cx, x_, sq, pss, rs, D)
    m = cx.modc
    for kc in range(KC):
        t_ = tmp[kc % 2]
        s.op("pool", lambda e: e.tensor_tensor(out=t_[:], in0=x_[:, kc, :], in1=rs[:], op=ALU.mult), reads=[x_, rs], writes=[t_])
        s.op("dve", lambda e: e.tensor_scalar(out=h_[:, kc, :], in0=t_[:], scalar1=m[:, GM(a) + kc:GM(a) + kc + 1],
                                              scalar2=m[:, SHIFT(a) + kc:SHIFT(a) + kc + 1], op0=ALU.mult, op1=ALU.add),
             reads=[t_, m], writes=[h_])


def ffn_precast(cx, i):
    s = cx.s
    for r in range(8):
        s.dma("pool", cx.wup_b[r * 256:(r + 1) * 256, :], cx.ffn_w_up[i][r * 256:(r + 1) * 256, :], writes=[cx.wup_buf])
    for r in range(4):
        s.dma("pool", cx.wdn_b[r * 1408:(r + 1) * 1408, :], cx.ffn_w_down[i][r * 1408:(r + 1) * 1408, :], writes=[cx.wdn_buf])


MIX_W = {"sb": ("sb_w_in", 6144, "sb_w_out"), "nsa": ("nsa_w_in", 5168, "nsa_w_out"), "diff": ("diff_w_in", 6144, "diff_w_out"),
         "mla": (None, 0, "mla_w_out")}


def mixer_precast(cx, layer):
    if cx.mix_precast_done.get(layer):
        return
    cx.mix_precast_done[layer] = True
    s = cx.s
    win, F, wout = MIX_W[MIXERS[layer % 4]]
    j = layer // 4
    if win is not None:
        src = getattr(cx, win)[j]
        for r in range(8):
            s.dma("pool", cx.win_b[r * 256:(r + 1) * 256, 0:F], src[r * 256:(r + 1) * 256, :], writes=[cx.win_buf])
    src = getattr(cx, wout)[j]
    for r in range(2):
        s.dma("pool", cx.wout_b[r * 1024:(r + 1) * 1024, :], src[r * 1024:(r + 1) * 1024, :], writes=[cx.wout_buf])


def ffn_phase(cx, i):
    s = cx.s
    if i in cx.next_mixer:
        mixer_precast(cx, cx.next_mixer[i])
    if not cx.precast_done.get(i):
        ffn_precast(cx, i)
        cx.precast_done[i] = True
    with ExitStack() as ph:
        cw = sb(cx, ph, [128, 3 * 88], F32, "cw")
        cb = sb(cx, ph, [128, 88], F32, "cb")
        cwsrc = cx.ffn_conv_w[i].rearrange("w (r p) -> (w r) p", p=128)
        for part in range(3):
            load_cols(cx, ph, cw, lambda: cw[:, part * 88:(part + 1) * 88], cwsrc[part * 88:(part + 1) * 88, :], 88)
        load_cols(cx, ph, cb, lambda: cb[:], cx.ffn_conv_b[i].rearrange("(r p) -> r p", p=128), 88)
        carry = sb(cx, ph, [128, 88, 2], F32, "carry")
        s.op("dve", lambda e: e.memset(carry[:], 0.0), writes=[carry])

        xc = [sb(cx, ph, [128, TB], F32, "fxc") for _ in range(3)]
        h_ = sb(cx, ph, [128, KC, TB], BF16, "fh")
        aT = sb(cx, ph, [128, HC, TB], BF16, "faT")
        sq = [sb(cx, ph, [128, TB], F32, "fsq") for _ in range(2)]
        tmp = [sb(cx, ph, [128, TB], F32, "ftmp") for _ in range(2)]
        rs = sb(cx, ph, [128, TB], F32, "frs")
        wup = [sb(cx, ph, [128, KC, 2, 256], BF16, "wup") for _ in range(2)]
        wdn = [sb(cx, ph, [128, 22, 256], BF16, "wdn") for _ in range(2)]
        ug = [sb(cx, ph, [128, TB + 2], F32, "ug") for _ in range(2)]
        uu = [sb(cx, ph, [128, TB + 2], F32, "uu") for _ in range(2)]
        cg = [sb(cx, ph, [128, TB], F32, "cg") for _ in range(2)]
        cu = [sb(cx, ph, [128, TB], F32, "cu") for _ in range(2)]
        sg = [sb(cx, ph, [128, TB], F32, "sg") for _ in range(2)]
        xo = [sb(cx, ph, [128, TB], F32, "xo") for _ in range(2)]
        pss = ps(cx, ph, [128, TB], F32, "fpss")
        pg = [ps(cx, ph, [128, TB], F32, "fpg") for _ in range(2)]
        pu = [ps(cx, ph, [128, TB], F32, "fpu") for _ in range(2)]
        pd = [ps(cx, ph, [128, TB], F32, "fpd") for _ in range(2)]
        wsrc = cx.wup_b.rearrange("(kc p) f -> p kc f", p=128)
        dsrc = cx.wdn_b.rearrange("(hc p) f -> p hc f", p=128)
        m = cx.modc
        NJ2 = HC // 2

        def load_wup(j2, w):
            s.dma("sp", w[:, :, 0, :], wsrc[:, :, j2 * 256:(j2 + 1) * 256], reads=[cx.wup_buf], writes=[w])
            s.dma("sp", w[:, :, 1, :], wsrc[:, :, FF + j2 * 256:FF + (j2 + 1) * 256], reads=[cx.wup_buf], writes=[w])

        def load_wdn(idx, w):
            d2, half = idx // 2, idx % 2
            s.dma("sp", w[:], dsrc[:, half * 22:(half + 1) * 22, d2 * 256:(d2 + 1) * 256], reads=[cx.wdn_buf], writes=[w])

        nx = 0
        for tb in range(NB):
            t0 = tb * TB
            load_wup(0, wup[0])
            for kc in range(KC):
                x_ = xc[nx % 3]
                nx += 1
                s.dma("sp", x_[:], cx.xT[kc, :, t0:t0 + TB], reads=[cx.xT_b[tb]], writes=[x_])
                q = sq[kc % 2]
                s.op("act", lambda e: e.activation(out=q[:], in_=x_[:], func=AF.Square), reads=[x_], writes=[q])
                mm(cx, pss, pss[:], cx.ones_f, cx.ones_f[:], q, q[:], start=(kc == 0), stop=(kc == KC - 1), inc=True)
            s.op("dve", lambda e: e.tensor_scalar(out=rs[:], in0=pss[:], scalar1=1.0 / D, scalar2=EPS, op0=ALU.mult, op1=ALU.add),
                 reads=[pss], writes=[rs])
            s.op("act", lambda e: e.activation(out=rs[:], in_=rs[:], func=AF.Sqrt), reads=[rs], writes=[rs])
            s.op("dve", lambda e: e.reciprocal(out=rs[:], in_=rs[:]), reads=[rs], writes=[rs])
            for kc in range(KC):
                x_ = xc[nx % 3]
                nx += 1
                s.dma("sp", x_[:], cx.xT[kc, :, t0:t0 + TB], reads=[cx.xT_b[tb]], writes=[x_])
                t_ = tmp[kc % 2]
                s.op("pool", lambda e: e.tensor_tensor(out=t_[:], in0=x_[:], in1=rs[:], op=ALU.mult), reads=[x_, rs], writes=[t_])
                s.op("dve", lambda e: e.tensor_scalar(out=h_[:, kc, :], in0=t_[:], scalar1=m[:, GM(1) + kc:GM(1) + kc + 1],
                                                      scalar2=m[:, SHIFT(1) + kc:SHIFT(1) + kc + 1], op0=ALU.mult, op1=ALU.add),
                     reads=[t_, m], writes=[h_])
            for j2 in range(NJ2):
                w = wup[j2 % 2]
                if j2 + 1 < NJ2:
                    load_wup(j2 + 1, wup[(j2 + 1) % 2])
                else:
                    load_wdn(0, wdn[0])
                for jj in range(2):
                    j = j2 * 2 + jj
                    g_ps, u_ps = pg[j % 2], pu[j % 2]
                    for kc in range(KC):
                        mm(cx, g_ps, g_ps[:], w, w[:, kc, 0, jj * 128:(jj + 1) * 128], h_, h_[:, kc, :], start=(kc == 0), stop=(kc == KC - 1))
                    for kc in range(KC):
                        mm(cx, u_ps, u_ps[:], w, w[:, kc, 1, jj * 128:(jj + 1) * 128], h_, h_[:, kc, :], start=(kc == 0), stop=(kc == KC - 1))
                    for (p_, u_, c_, col) in ((g_ps, ug[j % 2], cg[j % 2], j), (u_ps, uu[j % 2], cu[j % 2], HC + j)):
                        s.op("act", lambda e: e.copy(out=u_[:, 2:TB + 2], in_=p_[:]), reads=[p_], writes=[u_])
                        s.op("pool", lambda e: e.tensor_copy(out=u_[:, 0:2], in_=carry[:, col, :]), reads=[carry], writes=[u_])
                        s.op("pool", lambda e: e.tensor_copy(out=carry[:, col, :], in_=u_[:, TB:TB + 2]), reads=[u_], writes=[carry])
                        s.op("dve", lambda e: e.tensor_scalar(out=c_[:], in0=u_[:, 2:TB + 2], scalar1=cw[:, 2 * 88 + col:2 * 88 + col + 1],
                                                              scalar2=cb[:, col:col + 1], op0=ALU.mult, op1=ALU.add),
                             reads=[u_, cw, cb], writes=[c_])
                        s.op("dve", lambda e: e.scalar_tensor_tensor(out=c_[:], in0=u_[:, 1:TB + 1], scalar=cw[:, 88 + col:88 + col + 1],
                                                                     in1=c_[:], op0=ALU.mult, op1=ALU.add),
                             reads=[u_, cw, c_], writes=[c_])
                        s.op("dve", lambda e: e.scalar_tensor_tensor(out=c_[:], in0=u_[:, 0:TB], scalar=cw[:, col:col + 1],
                                                                     in1=c_[:], op0=ALU.mult, op1=ALU.add),
                             reads=[u_, cw, c_], writes=[c_])
                    s_ = sg[j % 2]
                    c_g, c_u = cg[j % 2], cu[j % 2]
                    s.op("act", lambda e: e.activation(out=s_[:], in_=c_g[:], func=AF.Silu), reads=[c_g], writes=[s_])
                    s.op("dve", lambda e: e.tensor_tensor(out=aT[:, j, :], in0=s_[:], in1=c_u[:], op=ALU.mult), reads=[s_, c_u], writes=[aT])
            nw = 0
            for d2 in range(KC // 2):
                for half in range(2):
                    w = wdn[nw % 2]
                    if nw + 1 < KC:
                        load_wdn(nw + 1, wdn[(nw + 1) % 2])
                    for dd in range(2):
                        p_ = pd[dd]
                        for hh in range(22):
                            hc = half * 22 + hh
                            mm(cx, p_, p_[:], w, w[:, hh, dd * 128:(dd + 1) * 128], aT, aT[:, hc, :], start=(hc == 0), stop=(hc == HC - 1),
                               inc=(hh == 21))
                    nw += 1
                for dd in range(2):
                    dc = d2 * 2 + dd
                    p_ = pd[dd]
                    x_ = xc[nx % 3]
                    nx += 1
                    s.dma("sp", x_[:], cx.xT[dc, :, t0:t0 + TB], reads=[cx.xT_b[tb]], writes=[x_])
                    o_ = xo[dc % 2]
                    s.op("dve", lambda e: e.scalar_tensor_tensor(out=o_[:], in0=p_[:], scalar=m[:, GATE(1) + dc:GATE(1) + dc + 1],
                                                                 in1=x_[:], op0=ALU.mult, op1=ALU.add),
                         reads=[p_, m, x_], writes=[o_])
                    s.dma("sp", cx.xT[dc, :, t0:t0 + TB], o_[:], reads=[o_], writes=[cx.xT_b[tb]])
        s.barrier()


def pk(w2d):
    return w2d.rearrange("(kc p) f -> p kc f", p=128)


def inproj_phase(cx, groups, extra=None):
    s = cx.s
    with ExitStack() as ph:
        x_ = sb(cx, ph, [128, KC, TB], F32, "ix")
        h_ = sb(cx, ph, [128, KC, TB], BF16, "ih")
        sq = [sb(cx, ph, [128, TB], F32, "isq") for _ in range(2)]
        tmp = [sb(cx, ph, [128, TB], F32, "itmp") for _ in range(2)]
        rs = sb(cx, ph, [128, TB], F32, "irs")
        wt = [sb(cx, ph, [128, KC, 512], BF16, "iw") for _ in range(2)]
        stg = [sb(cx, ph, [128, 4, 512], BF16, "istg") for _ in range(2)]
        gst = sb(cx, ph, [128, TB], F32, "igst")
        pss = ps(cx, ph, [128, TB], F32, "ipss")
        pp = [ps(cx, ph, [128, 512], F32, "ipp") for _ in range(4)]
        ng = len(groups)
        seq = [(tb, gi) for tb in range(NB) for gi in range(ng)]

        def wl(idx):
            tb, gi = seq[idx]
            kind, wsrc, c0, ncols, _ = groups[gi]
            w = wt[idx % 2]
            s.dma("sp", w[:, :, 0:ncols], wsrc[:, :, c0:c0 + ncols], reads=[cx.win_buf], writes=[w])

        k = 0
        wl(0)
        for idx, (tb, gi) in enumerate(seq):
            t0 = tb * TB
            if gi == 0:
                s.dma("sp", x_[:], cx.xT_pk[:, :, t0:t0 + TB], reads=[cx.xT_b[tb]], writes=[x_])
                norm_block(cx, 0, x_, h_, sq, pss, rs, tmp)
            if idx + 1 < len(seq):
                wl(idx + 1)
            kind, wsrc, c0, ncols, store_fn = groups[gi]
            w = wt[idx % 2]
            st = stg[idx % 2]
            if kind == "gate":
                p_ = pp[k % 4]
                k += 1
                for kc in range(KC):
                    mm(cx, p_, p_[0:ncols, :], w, w[:, kc, 0:ncols], h_, h_[:, kc, :], start=(kc == 0), stop=(kc == KC - 1))
                s.op("act", lambda e: e.activation(out=gst[0:ncols, :], in_=p_[0:ncols, :], func=AF.Sigmoid), reads=[p_], writes=[gst])
                store_fn(tb, gst)
                continue
            if kind == "fm":
                nch = (ncols + 127) // 128
                for j in range(nch):
                    cw_ = min(128, ncols - j * 128)
                    p_ = pp[k % 4]
                    k += 1
                    for kc in range(KC):
                        mm(cx, p_, p_[0:cw_, :], w, w[:, kc, j * 128:j * 128 + cw_], h_, h_[:, kc, :], start=(kc == 0), stop=(kc == KC - 1))
                    if j % 2 == 0:
                        s.op("act", lambda e: e.copy(out=st[0:cw_, j, :], in_=p_[0:cw_, :]), reads=[p_], writes=[st])
                    else:
                        s.op("dve", lambda e: e.tensor_copy(out=st[0:cw_, j, :], in_=p_[0:cw_, :]), reads=[p_], writes=[st])
            else:
                for t4 in range(4):
                    p_ = pp[k % 4]
                    k += 1
                    for kc in range(KC):
                        mm(cx, p_, p_[:, 0:ncols], h_, h_[:, kc, t4 * 128:(t4 + 1) * 128], w, w[:, kc, 0:ncols], start=(kc == 0), stop=(kc == KC - 1))
                    if t4 % 2 == 0:
                        s.op("act", lambda e: e.copy(out=st[:, t4, 0:ncols], in_=p_[:, 0:ncols]), reads=[p_], writes=[st])
                    else:
                        s.op("dve", lambda e: e.tensor_copy(out=st[:, t4, 0:ncols], in_=p_[:, 0:ncols]), reads=[p_], writes=[st])
            store_fn(tb, st)
        s.barrier()
    maybe_precast(cx)


def maybe_precast(cx):
    i = getattr(cx, "cur_layer", None)
    if i is not None and cx.has_ffn.get(i) and not cx.precast_done.get(i):
        ffn_precast(cx, i)
        cx.precast_done[i] = True


def outproj_phase(cx, oT_pk, w2d):
    s = cx.s
    with ExitStack() as ph:
        x_ = [sb(cx, ph, [128, KC, TB], F32, "ox") for _ in range(2)]
        o_ = [sb(cx, ph, [128, KC, TB], BF16, "oo") for _ in range(2)]
        wt = [sb(cx, ph, [128, KC, 512], BF16, "ow") for _ in range(2)]
        xo = [sb(cx, ph, [128, TB], F32, "oxo") for _ in range(2)]
        pd = [ps(cx, ph, [128, TB], F32, "opd") for _ in range(2)]
        wsrc = pk(cx.wout_b)
        m = cx.modc
        n = 0
        nw = 0
        s.dma("sp", wt[0][:], wsrc[:, :, 0:512], reads=[cx.wout_buf], writes=[wt[0]])
        for tb in range(NB):
            t0 = tb * TB
            xb, ob = x_[tb % 2], o_[tb % 2]
            s.dma("sp", xb[:], cx.xT_pk[:, :, t0:t0 + TB], reads=[cx.xT_b[tb]], writes=[xb])
            s.dma("sp", ob[:], oT_pk[:, :, t0:t0 + TB], writes=[ob])
            for d4 in range(4):
                w = wt[nw % 2]
                if not (tb == NB - 1 and d4 == 3):
                    nxt = (d4 + 1) % 4
                    s.dma("sp", wt[(nw + 1) % 2][:], wsrc[:, :, nxt * 512:(nxt + 1) * 512], reads=[cx.wout_buf], writes=[wt[(nw + 1) % 2]])
                nw += 1
                for dd in range(4):
                    dc = d4 * 4 + dd
                    p_ = pd[n % 2]
                    for hc in range(KC):
                        mm(cx, p_, p_[:], w, w[:, hc, dd * 128:(dd + 1) * 128], ob, ob[:, hc, :], start=(hc == 0), stop=(hc == KC - 1))
                    xo_ = xo[n % 2]
                    s.op("dve", lambda e: e.scalar_tensor_tensor(out=xo_[:], in0=p_[:], scalar=m[:, GATE(0) + dc:GATE(0) + dc + 1],
                                                                 in1=xb[:, dc, :], op0=ALU.mult, op1=ALU.add),
                         reads=[p_, m, xb], writes=[xo_])
                    s.dma("sp", cx.xT[dc, :, t0:t0 + TB], xo_[:], reads=[xo_], writes=[cx.xT_b[tb]])
                    n += 1
        s.barrier()


def fm_store(cx, scr_pk, chunk0):
    def f(tb, st):
        cx.s.dma("sp", scr_pk[:, chunk0:chunk0 + 4, tb * TB:(tb + 1) * TB], st[:], reads=[st])
    return f


def tm_store(cx, scr, col0, ncols=512):
    v = scr.rearrange("(t p) c -> p t c", p=128)

    def f(tb, st):
        cx.s.dma("sp", v[:, tb * 4:tb * 4 + 4, col0:col0 + ncols], st[:, :, 0:ncols], reads=[st])
    return f


def sb_mixer(cx, j):
    s = cx.s
    nc = cx.nc
    w_in = pk(cx.win_b)
    qT, kT = cx.scrA, cx.scrB
    vv = cx.scrV
    qT_pk = qT.rearrange("k p t -> p k t")
    kT_pk = kT.rearrange("k p t -> p k t")
    groups = []
    for g in range(4):
        groups.append(("fm", w_in, g * 512, 512, fm_store(cx, qT_pk, g * 4)))
    for g in range(4):
        groups.append(("fm", w_in, 2048 + g * 512, 512, fm_store(cx, kT_pk, g * 4)))
    for g in range(4):
        groups.append(("tm", w_in, 4096 + g * 512, 512, tm_store(cx, vv, g * 512)))
    inproj_phase(cx, groups)

    scale = 128 ** -0.5
    oT_pk = cx.scrO.rearrange("k p t -> p k t")
    vview = vv.rearrange("(t p) c -> p t c", p=128)
    with ExitStack() as ph:
        NH = 4
        qh = [sb(cx, ph, [128, S], BF16, "sq") for _ in range(NH)]
        kh = [sb(cx, ph, [128, S], BF16, "sk") for _ in range(NH)]
        vh = [sb(cx, ph, [128, 32, 128], BF16, "sv") for _ in range(NH)]

        class CB:
            pass
        cbs = []
        for ci in range(2):
            cb_ = CB()
            cb_.e_ = [sb(cx, ph, [128, TB], F32, "se") for _ in range(2)]
            cb_.sp_ = [sb(cx, ph, [128, TB], F32, "ssp") for _ in range(2)]
            cb_.spb = [sb(cx, ph, [128, TB], BF16, "sspb") for _ in range(3)]
            cb_.u_ = [sb(cx, ph, [128, TB], F32, "su") for _ in range(3)]
            cb_.ar = [sb(cx, ph, [128, TB], F32, "sar") for _ in range(2)]
            cb_.aT = [sb(cx, ph, [128, TB], BF16, "saT") for _ in range(2)]
            cb_.ost = [sb(cx, ph, [128, TB], BF16, "sost") for _ in range(2)]
            cbs.append(cb_)
        for cb_ in cbs:
            cb_.zps = [ps(cx, ph, [128, TB], F32, "szp") for _ in range(2)]
            cb_.R = ps(cx, ph, [128, TB], F32, "sR")
            cb_.acc = ps(cx, ph, [128, TB], F32, "sacc")

        def load_head(h):
            i = h % NH
            s.dma("sp", qh[i][:], qT[h, :, :], writes=[qh[i]])
            s.dma("sp", kh[i][:], kT[h, :, :], writes=[kh[i]])
            s.dma("sp", vh[i][:], vview[:, :, h * 128:(h + 1) * 128], writes=[vh[i]])

        def make_chain(h, c, cb_):
            q_, k_, v_ = qh[h % NH], kh[h % NH], vh[h % NH]
            kts = list(range(4 * c + 3, -1, -1))
            nk = len(kts)
            zps, R, a_ps = cb_.zps, cb_.R, cb_.acc
            e_, sp_, spb, u_, ar, aT = cb_.e_, cb_.sp_, cb_.spb, cb_.u_, cb_.ar, cb_.aT

            def S1(i):
                kt = kts[i]
                z = zps[i % 2]
                mm(cx, z, z[:], k_, k_[:, kt * 128:(kt + 1) * 128], q_, q_[:, c * TB:(c + 1) * TB], True, True)

            def S2(i):
                kt = kts[i]
                i2, i3 = i % 2, i % 3
                z = zps[i2]
                s.op("act", lambda e: e.activation(out=e_[i2][:], in_=z[:], func=AF.Exp, scale=scale), reads=[z], writes=[e_[i2]])
                s.op("act", lambda e: e.activation(out=sp_[i2][:], in_=e_[i2][:], func=AF.Ln, bias=cx.ones_f[:, 0:1]),
                     reads=[e_[i2], cx.ones_f], writes=[sp_[i2]])
                s.op("dve", lambda e: e.scalar_tensor_tensor(out=u_[i3][:], in0=z[:], scalar=scale, in1=sp_[i2][:],
                                                             op0=ALU.mult, op1=ALU.subtract),
                     reads=[z, sp_[i2]], writes=[u_[i3]])
                if kt >= 4 * c:
                    mk = cx.mask_lt[kt - 4 * c]
                    s.op("dve", lambda e: e.tensor_tensor(out=spb[i3][:], in0=sp_[i2][:], in1=mk[:], op=ALU.mult),
                         reads=[sp_[i2], mk], writes=[spb[i3]])
                else:
                    s.op("dve", lambda e: e.tensor_copy(out=spb[i3][:], in_=sp_[i2][:]), reads=[sp_[i2]], writes=[spb[i3]])

            def S3(i):
                if i > 0:
                    j3 = (i - 1) % 3
                    mm(cx, R, R[:], cx.tri_le, cx.tri_le[:], spb[j3], spb[j3][:], False, True)
                mm(cx, R, R[:], cx.tri_gt, cx.tri_gt[:], spb[i % 3], spb[i % 3][:], i == 0, True)

            def S4(i):
                kt = kts[i]
                i2, i3 = i % 2, i % 3
                s.op("dve", lambda e: e.tensor_tensor(out=ar[i2][:], in0=u_[i3][:], in1=R[:], op=ALU.subtract),
                     reads=[u_[i3], R], writes=[ar[i2]])
                s.op("act", lambda e: e.activation(out=aT[i2][:], in_=ar[i2][:], func=AF.Exp), reads=[ar[i2]], writes=[aT[i2]])
                if kt >= 4 * c:
                    mkb = cx.mask_lt_b[kt - 4 * c]
                    s.op("dve", lambda e: e.tensor_tensor(out=aT[i2][:], in0=aT[i2][:], in1=mkb[:], op=ALU.mult),
                         reads=[aT[i2], mkb], writes=[aT[i2]])

            def S5(i):
                kt = kts[i]
                i2 = i % 2
                mm(cx, a_ps, a_ps[:], v_, v_[:, kt, :], aT[i2], aT[i2][:], start=(i == 0), stop=(i == nk - 1), inc=True)

            def fin():
                o_st = cb_.ost[c % 2]
                s.op("act", lambda e: e.copy(out=o_st[:], in_=a_ps[:]), reads=[a_ps], writes=[o_st])
                s.dma("sp", cx.scrO[h, :, c * TB:(c + 1) * TB], o_st[:], reads=[o_st])
            return nk, [S1, S2, S3, S4, S5], fin

        load_head(0)
        load_head(1)
        for hp in range(8):
            if hp + 1 < 8:
                load_head(2 * hp + 2)
                load_head(2 * hp + 3)
            for c in range(NB):
                chains = [make_chain(2 * hp + ci, c, cbs[ci]) for ci in range(2)]
                nk = chains[0][0]
                for t in range(nk + 4):
                    for si in range(4, -1, -1):
                        i = t - si
                        if 0 <= i < nk:
                            for ch in chains:
                                ch[1][si](i)
                for ch in chains:
                    ch[2]()
        s.barrier()
    outproj_phase(cx, oT_pk, cx.sb_w_out[j])


def recip_act(cx, out_t, in_t, tmp_t):
    cx.s.op("act", lambda e: e.activation(out=tmp_t[:], in_=in_t[:], func=AF.Ln), reads=[in_t], writes=[tmp_t])
    cx.s.op("act", lambda e: e.activation(out=out_t[:], in_=tmp_t[:], func=AF.Exp, scale=-1.0), reads=[tmp_t], writes=[out_t])


def pipeline(n, stages):
    ns = len(stages)
    for t in range(n + ns - 1):
        for si in range(ns - 1, -1, -1):
            i = t - si
            if 0 <= i < n:
                stages[si](i)


def softmax_chain(cx, blocks, sps, pT, sbf, dens, accs, scale, far_t=None):
    s = cx.s
    n = len(blocks)
    nm = len(dens)

    def S1(i):
        b = blocks[i]
        if b.get("pre") is not None:
            b["pre"]()
        for m_ in range(nm):
            sp_ = sps[m_][i % 2]
            sc = b["score"][m_]
            for k, (lt, la, rt, ra) in enumerate(sc):
                mm(cx, sp_, sp_[:], lt, la, rt, ra, start=(k == 0), stop=(k == len(sc) - 1))

    def S2(i):
        b = blocks[i]
        for m_ in range(nm):
            sp_ = sps[m_][i % 2]
            p_ = pT[m_][i % len(pT[m_])]
            if b.get("tab") is not None:
                tt, ta = b["tab"]
                b_ = sbf[(i * nm + m_) % len(sbf)]
                s.op("dve", lambda e: e.scalar_tensor_tensor(out=b_[:], in0=sp_[:], scalar=scale, in1=ta, op0=ALU.mult, op1=ALU.add),
                     reads=[sp_, tt], writes=[b_])
                s.op("act", lambda e: e.activation(out=p_[:], in_=b_[:], func=AF.Exp), reads=[b_], writes=[p_])
            elif b.get("bias") is not None:
                s.op("act", lambda e: e.activation(out=p_[:], in_=sp_[:], func=AF.Exp, scale=scale, bias=b["bias"]), reads=[sp_, far_t], writes=[p_])
            else:
                s.op("act", lambda e: e.activation(out=p_[:], in_=sp_[:], func=AF.Exp, scale=scale), reads=[sp_], writes=[p_])
            if b.get("mask") is not None:
                mt, ma = b["mask"]
                s.op(b.get("mask_eng", "dve"), lambda e: e.tensor_tensor(out=p_[:], in0=p_[:], in1=ma, op=ALU.mult), reads=[p_, mt], writes=[p_])

    def S3(i):
        b = blocks[i]
        vt, va = b["v"]
        for m_ in range(nm):
            p_ = pT[m_][i % len(pT[m_])]
            mm(cx, dens[m_], dens[m_][:], cx.ones_b, cx.ones_b[:], p_, p_[:], start=(i == 0), stop=(i == n - 1), inc=True)
            mm(cx, accs[m_], accs[m_][:], vt, va, p_, p_[:], start=(i == 0), stop=(i == n - 1), inc=True)
            if b.get("extra") is not None:
                et, ea, ep = b["extra"]
                mm(cx, ep, ep[:], et, ea, p_, p_[:], start=(i == 0), stop=(i == n - 1), inc=True)

    pipeline(n, [S1, S2, S3])


def mla_mixer(cx, j):
    s = cx.s
    w_in = pk(cx.mla_w_in[j])
    w_qb = pk(cx.mla_w_qb[j])
    w_kvb = pk(cx.mla_w_kvb[j])
    qnT, knT, qrT, krT, vv = cx.scrA, cx.scrB, cx.scrC, cx.scrD, cx.scrV
    vview = vv.rearrange("(t p) c -> p t c", p=128)
    with ExitStack() as ph:
        qg = sb(cx, ph, [128, 6], F32, "mqg")
        kg = sb(cx, ph, [128, 4], F32, "mkg")
        load_cols(cx, ph, qg, lambda: qg[:], cx.mla_q_g[j].rearrange("(r p) -> r p", p=128), 6)
        load_cols(cx, ph, kg, lambda: kg[:], cx.mla_kv_g[j].rearrange("(r p) -> r p", p=128), 4)
        x_ = sb(cx, ph, [128, KC, TB], F32, "mx")
        h_ = sb(cx, ph, [128, KC, TB], BF16, "mh")
        sq = [sb(cx, ph, [128, TB], F32, "msq") for _ in range(2)]
        tmp = [sb(cx, ph, [128, TB], F32, "mtmp") for _ in range(2)]
        rs = sb(cx, ph, [128, TB], F32, "mrs")
        w1 = sb(cx, ph, [128, KC, 1344], BF16, "mw1")
        cq = sb(cx, ph, [128, 6, TB], F32, "mcq")
        ckv = sb(cx, ph, [128, 4, TB], F32, "mckv")
        cqn = sb(cx, ph, [128, 6, TB], BF16, "mcqn")
        ckvn = sb(cx, ph, [128, 4, TB], BF16, "mckvn")
        cs = sb(cx, ph, [64, TB], F32, "mcos")
        sn = sb(cx, ph, [64, TB], F32, "msin")
        r1 = [sb(cx, ph, [64, TB], F32, "mr1") for _ in range(2)]
        r2 = [sb(cx, ph, [64, TB], F32, "mr2") for _ in range(2)]
        wq = [sb(cx, ph, [128, 6, 192], BF16, "mwq") for _ in range(2)]
        wk = [sb(cx, ph, [128, 4, 256], BF16, "mwk") for _ in range(2)]
        stq = [sb(cx, ph, [128, TB], BF16, "mstq") for _ in range(2)]
        str_ = [sb(cx, ph, [64, TB], BF16, "mstr") for _ in range(2)]
        stk = [sb(cx, ph, [128, TB], BF16, "mstk") for _ in range(2)]
        stv = [sb(cx, ph, [128, 4, 128], BF16, "mstv") for _ in range(2)]
        pss = ps(cx, ph, [128, TB], F32, "mpss")
        pp = [ps(cx, ph, [128, TB], F32, "mpp") for _ in range(3)]
        pr = [ps(cx, ph, [64, TB], F32, "mpr") for _ in range(4)]
        s.dma("pool", w1[:], w_in[:, :, :], writes=[w1])
        k = 0

        def rope(pa, pb, out_ap, out_t, i2):
            s.op("dve", lambda e: e.tensor_tensor(out=r1[i2][:], in0=pa[0:64, :], in1=cs[:], op=ALU.mult), reads=[pa, cs], writes=[r1[i2]])
            s.op("dve", lambda e: e.tensor_tensor(out=r2[i2][:], in0=pb[0:64, :], in1=sn[:], op=ALU.mult), reads=[pb, sn], writes=[r2[i2]])
            s.op("pool", lambda e: e.tensor_tensor(out=out_ap, in0=r1[i2][:], in1=r2[i2][:], op=ALU.add), reads=[r1[i2], r2[i2]], writes=[out_t])

        nr = 0
        for tb in range(NB):
            t0 = tb * TB
            s.dma("sp", x_[:], cx.xT_pk[:, :, t0:t0 + TB], reads=[cx.xT_b[tb]], writes=[x_])
            s.dma("sp", cs[:], cx.consts_d["rope_cos"][:, t0:t0 + TB], writes=[cs])
            s.dma("sp", sn[:], cx.consts_d["rope_sin"][:, t0:t0 + TB], writes=[sn])
            norm_block(cx, 0, x_, h_, sq, pss, rs, tmp)
            for jj in range(10):
                p_ = pp[k % 3]
                k += 1
                for kc in range(KC):
                    mm(cx, p_, p_[:], w1, w1[:, kc, jj * 128:(jj + 1) * 128], h_, h_[:, kc, :], start=(kc == 0), stop=(kc == KC - 1))
                dst_t, dst = (cq, cq[:, jj, :]) if jj < 6 else (ckv, ckv[:, jj - 6, :])
                if jj % 2 == 0:
                    s.op("act", lambda e: e.copy(out=dst, in_=p_[:]), reads=[p_], writes=[dst_t])
                else:
                    s.op("dve", lambda e: e.tensor_copy(out=dst, in_=p_[:]), reads=[p_], writes=[dst_t])
            pa, pb = pr[0], pr[1]
            for kc in range(KC):
                mm(cx, pa, pa[0:64, :], w1, w1[:, kc, 1280:1344], h_, h_[:, kc, :], start=(kc == 0), stop=(kc == KC - 1))
            for kc in range(KC):
                mm(cx, pb, pb[0:32, :], w1, w1[:, kc, 1312:1344], h_, h_[:, kc, :], start=(kc == 0), stop=(kc == KC - 1))
            for kc in range(KC):
                mm(cx, pb, pb[32:64, :], w1, w1[:, kc, 1280:1312], h_, h_[:, kc, :], start=(kc == 0), stop=(kc == KC - 1))
            st_ = str_[nr % 2]
            rope(pa, pb, st_[:], st_, nr % 2)
            s.dma("sp", krT[:, t0:t0 + TB], st_[:], reads=[st_])
            nr += 1
            for (src, dstn, g_, nch, nf) in ((cq, cqn, qg, 6, 768), (ckv, ckvn, kg, 4, 512)):
                rstd_block(cx, src, sq, pss, rs, nf, nch=nch)
                for kc in range(nch):
                    t_ = tmp[kc % 2]
                    s.op("pool", lambda e: e.tensor_tensor(out=t_[:], in0=src[:, kc, :], in1=rs[:], op=ALU.mult), reads=[src, rs], writes=[t_])
                    s.op("dve", lambda e: e.tensor_scalar(out=dstn[:, kc, :], in0=t_[:], scalar1=g_[:, kc:kc + 1], scalar2=None, op0=ALU.mult),
                         reads=[t_, g_], writes=[dstn])
            def load_hw(h):
                s.dma("pool", wq[h % 2][:], w_qb[:, :, h * 192:(h + 1) * 192], writes=[wq[h % 2]])
                s.dma("pool", wk[h % 2][:], w_kvb[:, :, h * 256:(h + 1) * 256], writes=[wk[h % 2]])

            load_hw(0)
            for h in range(16):
                wq_, wk_ = wq[h % 2], wk[h % 2]
                p_ = pp[k % 3]
                k += 1
                for kc in range(6):
                    mm(cx, p_, p_[:], wq_, wq_[:, kc, 0:128], cqn, cqn[:, kc, :], start=(kc == 0), stop=(kc == 5))
                sq_ = stq[h % 2]
                s.op("act", lambda e: e.copy(out=sq_[:], in_=p_[:]), reads=[p_], writes=[sq_])
                s.dma("sp", qnT[h, :, t0:t0 + TB], sq_[:], reads=[sq_])
                pa, pb = pr[2 * (h % 2)], pr[2 * (h % 2) + 1]
                for kc in range(6):
                    mm(cx, pa, pa[0:64, :], wq_, wq_[:, kc, 128:192], cqn, cqn[:, kc, :], start=(kc == 0), stop=(kc == 5))
                for kc in range(6):
                    mm(cx, pb, pb[0:32, :], wq_, wq_[:, kc, 160:192], cqn, cqn[:, kc, :], start=(kc == 0), stop=(kc == 5))
                for kc in range(6):
                    mm(cx, pb, pb[32:64, :], wq_, wq_[:, kc, 128:160], cqn, cqn[:, kc, :], start=(kc == 0), stop=(kc == 5))
                st_ = str_[nr % 2]
                rope(pa, pb, st_[:], st_, nr % 2)
                s.dma("sp", qrT[h, :, t0:t0 + TB], st_[:], reads=[st_])
                nr += 1
                p_ = pp[k % 3]
                k += 1
                for kc in range(4):
                    mm(cx, p_, p_[:], wk_, wk_[:, kc, 0:128], ckvn, ckvn[:, kc, :], start=(kc == 0), stop=(kc == 3))
                sk_ = stk[h % 2]
                s.op("dve", lambda e: e.tensor_copy(out=sk_[:], in_=p_[:]), reads=[p_], writes=[sk_])
                s.dma("sp", knT[h, :, t0:t0 + TB], sk_[:], reads=[sk_])
                p_ = pp[k % 3]
                k += 1
                for t4 in range(4):
                    for kc in range(4):
                        mm(cx, p_, p_[:, t4 * 128:(t4 + 1) * 128], ckvn, ckvn[:, kc, t4 * 128:(t4 + 1) * 128], wk_, wk_[:, kc, 128:256],
                           start=(kc == 0), stop=(kc == 3))
                if h + 1 < 16:
                    load_hw(h + 1)
                sv_ = stv[h % 2]
                s.op("act", lambda e: e.copy(out=sv_[:].rearrange("p a b -> p (a b)"), in_=p_[:]), reads=[p_], writes=[sv_])
                s.dma("sp", vview[:, tb * 4:tb * 4 + 4, h * 128:(h + 1) * 128], sv_[:], reads=[sv_])
        s.barrier()
    maybe_precast(cx)

    scale = 192 ** -0.5
    with ExitStack() as ph:
        qn = [sb(cx, ph, [128, S], BF16, "aqn") for _ in range(2)]
        qr = [sb(cx, ph, [64, S], BF16, "aqr") for _ in range(2)]
        kn = [sb(cx, ph, [128, S], BF16, "akn") for _ in range(2)]
        kr = sb(cx, ph, [64, S], BF16, "akr")
        vh = [sb(cx, ph, [128, 32, 128], BF16, "av") for _ in range(2)]
        pT = [sb(cx, ph, [128, TB], BF16, "apT") for _ in range(3)]
        rr = sb(cx, ph, [128, TB], F32, "arr")
        rr2 = sb(cx, ph, [128, TB], F32, "arr2")
        ost = [sb(cx, ph, [128, TB], BF16, "aost") for _ in range(2)]
        sps4 = [ps(cx, ph, [128, TB], F32, "asp") for _ in range(4)]
        den = [ps(cx, ph, [128, TB], F32, "aden") for _ in range(2)]
        acc = [ps(cx, ph, [128, TB], F32, "aacc") for _ in range(2)]
        s.dma("sp", kr[:], krT[:, :], writes=[kr])

        def load_head(h, i):
            s.dma("sp", qn[i][:], qnT[h, :, :], writes=[qn[i]])
            s.dma("sp", qr[i][:], qrT[h, :, :], writes=[qr[i]])
            s.dma("sp", kn[i][:], knT[h, :, :], writes=[kn[i]])
            s.dma("sp", vh[i][:], vview[:, :, h * 128:(h + 1) * 128], writes=[vh[i]])

        load_head(0, 0)
        nq = 0
        for h in range(16):
            if h + 1 < 16:
                load_head(h + 1, (h + 1) % 2)
            i = h % 2
            for c in range(NB):
                d_, a_ = den[nq % 2], acc[nq % 2]
                nkt = 4 * c + 4
                qs = slice(c * TB, (c + 1) * TB)
                blocks = []
                for kt in range(nkt):
                    ks = slice(kt * 128, (kt + 1) * 128)
                    blk = {"score": [[(kn[i], kn[i][:, ks], qn[i], qn[i][:, qs]), (kr, kr[:, ks], qr[i], qr[i][:, qs])]],
                           "v": (vh[i], vh[i][:, kt, :])}
                    if kt >= 4 * c:
                        mk = cx.mask_le_b[kt - 4 * c]
                        blk["mask"] = (mk, mk[:])
                    blocks.append(blk)
                softmax_chain(cx, blocks, [sps4[(nq % 2) * 2:(nq % 2) * 2 + 2]], [pT], None, [d_], [a_], scale)
                recip_act(cx, rr, d_, rr2)
                o_st = ost[nq % 2]
                s.op("dve", lambda e: e.tensor_tensor(out=o_st[:], in0=a_[:], in1=rr[:], op=ALU.mult), reads=[a_, rr], writes=[o_st])
                s.dma("sp", cx.scrO[h, :, c * TB:(c + 1) * TB], o_st[:], reads=[o_st])
                nq += 1
        s.barrier()
    outproj_phase(cx, cx.scrO.rearrange("k p t -> p k t"), cx.mla_w_out[j])


def qkv_groups(cx, w_in):
    qT_pk = cx.scrA.rearrange("k p t -> p k t")
    kT_pk = cx.scrB.rearrange("k p t -> p k t")
    groups = []
    for g in range(4):
        groups.append(("fm", w_in, g * 512, 512, fm_store(cx, qT_pk, g * 4)))
    for g in range(4):
        groups.append(("fm", w_in, 2048 + g * 512, 512, fm_store(cx, kT_pk, g * 4)))
    for g in range(4):
        groups.append(("tm", w_in, 4096 + g * 512, 512, tm_store(cx, cx.scrV, g * 512)))
    return groups


def diff_mixer(cx, j, layer):
    s = cx.s
    lam_init = 0.8 - 0.6 * math.exp(-0.3 * layer)
    inproj_phase(cx, qkv_groups(cx, pk(cx.win_b)))
    qT, kT, vv = cx.scrA, cx.scrB, cx.scrV
    vview = vv.rearrange("(t p) c -> p t c", p=128)
    scale = 64 ** -0.5
    with ExitStack() as ph:
        hg = sb(cx, ph, [128, 1], F32, "dhg")
        neglam = sb(cx, ph, [128, 1], F32, "dnl")
        far = sb(cx, ph, [128, 16], F32, "dfar")
        load_cols(cx, ph, hg, lambda: hg[:], cx.diff_head_g[j].rearrange("(r p) -> r p", p=128), 1)
        s.op("dve", lambda e: e.tensor_scalar(out=hg[:], in0=hg[:], scalar1=1.0 - lam_init, scalar2=None, op0=ALU.mult), reads=[hg], writes=[hg])
        s.dma("sp", far[:], cx.consts_d["t5far"][:, :], writes=[far])
        with ExitStack() as lp:
            lrow = sb(cx, lp, [1, 256], F32, "dlrow")
            lpr = sb(cx, lp, [1, 128], F32, "dlpr")
            lsum = sb(cx, lp, [1, 2], F32, "dlsum")
            lex = sb(cx, lp, [1, 2], F32, "dlex")
            lv = sb(cx, lp, [1, 1], F32, "dlv")
            lps = ps(cx, lp, [128, 1], F32, "dlps")
            s.dma("sp", lrow[:], cx.diff_lambda[j].rearrange("a d -> (a d)").rearrange("(o n) -> o n", o=1), writes=[lrow])
            lr3 = lrow[:].rearrange("o (a d) -> o a d", a=4)
            s.op("dve", lambda e: e.tensor_tensor(out=lpr[:].rearrange("o (a d) -> o a d", a=2), in0=lr3[:, 0:4:2, :], in1=lr3[:, 1:4:2, :], op=ALU.mult),
                 reads=[lrow], writes=[lpr])
            s.op("dve", lambda e: e.reduce_sum(out=lsum[:], in_=lpr[:].rearrange("o (a d) -> o a d", a=2), axis=AX.X), reads=[lpr], writes=[lsum])
            s.op("act", lambda e: e.activation(out=lex[:], in_=lsum[:], func=AF.Exp), reads=[lsum], writes=[lex])
            s.op("dve", lambda e: e.tensor_tensor(out=lv[:], in0=lex[:, 1:2], in1=lex[:, 0:1], op=ALU.subtract), reads=[lex], writes=[lv])
            s.op("dve", lambda e: e.tensor_scalar(out=lv[:], in0=lv[:], scalar1=-lam_init, scalar2=None, op0=ALU.add), reads=[lv], writes=[lv])
            mm(cx, lps, lps[:], cx.ones_f, cx.ones_f[0:1, :], lv, lv[:], True, True)
            s.op("dve", lambda e: e.tensor_copy(out=neglam[:], in_=lps[:]), reads=[lps], writes=[neglam])
            s.barrier()

        qh = [sb(cx, ph, [128, S], BF16, "dq") for _ in range(2)]
        kh = [sb(cx, ph, [128, S], BF16, "dk") for _ in range(2)]
        vh = [sb(cx, ph, [128, 32, 128], BF16, "dv") for _ in range(2)]
        tab = [sb(cx, ph, [128, 5, TB], F32, "dtab") for _ in range(2)]
        sbf = [sb(cx, ph, [128, TB], F32, "dsbf") for _ in range(4)]
        pT = [sb(cx, ph, [128, TB], BF16, "dpT") for _ in range(4)]
        rr = [sb(cx, ph, [128, TB], F32, "drr") for _ in range(2)]
        t0_ = sb(cx, ph, [128, TB], F32, "dt0")
        t1_ = sb(cx, ph, [128, TB], F32, "dt1")
        o_ = sb(cx, ph, [128, TB], F32, "do")
        osq = sb(cx, ph, [128, TB], F32, "dosq")
        rs = sb(cx, ph, [128, TB], F32, "drs")
        ost = [sb(cx, ph, [128, TB], BF16, "dost") for _ in range(2)]
        sps = [ps(cx, ph, [128, TB], F32, "dsp") for _ in range(4)]
        den = [ps(cx, ph, [128, TB], F32, "dden") for _ in range(2)]
        acc = [ps(cx, ph, [128, TB], F32, "dacc") for _ in range(2)]
        pn = den[0]

        def load_head(h, i):
            s.dma("sp", qh[i][:], qT[h, :, :], writes=[qh[i]])
            s.dma("sp", kh[i][:], kT[h, :, :], writes=[kh[i]])
            s.dma("sp", vh[i][:], vview[:, :, h * 128:(h + 1) * 128], writes=[vh[i]])
            s.dma("sp", tab[i][:], cx.consts_d["t5tab"][h].rearrange("m p q -> p m q"), writes=[tab[i]])

        load_head(0, 0)
        nq = 0
        for h in range(16):
            if h + 1 < 16:
                load_head(h + 1, (h + 1) % 2)
            i = h % 2
            for c in range(NB):
                nkt = 4 * c + 4
                qs = slice(c * TB, (c + 1) * TB)
                blocks = []
                for kt in range(nkt):
                    ks = slice(kt * 128, (kt + 1) * 128)
                    mi = kt - 4 * c + 1
                    blk = {"score": [[(kh[i], kh[i][64 * m_:64 * m_ + 64, ks], qh[i], qh[i][64 * m_:64 * m_ + 64, qs])] for m_ in range(2)],
                           "v": (vh[i], vh[i][:, kt, :])}
                    if mi >= 0:
                        blk["tab"] = (tab[i], tab[i][:, mi, :])
                    else:
                        blk["bias"] = far[:, h:h + 1]
                    blocks.append(blk)
                softmax_chain(cx, blocks, [sps[0:2], sps[2:4]], [pT[0:2], pT[2:4]], sbf, den, acc, scale, far_t=far)
                for m_ in range(2):
                    recip_act(cx, rr[m_], den[m_], osq)
                s.op("dve", lambda e: e.tensor_tensor(out=t0_[:], in0=acc[0][:], in1=rr[0][:], op=ALU.mult), reads=[acc[0], rr[0]], writes=[t0_])
                s.op("dve", lambda e: e.tensor_tensor(out=t1_[:], in0=acc[1][:], in1=rr[1][:], op=ALU.mult), reads=[acc[1], rr[1]], writes=[t1_])
                s.op("dve", lambda e: e.scalar_tensor_tensor(out=o_[:], in0=t1_[:], scalar=neglam[:, 0:1], in1=t0_[:], op0=ALU.mult, op1=ALU.add),
                     reads=[t1_, neglam, t0_], writes=[o_])
                s.op("act", lambda e: e.activation(out=osq[:], in_=o_[:], func=AF.Square), reads=[o_], writes=[osq])
                mm(cx, pn, pn[:], cx.ones_f, cx.ones_f[:], osq, osq[:], True, True)
                s.op("dve", lambda e: e.tensor_scalar(out=rs[:], in0=pn[:], scalar1=1.0 / 128, scalar2=EPS, op0=ALU.mult, op1=ALU.add), reads=[pn], writes=[rs])
                s.op("act", lambda e: e.activation(out=rs[:], in_=rs[:], func=AF.Sqrt), reads=[rs], writes=[rs])
                s.op("dve", lambda e: e.reciprocal(out=rs[:], in_=rs[:]), reads=[rs], writes=[rs])
                s.op("pool", lambda e: e.tensor_tensor(out=o_[:], in0=o_[:], in1=rs[:], op=ALU.mult), reads=[o_, rs], writes=[o_])
                o_st = ost[nq % 2]
                s.op("dve", lambda e: e.tensor_scalar(out=o_st[:], in0=o_[:], scalar1=hg[:, 0:1], scalar2=None, op0=ALU.mult), reads=[o_, hg], writes=[o_st])
                s.dma("sp", cx.scrO[h, :, c * TB:(c + 1) * TB], o_st[:], reads=[o_st])
                nq += 1
        s.barrier()
    outproj_phase(cx, cx.scrO.rearrange("k p t -> p k t"), cx.diff_w_out[j])


def nsa_mixer(cx, j):
    s = cx.s
    w_in = pk(cx.win_b)
    qT, kvT, vv, gsc = cx.scrA, cx.scrB, cx.scrV, cx.scrG
    qT_pk = qT.rearrange("k p t -> p k t")
    kvT_pk = kvT.rearrange("k p t -> p k t")
    groups = []
    for g in range(4):
        groups.append(("fm", w_in, g * 512, 512, fm_store(cx, qT_pk, g * 4)))
    groups.append(("fm", w_in, 2048, 512, fm_store(cx, kvT_pk, 0)))
    groups.append(("fm", w_in, 2560, 512, fm_store(cx, kvT_pk, 4)))
    groups.append(("fm", w_in, 3072, 512, fm_store(cx, kvT_pk, 8)))
    groups.append(("tm", w_in, 3584, 512, tm_store(cx, vv, 0)))
    groups.append(("fm", w_in, 4096, 512, fm_store(cx, kvT_pk, 12)))
    groups.append(("tm", w_in, 4608, 512, tm_store(cx, vv, 512)))

    def gate_store(tb, st):
        s.dma("sp", gsc[:, tb * TB:(tb + 1) * TB], st[0:48, :], reads=[st])
    groups.append(("gate", w_in, 5120, 48, gate_store))
    inproj_phase(cx, groups)

    vview = vv.rearrange("(t p) c -> p t c", p=128)
    scale = 128 ** -0.5
    C = cx.consts_d
    with ExitStack() as ph:
        kcT = sb(cx, ph, [128, 4, 256], BF16, "nkcT")
        vc = sb(cx, ph, [128, 4, 2, 128], BF16, "nvc")
        s.op("dve", lambda e: e.memset(kcT[:], 0.0), writes=[kcT])
        s.op("dve", lambda e: e.memset(vc[:], 0.0), writes=[vc])
        with ExitStack() as cp:
            raw = [sb(cx, cp, [128, S], BF16, "nraw") for _ in range(2)]
            w1 = sb(cx, cp, [128, 32, 128], BF16, "nw1")
            w2 = sb(cx, cp, [128, 128], BF16, "nw2")
            pef = sb(cx, cp, [128, 32], F32, "npef")
            peb = sb(cx, cp, [128, 32], BF16, "npeb")
            b1 = sb(cx, cp, [128, 1], F32, "nb1")
            s1 = sb(cx, cp, [128, 256], BF16, "ns1")
            ph1 = ps(cx, cp, [128, 256], F32, "nph1")
            pb = ps(cx, cp, [128, 1], F32, "npb")
            pk2 = ps(cx, cp, [128, 256], F32, "npk2")
            pv = ps(cx, cp, [128, 128], F32, "npv")
            nr = 0
            for kv in range(2):
                s.dma("pool", w1[:], pk(cx.nsa_cmp_w1[j, kv]), writes=[w1])
                s.dma("pool", w2[:], cx.nsa_cmp_w2[j, kv], writes=[w2])
                load_cols(cx, cp, pef, lambda: pef[:], cx.nsa_cmp_pe[j, kv], 32)
                s.op("dve", lambda e: e.tensor_copy(out=peb[:], in_=pef[:]), reads=[pef], writes=[peb])
                for l in range(32):
                    mm(cx, pb, pb[:], w1, w1[:, l, :], peb, peb[:, l:l + 1], start=(l == 0), stop=(l == 31))
                s.op("dve", lambda e: e.tensor_copy(out=b1[:], in_=pb[:]), reads=[pb], writes=[b1])
                for g in range(4):
                    rw = raw[nr % 2]
                    nr += 1
                    s.dma("sp", rw[:], kvT[kv * 4 + g, :, :], writes=[rw])
                    r3 = rw[:].rearrange("p (c s) -> p c s", s=16)
                    for l in range(32):
                        rhs = r3[:, 0:255, l] if l < 16 else r3[:, 1:256, l - 16]
                        mm(cx, ph1, ph1[:, 0:255], w1, w1[:, l, :], rw, rhs, start=(l == 0), stop=(l == 31))
                    s.op("act", lambda e: e.activation(out=s1[:, 0:255], in_=ph1[:, 0:255], func=AF.Silu, bias=b1[:, 0:1]),
                         reads=[ph1, b1], writes=[s1])
                    if kv == 0:
                        mm(cx, pk2, pk2[:, 0:255], w2, w2[:], s1, s1[:, 0:255], True, True)
                        s.op("dve", lambda e: e.tensor_copy(out=kcT[:, g, 0:255], in_=pk2[:, 0:255]), reads=[pk2], writes=[kcT])
                    else:
                        for cc in range(2):
                            M = 128 if cc == 0 else 127
                            mm(cx, pv, pv[0:M, :], s1, s1[:, cc * 128:cc * 128 + M], w2, w2[:], True, True)
                            s.op("dve", lambda e: e.tensor_copy(out=vc[0:M, g, cc, :], in_=pv[0:M, :]), reads=[pv], writes=[vc])
            s.barrier()

        gbs = [sb(cx, ph, [128, TB], F32, "ngbs") for _ in range(2)]
        ovl = sb(cx, ph, [128, 2, 64], BF16, "novl")
        Eexp = sb(cx, ph, [64, S], BF16, "nE")
        far = sb(cx, ph, [128, 16], F32, "nfar")
        wmask = [sb(cx, ph, [128, TB], BF16, "nwm") for _ in range(4)]
        ksel = sb(cx, ph, [128, S], BF16, "nksel")
        kwin = sb(cx, ph, [128, S], BF16, "nkwin")
        vsel = sb(cx, ph, [128, 32, 128], BF16, "nvsel")
        vwin = sb(cx, ph, [128, 32, 128], BF16, "nvwin")
        qh = [sb(cx, ph, [128, S], BF16, "nq") for _ in range(4)]
        tabb = [sb(cx, ph, [128, 5, TB], F32, "ntab") for _ in range(2)]
        scA = sb(cx, ph, [128, 4, 64], F32, "nscA")
        scB = sb(cx, ph, [128, 4, 64], F32, "nscB")
        ctab = [sb(cx, ph, [128, TB], F32, "nctab") for _ in range(2)]
        sbf = [sb(cx, ph, [128, TB], F32, "nsbf") for _ in range(2)]
        pT = [sb(cx, ph, [128, TB], BF16, "npT") for _ in range(3)]
        mskall = sb(cx, ph, [128, 32, TB], BF16, "nmsk")
        rr = sb(cx, ph, [128, TB], F32, "nrr")
        fbr = sb(cx, ph, [128, TB], F32, "nfbr")
        tmpo = sb(cx, ph, [128, TB], F32, "ntmpo")
        oacc = [sb(cx, ph, [128, TB], F32, "noacc") for _ in range(4)]
        ost = [sb(cx, ph, [128, TB], BF16, "nost") for _ in range(2)]
        impT = sb(cx, ph, [64, TB], F32, "nimpT")
        itmp = sb(cx, ph, [64, TB], F32, "nitmp")
        sc = [sb(cx, ph, [128, 64], F32, "nsc") for _ in range(4)]
        sc2 = [sb(cx, ph, [128, 64], F32, "nsc2") for _ in range(4)]
        mx8 = [sb(cx, ph, [128, 8], F32, "nmx8") for _ in range(4)]
        selq = [sb(cx, ph, [128, 64], F32, "nselq") for _ in range(4)]
        selT = sb(cx, ph, [64, TB], BF16, "nselT")
        sps = [ps(cx, ph, [128, TB], F32, "nsp") for _ in range(2)]
        dens = [ps(cx, ph, [128, TB], F32, "nden") for _ in range(2)]
        accs = [ps(cx, ph, [128, TB], F32, "nacc") for _ in range(2)]
        imp_ps = ps(cx, ph, [64, TB], F32, "nimp")
        msk_ps = ps(cx, ph, [128, TB], F32, "nmskp")
        tp = msk_ps

        s.dma("pool", ovl[:], C["ovl"].rearrange("c p n -> p c n"), writes=[ovl])
        s.dma("pool", Eexp[:], C["eexp"][:, :], writes=[Eexp])
        s.dma("sp", far[:], C["t5far"][:, :], writes=[far])
        for o_ in range(4):
            s.dma("pool", wmask[o_][:], C["wmask"][o_], writes=[wmask[o_]])

        state = {"n": 0, "no": 0, "ng": 0, "nt": 0, "nc": 0}

        def get_tab(hd):
            t_ = tabb[state["nt"] % 2]
            state["nt"] += 1
            s.dma("sp", t_[:], C["t5tab"][hd].rearrange("m p q -> p m q"), writes=[t_])
            return t_

        def finish(br, r, hd, qs, first):
            den, acc = dens[state["nc"] % 2], accs[state["nc"] % 2]
            s.op("dve", lambda e: e.tensor_scalar(out=rr[:], in0=den[:], scalar1=1e-30, scalar2=None, op0=ALU.max), reads=[den], writes=[rr])
            row = br * 16 + hd
            gb = gbs[state["ng"] % 2]
            state["ng"] += 1
            s.dma("sp", gb[:], gsc[row:row + 1, qs].partition_broadcast(128), writes=[gb])
            recip_act(cx, rr, rr, tmpo)
            s.op("dve", lambda e: e.tensor_tensor(out=fbr[:], in0=gb[:], in1=rr[:], op=ALU.mult), reads=[gb, rr], writes=[fbr])
            if first:
                s.op("dve", lambda e: e.tensor_tensor(out=oacc[r][:], in0=acc[:], in1=fbr[:], op=ALU.mult), reads=[acc, fbr], writes=[oacc[r]])
            else:
                s.op("dve", lambda e: e.tensor_tensor(out=tmpo[:], in0=acc[:], in1=fbr[:], op=ALU.mult), reads=[acc, fbr], writes=[tmpo])
                s.op("dve", lambda e: e.tensor_tensor(out=oacc[r][:], in0=oacc[r][:], in1=tmpo[:], op=ALU.add), reads=[oacc[r], tmpo], writes=[oacc[r]])
            state["nc"] += 1

        def chain(blocks):
            softmax_chain(cx, blocks, [sps], [pT], sbf, [dens[state["nc"] % 2]], [accs[state["nc"] % 2]], scale, far_t=far)

        for g in range(4):
            s.dma("sp", ksel[:], kvT[8 + g, :, :], writes=[ksel])
            s.dma("sp", kwin[:], kvT[12 + g, :, :], writes=[kwin])
            s.dma("sp", vsel[:], vview[:, :, g * 128:(g + 1) * 128], writes=[vsel])
            s.dma("sp", vwin[:], vview[:, :, 512 + g * 128:512 + (g + 1) * 128], writes=[vwin])
            for r in range(4):
                hd = g * 4 + r
                s.dma("sp", qh[r][:], qT[hd, :, :], writes=[qh[r]])
            for cq in range(NB):
                qs = slice(cq * TB, (cq + 1) * TB)
                s.dma("sp", scA[:], C["scA"][cq * 4:cq * 4 + 4].rearrange("t p n -> p t n"), writes=[scA])
                s.dma("sp", scB[:], C["scB"][cq * 4:cq * 4 + 4].rearrange("t p n -> p t n"), writes=[scB])
                ccs = [0] if cq <= 3 else [0, 1]
                for r in range(4):
                    hd = g * 4 + r
                    blocks = []
                    for cc in ccs:
                        m_ = 32 * cq - 128 * cc - 2
                        row0 = 222 - m_
                        ct = ctab[state["n"] % 2]
                        state["n"] += 1

                        def pre(ct=ct, row0=row0, hd=hd):
                            s.dma("sp", ct[:], C["t5cmp"][hd, row0:row0 + 128, :], writes=[ct])
                        blocks.append({"pre": pre, "score": [[(kcT, kcT[:, g, cc * 128:(cc + 1) * 128], qh[r], qh[r][:, qs])]],
                                       "tab": (ct, ct[:]), "v": (vc, vc[:, g, cc, :]), "extra": (ovl, ovl[:, cc, :], imp_ps)})
                    chain(blocks)
                    finish(0, r, hd, qs, True)
                    if r == 0:
                        s.op("dve", lambda e: e.tensor_tensor(out=impT[:], in0=imp_ps[:], in1=rr[0:64, :], op=ALU.mult), reads=[imp_ps, rr], writes=[impT])
                    else:
                        s.op("dve", lambda e: e.tensor_tensor(out=itmp[:], in0=imp_ps[:], in1=rr[0:64, :], op=ALU.mult), reads=[imp_ps, rr], writes=[itmp])
                        s.op("pool", lambda e: e.tensor_tensor(out=impT[:], in0=impT[:], in1=itmp[:], op=ALU.add), reads=[impT, itmp], writes=[impT])
                R4 = range(4)
                for t4 in R4:
                    s.op("pe", lambda e: e.transpose(tp[:, t4 * 64:(t4 + 1) * 64], impT[:, t4 * 128:(t4 + 1) * 128], cx.ident_f[0:64, 0:64]),
                         reads=[impT, cx.ident_f], writes=[tp])
                for t4 in R4:
                    s.op("dve", lambda e: e.tensor_tensor(out=sc[t4][:], in0=tp[:, t4 * 64:(t4 + 1) * 64], in1=scA[:, t4, :], op=ALU.mult), reads=[tp, scA], writes=[sc[t4]])
                for t4 in R4:
                    s.op("dve", lambda e: e.tensor_tensor(out=sc[t4][:], in0=sc[t4][:], in1=scB[:, t4, :], op=ALU.add), reads=[sc[t4], scB], writes=[sc[t4]])
                for t4 in R4:
                    s.op("dve", lambda e: e.max(out=mx8[t4][:], in_=sc[t4][:]), reads=[sc[t4]], writes=[mx8[t4]])
                for t4 in R4:
                    s.op("dve", lambda e: e.match_replace(out=sc2[t4][:], in_to_replace=mx8[t4][:], in_values=sc[t4][:], imm_value=-3.0), reads=[mx8[t4], sc[t4]], writes=[sc2[t4]])
                for t4 in R4:
                    s.op("dve", lambda e: e.max(out=mx8[t4][:], in_=sc2[t4][:]), reads=[sc2[t4]], writes=[mx8[t4]])
                for t4 in R4:
                    s.op("dve", lambda e: e.match_replace(out=sc2[t4][:], in_to_replace=mx8[t4][:], in_values=sc2[t4][:], imm_value=-3.0), reads=[mx8[t4], sc2[t4]], writes=[sc2[t4]])
                for t4 in R4:
                    s.op("dve", lambda e: e.tensor_tensor(out=selq[t4][:], in0=sc[t4][:], in1=sc2[t4][:], op=ALU.is_gt), reads=[sc[t4], sc2[t4]], writes=[selq[t4]])
                for t4 in R4:
                    s.op("pe", lambda e: e.transpose(tp[0:64, t4 * 128:(t4 + 1) * 128], selq[t4][:], cx.ident_f[:]), reads=[selq[t4], cx.ident_f], writes=[tp])
                s.op("dve", lambda e: e.tensor_copy(out=selT[:], in_=tp[0:64, :]), reads=[tp], writes=[selT])
                nkt = 4 * cq + 4
                for kt in range(nkt):
                    mm(cx, msk_ps, msk_ps[:], Eexp, Eexp[:, kt * 128:(kt + 1) * 128], selT, selT[:], True, True)
                    s.op("act", lambda e: e.copy(out=mskall[:, kt, :], in_=msk_ps[:]), reads=[msk_ps], writes=[mskall])
                for r in range(4):
                    hd = g * 4 + r
                    tb_ = get_tab(hd)
                    blocks = []
                    for kt in range(nkt):
                        mi = kt - 4 * cq + 1
                        ks = slice(kt * 128, (kt + 1) * 128)
                        blk = {"score": [[(ksel, ksel[:, ks], qh[r], qh[r][:, qs])]], "mask": (mskall, mskall[:, kt, :]),
                               "mask_eng": "dve", "v": (vsel, vsel[:, kt, :])}
                        if mi >= 0:
                            blk["tab"] = (tb_, tb_[:, mi, :])
                        else:
                            blk["bias"] = far[:, hd:hd + 1]
                        blocks.append(blk)
                    chain(blocks)
                    finish(1, r, hd, qs, False)
                for r in range(4):
                    hd = g * 4 + r
                    kts = list(range(max(0, 4 * cq - 4), 4 * cq + 4))
                    tb_ = get_tab(hd)
                    blocks = []
                    for kt in kts:
                        off = kt - 4 * cq
                        ks = slice(kt * 128, (kt + 1) * 128)
                        blk = {"score": [[(kwin, kwin[:, ks], qh[r], qh[r][:, qs])]], "v": (vwin, vwin[:, kt, :])}
                        if off >= 0:
                            blk["tab"] = (tb_, tb_[:, off + 1, :])
                        elif off == -1:
                            blk["tab"] = (tb_, tb_[:, 0, :])
                            blk["mask"] = (wmask[3], wmask[3][:])
                        else:
                            blk["bias"] = far[:, hd:hd + 1]
                            blk["mask"] = (wmask[off + 4], wmask[off + 4][:])
                        blocks.append(blk)
                    chain(blocks)
                    finish(2, r, hd, qs, False)
                    o_st = ost[state["no"] % 2]
                    state["no"] += 1
                    s.op("act", lambda e: e.copy(out=o_st[:], in_=oacc[r][:]), reads=[oacc[r]], writes=[o_st])
                    s.dma("sp", cx.scrO[hd, :, qs], o_st[:], reads=[o_st])
        s.barrier()
    outproj_phase(cx, cx.scrO.rearrange("k p t -> p k t"), cx.nsa_w_out[j])


def weight_shapes(nl):
    return {
        "ada_w": (nl, D, 6 * D), "ada_b": (nl, 6 * D), "norm_g": (nl, 2, D), "final_g": (D,),
        "ffn_w_up": (nl, D, 2 * FF), "ffn_conv_w": (nl, 3, 2 * FF), "ffn_conv_b": (nl, 2 * FF),
        "ffn_w_down": (nl, FF, D),
        "sb_w_in": (1, D, 6144), "sb_w_out": (1, D, D),
        "nsa_w_in": (1, D, 5168), "nsa_cmp_pe": (1, 2, 32, 128), "nsa_cmp_w1": (1, 2, 4096, 128),
        "nsa_cmp_w2": (1, 2, 128, 128), "nsa_w_out": (1, D, D),
        "diff_w_in": (1, D, 6144), "diff_lambda": (1, 4, 64), "diff_head_g": (1, 128), "diff_w_out": (1, D, D),
        "mla_w_in": (1, D, 1344), "mla_q_g": (1, 768), "mla_w_qb": (1, 768, 3072), "mla_kv_g": (1, 512),
        "mla_w_kvb": (1, 512, 4096), "mla_w_out": (1, D, D),
    }


def build(stages, nl=DEPTH):
    nc = bass.Bass("TRN2", target_bir_lowering=False)
    cx = Ctx()
    cx.nc = nc
    cx.x = nc.dram_tensor("x", [S, D], F32, kind="ExternalInput").ap()
    cx.c16 = nc.dram_tensor("c16", [KC, 128], F32, kind="ExternalInput").ap()
    cx.identf_d = nc.dram_tensor("identf", [128, 128], F32, kind="ExternalInput").ap()
    for name, shp in weight_shapes(nl).items():
        setattr(cx, name, nc.dram_tensor(name, list(shp), F32, kind="ExternalInput").ap())
    cx.out = nc.dram_tensor("out", [S, D], F32, kind="ExternalOutput").ap()
    cx.xT = nc.dram_tensor("xT", [KC, 128, S], F32).ap()
    cx.xT_pk = cx.xT.rearrange("k p t -> p k t")
    cx.xT_b = [Buf() for _ in range(NB)]
    cx.scrA = nc.dram_tensor("scrA", [KC, 128, S], BF16).ap()
    cx.scrB = nc.dram_tensor("scrB", [KC, 128, S], BF16).ap()
    cx.scrO = nc.dram_tensor("scrO", [KC, 128, S], BF16).ap()
    cx.scrV = nc.dram_tensor("scrV", [S, D], BF16).ap()
    cx.scrC = nc.dram_tensor("scrC", [KC, 64, S], BF16).ap()
    cx.scrD = nc.dram_tensor("scrD", [64, S], BF16).ap()
    cx.scrG = nc.dram_tensor("scrG", [48, S], F32).ap()
    cx.wup_b = nc.dram_tensor("wup_b", [D, 2 * FF], BF16).ap()
    cx.wdn_b = nc.dram_tensor("wdn_b", [FF, D], BF16).ap()
    cx.wup_buf, cx.wdn_buf = Buf(), Buf()
    cx.win_b = nc.dram_tensor("win_b", [D, 6144], BF16).ap()
    cx.wout_b = nc.dram_tensor("wout_b", [D, D], BF16).ap()
    cx.win_buf, cx.wout_buf = Buf(), Buf()
    cx.mix_precast_done = {}
    cx.precast_done = {}
    cx.consts_d = {k: nc.dram_tensor(k, list(v), F32, kind="ExternalInput").ap() for k, v in CONST_SHAPES.items() if k != "identf"}
    with ExitStack() as top:
        cx.s = Sched(nc, top)
        s = cx.s
        cx.ident_f = sb(cx, top, [128, 128], F32, "identf")
        cx.ones_f = sb(cx, top, [128, 128], F32, "onesf")
        cx.cact = sb(cx, top, [128, KC], F32, "cact")
        cx.modc = sb(cx, top, [128, 128], F32, "modc")
        s.dma("sp", cx.ident_f[:], cx.identf_d[:, :], writes=[cx.ident_f])
        s.op("dve", lambda e: e.memset(cx.ones_f[:], 1.0), writes=[cx.ones_f])
        cx.ones_b = sb(cx, top, [128, 128], BF16, "onesb")
        s.op("dve", lambda e: e.memset(cx.ones_b[:], 1.0), writes=[cx.ones_b])
        cx.mask_lt, cx.mask_lt_b, cx.mask_le_b = [], [], []
        cx.tri_gt = sb(cx, top, [128, 128], BF16, "trigt")
        cx.tri_le = sb(cx, top, [128, 128], BF16, "trile")
        for m_ in range(4):
            cx.mask_lt.append(sb(cx, top, [128, TB], F32, "mlt"))
            cx.mask_lt_b.append(sb(cx, top, [128, TB], BF16, "mltb"))
            cx.mask_le_b.append(sb(cx, top, [128, TB], BF16, "mleb"))
        with ExitStack() as ph:
            tmpf = sb(cx, ph, [128, 128], F32, "tmpf")
            s.dma("sp", tmpf[:], cx.consts_d["tri_gt"][:, :], writes=[tmpf])
            s.op("dve", lambda e: e.tensor_copy(out=cx.tri_gt[:], in_=tmpf[:]), reads=[tmpf], writes=[cx.tri_gt])
            s.op("dve", lambda e: e.tensor_tensor(out=cx.tri_le[:], in0=cx.ones_b[:], in1=cx.tri_gt[:], op=ALU.subtract),
                 reads=[cx.ones_b, cx.tri_gt], writes=[cx.tri_le])
            for m_ in range(4):
                f_, b_, b2 = cx.mask_lt[m_], cx.mask_lt_b[m_], cx.mask_le_b[m_]
                s.dma("sp", f_[:], cx.consts_d["mask_lt"][m_], writes=[f_])
                s.op("dve", lambda e: e.tensor_copy(out=b_[:], in_=f_[:]), reads=[f_], writes=[b_])
                tl = sb(cx, ph, [128, TB], F32, "mle")
                s.dma("sp", tl[:], cx.consts_d["mask_le"][m_], writes=[tl])
                s.op("dve", lambda e: e.tensor_copy(out=b2[:], in_=tl[:]), reads=[tl], writes=[b2])
            s.barrier()
        with ExitStack() as ph:
            craw = sb(cx, ph, [128, KC], F32, "craw")
            load_cols(cx, ph, craw, lambda: craw[:], cx.c16[:, :], KC)
            s.op("act", lambda e: e.activation(out=cx.cact[:], in_=craw[:], func=AF.Silu), reads=[craw], writes=[cx.cact])
            s.barrier()
        cx.has_ffn = {st[1]: True for st in stages if isinstance(st, tuple) and st[0] == "ffn"}
        mixer_layers = [st[2] for st in stages if isinstance(st, tuple) and len(st) == 3]
        cx.next_mixer = {}
        for a_, b_ in zip(mixer_layers[:-1], mixer_layers[1:]):
            cx.next_mixer[a_] = b_
        for st in stages:
            if isinstance(st, tuple) and len(st) == 3:
                cx.cur_layer = st[2]
                mixer_precast(cx, st[2])
            if st == "pro":
                prologue(cx)
            elif st == "epi":
                epilogue(cx, True)
            elif st == "epi_raw":
                epilogue(cx, False)
            elif st[0] == "ada":
                ada_phase(cx, st[1])
            elif st[0] == "ffn":
                ffn_phase(cx, st[1])
            elif st[0] == "nsa":
                nsa_mixer(cx, st[1])
            elif st[0] == "sb":
                sb_mixer(cx, st[1])
            elif st[0] == "mla":
                mla_mixer(cx, st[1])
            elif st[0] == "diff":
                diff_mixer(cx, st[1], st[2])
            else:
                raise ValueError(st)
        s.barrier()
    return nc


CONST_SHAPES = {"identf": (128, 128), "tri_gt": (128, 128), "mask_lt": (4, 128, TB), "mask_le": (4, 128, TB),
                "rope_cos": (64, S), "rope_sin": (64, S), "t5far": (128, 16), "t5tab": (16, 5, 128, TB),
                "t5cmp": (16, 480, TB), "ovl": (2, 128, 64), "eexp": (64, S),
                "wmask": (4, 128, TB), "scA": (32, 128, 64), "scB": (32, 128, 64)}


def t5_bucket_np(dist):
    n = np.maximum(dist, 0)
    nf = np.maximum(n, 1).astype(np.float32)
    large = 16 + (np.log(nf / np.float32(16)) / np.float32(math.log(128 / 16)) * np.float32(16)).astype(np.int32)
    return np.where(n < 16, n, np.minimum(large, 31))


def make_consts(t5_bias):
    j = np.arange(128)[:, None]
    i = np.arange(TB)[None, :]
    c = {"identf": np.eye(128, dtype=np.float32)}
    c["tri_gt"] = (np.arange(128)[:, None] > np.arange(128)[None, :]).astype(np.float32)
    c["mask_lt"] = np.stack([((128 * m + j) < i) for m in range(4)]).astype(np.float32)
    c["mask_le"] = np.stack([((128 * m + j) <= i) for m in range(4)]).astype(np.float32)
    half = 32
    inv = np.power(np.float32(10000.0), -np.arange(half, dtype=np.float32) / np.float32(half)).astype(np.float32)
    ang = (np.arange(S, dtype=np.float32)[None, :] * inv[:, None]).astype(np.float32)
    cos, sin = np.cos(ang).astype(np.float32), np.sin(ang).astype(np.float32)
    c["rope_cos"] = np.concatenate([cos, cos], 0)
    c["rope_sin"] = np.concatenate([-sin, sin], 0)
    c["t5far"] = np.ascontiguousarray(np.broadcast_to(t5_bias[31][None, :], (128, 16))).astype(np.float32)
    tabs = np.empty((16, 5, 128, TB), np.float32)
    for mi in range(5):
        dist = i - j - 128 * (mi - 1)
        g = t5_bias[t5_bucket_np(dist)]
        g = np.where((dist >= 0)[:, :, None], g, np.float32(-1e30))
        tabs[:, mi] = np.moveaxis(g, -1, 0)
    c["t5tab"] = tabs
    up = (np.arange(480) - 222)[:, None]
    dist = i - 16 * up + 1
    g = t5_bias[t5_bucket_np(dist)]
    g = np.where((dist >= 0)[:, :, None], g, np.float32(-1e30))
    c["t5cmp"] = np.ascontiguousarray(np.moveaxis(g, -1, 0)).astype(np.float32)
    cidx = np.arange(256)[:, None]
    nidx = np.arange(64)[None, :]
    ov = ((16 * cidx < 64 * nidx + 64) & (16 * cidx + 32 > 64 * nidx) & (cidx < 255)).astype(np.float32)
    c["ovl"] = ov.reshape(2, 128, 64)
    c["eexp"] = (np.arange(64)[:, None] == (np.arange(S)[None, :] // 64)).astype(np.float32)
    c["wmask"] = np.stack([((i - j - 128 * off) < 512) for off in (-4, -3, -2, -1)]).astype(np.float32)
    qpos = np.arange(S)[:, None]
    causal = (64 * nidx) <= qpos
    cur = qpos // 64
    forced = (nidx == 0) | (nidx == cur) | (nidx == cur - 1)
    c["scA"] = (causal & ~forced).astype(np.float32).reshape(32, 128, 64)
    c["scB"] = np.where(causal, np.where(forced, np.float32(1e9), np.float32(0)), np.float32(-1)).astype(np.float32).reshape(32, 128, 64)
    return c


MIXERS = ["sb", "nsa", "diff", "mla"]
_WNAMES = None


def full_stages(skip=()):
    st = ["pro"]
    for i in range(DEPTH):
        st.append(("ada", i))
        name = MIXERS[i % 4]
        if name not in skip:
            st.append((name, i // 4, i))
        st.append(("ffn", i))
    st.append("epi")
    return st


def kernel(**inputs):
    inputs = {k: np.ascontiguousarray(np.asarray(v, dtype=np.float32)) for k, v in inputs.items()}
    nc = build(full_stages(), nl=DEPTH)
    consts = make_consts(inputs["t5_bias"])
    B = inputs["x"].shape[0]
    in_maps = []
    for b in range(B):
        m = {"x": inputs["x"][b], "c16": inputs["c"][b].reshape(KC, 128)}
        m.update(consts)
        for k in weight_shapes(DEPTH):
            m[k] = inputs[k]
        in_maps.append(m)
    res = run_bass_kernel_spmd(nc, in_maps, core_ids=list(range(B)))
    return np.stack([r["out"] for r in res.results], axis=0).astype(np.float32)
```
